# Optimizing a Trainium2 kernel written in Bass

```python
import jax, jax.numpy as jnp
from jax import lax
import numpy as np

D_MODEL = 2048
BATCH = 4
SEQ = 4096
DEPTH = 1

GRID_W = 64
CTX_LEN = 256
EPS = 1e-6

M_HEADS = 8
M_DQK = 128
M_DV = 256
M_CONV = 5
M_CHUNK = 64
FGATE_BIAS_LO = 3.0
FGATE_BIAS_HI = 6.0

A_HEADS = 16
A_NOPE = 128
A_ROPE = 64
A_QK = A_NOPE + A_ROPE
A_DV = 128
KV_RANK = 512
ROPE_FREQS = A_ROPE // 4
ROPE_THETA = 10000.0
Q_BLOCK = 128

M_QK_W = M_HEADS * M_DQK
M_V_W = M_HEADS * M_DV
M_GATE_W = 4 * M_HEADS
A_Q_W = A_HEADS * A_QK
A_V_W = A_HEADS * A_DV
KV_SIZES = (M_QK_W, M_V_W, M_GATE_W, KV_RANK, A_ROPE)
Q_SIZES = (M_QK_W, M_V_W, M_V_W, A_Q_W, A_V_W, 2 * D_MODEL)
KV_COLS = M_QK_W + M_V_W + M_GATE_W + KV_RANK + A_ROPE
IN_COLS = KV_COLS + M_QK_W + M_V_W + M_V_W + A_Q_W + A_V_W + 2 * D_MODEL

kernel_name = "hybrid_mlstm_mla_prefix_block"


def split_cols(a, sizes):
    idx = np.cumsum(sizes)[:-1].tolist()
    return jnp.split(a, idx, axis=-1)


def rmsnorm(x, g):
    xf = x.astype(jnp.float32)
    y = xf * lax.rsqrt(jnp.mean(xf * xf, axis=-1, keepdims=True) + EPS)
    return (y * g.astype(jnp.float32)).astype(x.dtype)


def adaln(cvec, ada_w, ada_b):
    mod = jax.nn.silu(cvec) @ ada_w + ada_b
    return jnp.split(mod, 3, axis=-1)


def flip_seq(a):
    return jnp.flip(a, axis=2)


def centred_dwconv(x, w, b):
    k, ch = w.shape
    y = lax.conv_general_dilated(x, w[:, None, :].astype(x.dtype), window_strides=(1,),
                                 padding=[(k // 2, k // 2)],
                                 dimension_numbers=("NWC", "WIO", "NWC"),
                                 feature_group_count=ch)
    return y + b


def mlstm_heads(a, dh):
    b, t, _ = a.shape
    return a.reshape(b, t, -1, dh).transpose(0, 2, 1, 3)


def mlstm_qk(raw, w, b):
    return mlstm_heads(jax.nn.silu(centred_dwconv(raw, w, b)), M_DQK)


def mlstm_gates(raw, gate_b):
    b, t, _ = raw.shape
    g = (raw.astype(jnp.float32) + gate_b.astype(jnp.float32)).reshape(b, t, 4, M_HEADS)
    g = jnp.transpose(g, (2, 0, 3, 1))
    return (g[0], jax.nn.log_sigmoid(g[1]), g[2], jax.nn.log_sigmoid(g[3]))


def mlstm_zero_state(b):
    return (jnp.zeros((b, M_HEADS, M_DQK, M_DV), jnp.float32),
            jnp.zeros((b, M_HEADS, M_DQK), jnp.float32),
            jnp.zeros((b, M_HEADS), jnp.float32))


def mlstm_final_state(k, v, ig, lf):
    k = k.astype(jnp.float32)
    v = v.astype(jnp.float32)
    bcum = jnp.cumsum(lf, axis=-1)
    w_s = bcum[..., -1:] - bcum + ig
    m = jnp.max(w_s, axis=-1)
    ws = jnp.exp(w_s - m[..., None])
    c_state = jnp.einsum("bhs,bhsd,bhsv->bhdv", ws, k, v)
    n_state = jnp.einsum("bhs,bhsd->bhd", ws, k)
    return (c_state, n_state, m)


def mlstm_chunkwise(q, k, v, ig, lf, state0):
    b, h, t, _ = q.shape
    nc = t // M_CHUNK

    def to_chunks(a):
        a = a.astype(jnp.float32)
        return jnp.moveaxis(a.reshape(a.shape[:2] + (nc, M_CHUNK) + a.shape[3:]), 2, 0)

    xs = (to_chunks(q * (M_DQK ** -0.5)), to_chunks(k), to_chunks(v), to_chunks(ig), to_chunks(lf))
    mask = jnp.tril(jnp.ones((M_CHUNK, M_CHUNK), dtype=bool))

    def step(carry, inp):
        c_prev, n_prev, m_prev = carry
        qc, kc, vc, ic, fc = inp
        bcum = jnp.cumsum(fc, axis=-1)
        d = bcum[..., :, None] - bcum[..., None, :] + ic[..., None, :]
        d = jnp.where(mask, d, -jnp.inf)
        inter = bcum + m_prev[..., None]
        m_row = jnp.maximum(inter, jnp.max(d, axis=-1))
        s_inter = jnp.exp(inter - m_row)
        qk = jnp.einsum("bhtd,bhsd->bhts", qc, kc) * jnp.exp(d - m_row[..., None])
        num = (jnp.einsum("bhts,bhsv->bhtv", qk, vc)
               + s_inter[..., None] * jnp.einsum("bhtd,bhdv->bhtv", qc, c_prev))
        den = jnp.sum(qk, axis=-1) + s_inter * jnp.einsum("bhtd,bhd->bht", qc, n_prev)
        h_out = num / jnp.maximum(jnp.abs(den), jnp.exp(-m_row))[..., None]
        b_tot = bcum[..., -1]
        w_s = b_tot[..., None] - bcum + ic
        m_new = jnp.maximum(b_tot + m_prev, jnp.max(w_s, axis=-1))
        decay = jnp.exp(b_tot + m_prev - m_new)
        ws = jnp.exp(w_s - m_new[..., None])
        c_new = decay[..., None, None] * c_prev + jnp.einsum("bhs,bhsd,bhsv->bhdv", ws, kc, vc)
        n_new = decay[..., None] * n_prev + jnp.einsum("bhs,bhsd->bhd", ws, kc)
        return (c_new, n_new, m_new), h_out

    _, hs = lax.scan(step, state0, xs)
    return jnp.moveaxis(hs, 0, 2).reshape(b, h, t, M_DV)


def mlstm_bidir(q, k, v, gates, state_f, state_b):
    ig_f, lf_f, ig_b, lf_b = gates
    h_f = mlstm_chunkwise(q, k, v, ig_f, lf_f, state_f)
    h_b = flip_seq(mlstm_chunkwise(flip_seq(q), flip_seq(k), flip_seq(v),
                                   flip_seq(ig_b), flip_seq(lf_b), state_b))
    return h_f + h_b


def axial_angles(rows):
    row = jnp.repeat(jnp.arange(rows), GRID_W).astype(jnp.float32)
    col = jnp.tile(jnp.arange(GRID_W), rows).astype(jnp.float32)
    freqs = ROPE_THETA ** (-jnp.arange(ROPE_FREQS, dtype=jnp.float32) / ROPE_FREQS)
    return jnp.stack([row[:, None] * freqs, col[:, None] * freqs], axis=1)


def axial_rope(t, ang):
    b, s, h, _ = t.shape
    nope, rope = t[..., :A_NOPE], t[..., A_NOPE:]
    r = rope.reshape(b, s, h, 2, 2, ROPE_FREQS)
    cos = jnp.cos(ang)[None, :, None].astype(t.dtype)
    sin = jnp.sin(ang)[None, :, None].astype(t.dtype)
    x1, x2 = r[..., 0, :], r[..., 1, :]
    rot = jnp.stack([x1 * cos - x2 * sin, x1 * sin + x2 * cos], axis=-2).reshape(b, s, h, A_ROPE)
    return jnp.concatenate([nope, rot], axis=-1)


def mla_kv(ckv, k_rope, kv_norm_g, w_uk, w_uv, k_norm_g):
    b, t, _ = ckv.shape
    cn = rmsnorm(ckv, kv_norm_g)
    k_nope = (cn @ w_uk).reshape(b, t, A_HEADS, A_NOPE)
    v = (cn @ w_uv).reshape(b, t, A_HEADS, A_DV)
    k_r = jnp.broadcast_to(k_rope[:, :, None, :], (b, t, A_HEADS, A_ROPE))
    k = rmsnorm(jnp.concatenate([k_nope, k_r], axis=-1), k_norm_g)
    return k, v


def mla_q(qa, q_norm_g):
    b, t, _ = qa.shape
    return rmsnorm(qa.reshape(b, t, A_HEADS, A_QK), q_norm_g)


def block_attention(q, k, v):
    b, s, h, dh = q.shape
    nb = s // Q_BLOCK
    qb = jnp.moveaxis(q.reshape(b, nb, Q_BLOCK, h, dh), 1, 0)
    scale = dh ** -0.5

    def one_block(qblk):
        sc = jnp.einsum("bqhd,bkhd->bhqk", qblk, k, preferred_element_type=jnp.float32) * scale
        p = jax.nn.softmax(sc, axis=-1).astype(v.dtype)
        return jnp.einsum("bhqk,bkhd->bqhd", p, v)

    o = lax.map(one_block, qb)
    return jnp.moveaxis(o, 0, 1).reshape(b, s, h * v.shape[-1])


def merge_branches(h_m, o_gate, z_m, o_a, z_a, g_merge, mh_norm_g, w_proj_m, w_proj_a, w_out):
    b, _, t, _ = h_m.shape
    hm = rmsnorm(h_m.transpose(0, 2, 1, 3), mh_norm_g.reshape(M_HEADS, M_DV)).reshape(b, t, M_V_W)
    hm = hm.astype(z_m.dtype) * jax.nn.sigmoid(o_gate) * jax.nn.silu(z_m)
    p_m = hm @ w_proj_m
    p_a = (o_a * jax.nn.silu(z_a)) @ w_proj_a
    g_m, g_a = jnp.split(jax.nn.sigmoid(g_merge), 2, axis=-1)
    return (g_m * p_m + g_a * p_a) @ w_out


def hybrid_layer(x, ctx, c, c_ctx, ada_w, ada_b, norm_g, w_in, conv_w, conv_b, gate_b, mh_norm_g,
                 q_norm_g, k_norm_g, kv_norm_g, w_uk, w_uv, w_proj_m, w_proj_a, w_out, ang, update_ctx):
    shift, scale, gate = adaln(c, ada_w, ada_b)
    shift_c, scale_c, gate_c = adaln(c_ctx, ada_w, ada_b)
    h = rmsnorm(x, norm_g) * (1 + scale[:, None]) + shift[:, None]
    hc = rmsnorm(ctx, norm_g) * (1 + scale_c) + shift_c
    proj = h @ w_in
    proj_c = hc @ (w_in if update_ctx else w_in[:, :KV_COLS])
    cw_q, cw_k = conv_w[:, :M_QK_W], conv_w[:, M_QK_W:]
    cb_q, cb_k = conv_b[:M_QK_W], conv_b[M_QK_W:]

    km_c, vm_c, gt_c, ckv_c, kr_c = split_cols(proj_c[..., :KV_COLS], KV_SIZES)
    k_mc = mlstm_qk(km_c, cw_k, cb_k)
    v_mc = mlstm_heads(vm_c, M_DV)
    gates_c = mlstm_gates(gt_c, gate_b)
    state_f = mlstm_final_state(k_mc, v_mc, gates_c[0], gates_c[1])
    state_b = mlstm_final_state(flip_seq(k_mc), flip_seq(v_mc), flip_seq(gates_c[2]), flip_seq(gates_c[3]))
    k_ac, v_ac = mla_kv(ckv_c, kr_c, kv_norm_g, w_uk, w_uv, k_norm_g)

    km, vm, gt, ckv, kr = split_cols(proj[..., :KV_COLS], KV_SIZES)
    qm, om, zm, qa, za, gm = split_cols(proj[..., KV_COLS:], Q_SIZES)
    h_m = mlstm_bidir(mlstm_qk(qm, cw_q, cb_q), mlstm_qk(km, cw_k, cb_k), mlstm_heads(vm, M_DV),
                      mlstm_gates(gt, gate_b), state_f, state_b)
    k_al, v_al = mla_kv(ckv, kr, kv_norm_g, w_uk, w_uv, k_norm_g)
    k_al = axial_rope(k_al, ang)
    q_al = axial_rope(mla_q(qa, q_norm_g), ang)
    o_a = block_attention(q_al, jnp.concatenate([k_ac, k_al], axis=1),
                          jnp.concatenate([v_ac, v_al], axis=1))
    x = x + gate[:, None] * merge_branches(h_m, om, zm, o_a, za, gm, mh_norm_g, w_proj_m, w_proj_a, w_out)

    if update_ctx:
        qm_c, om_c, zm_c, qa_c, za_c, gm_c = split_cols(proj_c[..., KV_COLS:], Q_SIZES)
        zero = mlstm_zero_state(ctx.shape[0])
        h_mc = mlstm_bidir(mlstm_qk(qm_c, cw_q, cb_q), k_mc, v_mc, gates_c, zero, zero)
        o_ac = block_attention(mla_q(qa_c, q_norm_g), k_ac, v_ac)
        ctx = ctx + gate_c * merge_branches(h_mc, om_c, zm_c, o_ac, za_c, gm_c, mh_norm_g,
                                            w_proj_m, w_proj_a, w_out)
    return x, ctx


def setup_inputs(seed: int = 0) -> dict:
    key = jax.random.key(seed)
    ks = jax.random.split(key, 24)

    def nrm(k, shape, s):
        return jax.random.normal(k, shape, jnp.float32) * s

    fbias = jnp.linspace(FGATE_BIAS_LO, FGATE_BIAS_HI, M_HEADS, dtype=jnp.float32)
    gate_b = jnp.concatenate([
        nrm(ks[10], (DEPTH, M_HEADS), 0.1),
        fbias + nrm(ks[11], (DEPTH, M_HEADS), 0.1),
        nrm(ks[12], (DEPTH, M_HEADS), 0.1),
        fbias + nrm(ks[13], (DEPTH, M_HEADS), 0.1)], axis=-1)
    return {
        "x": nrm(ks[0], (BATCH, SEQ, D_MODEL), 1.0),
        "c": nrm(ks[1], (BATCH, D_MODEL), 1.0),
        "ctx": nrm(ks[2], (BATCH, CTX_LEN, D_MODEL), 1.0),
        "c_ctx": nrm(ks[3], (D_MODEL,), 1.0),
        "ada_w": nrm(ks[4], (DEPTH, D_MODEL, 3 * D_MODEL), 0.5 * D_MODEL ** -0.5),
        "ada_b": nrm(ks[5], (DEPTH, 3 * D_MODEL), 0.02),
        "norm_g": 1.0 + nrm(ks[6], (DEPTH, D_MODEL), 0.02),
        "w_in": nrm(ks[7], (DEPTH, D_MODEL, IN_COLS), D_MODEL ** -0.5),
        "conv_w": nrm(ks[8], (DEPTH, M_CONV, 2 * M_QK_W), M_CONV ** -0.5),
        "conv_b": nrm(ks[9], (DEPTH, 2 * M_QK_W), 0.02),
        "gate_b": gate_b,
        "mh_norm_g": 1.0 + nrm(ks[14], (DEPTH, M_V_W), 0.02),
        "q_norm_g": 1.0 + nrm(ks[15], (DEPTH, A_QK), 0.02),
        "k_norm_g": 1.0 + nrm(ks[16], (DEPTH, A_QK), 0.02),
        "kv_norm_g": 1.0 + nrm(ks[17], (DEPTH, KV_RANK), 0.02),
        "w_uk": nrm(ks[18], (DEPTH, KV_RANK, A_HEADS * A_NOPE), KV_RANK ** -0.5),
        "w_uv": nrm(ks[19], (DEPTH, KV_RANK, A_V_W), KV_RANK ** -0.5),
        "w_proj_m": nrm(ks[20], (DEPTH, M_V_W, D_MODEL), M_V_W ** -0.5),
        "w_proj_a": nrm(ks[21], (DEPTH, A_V_W, D_MODEL), A_V_W ** -0.5),
        "w_out": nrm(ks[22], (DEPTH, D_MODEL, D_MODEL), D_MODEL ** -0.5),
    }


def reference(x, c, ctx, c_ctx, ada_w, ada_b, norm_g, w_in, conv_w, conv_b, gate_b, mh_norm_g,
              q_norm_g, k_norm_g, kv_norm_g, w_uk, w_uv, w_proj_m, w_proj_a, w_out):
    rows = x.shape[1] // GRID_W
    ang = axial_angles(rows)
    for layer in range(DEPTH):
        x, ctx = hybrid_layer(x, ctx, c, c_ctx, ada_w[layer], ada_b[layer], norm_g[layer], w_in[layer],
                              conv_w[layer], conv_b[layer], gate_b[layer], mh_norm_g[layer],
                              q_norm_g[layer], k_norm_g[layer], kv_norm_g[layer], w_uk[layer],
                              w_uv[layer], w_proj_m[layer], w_proj_a[layer], w_out[layer], ang,
                              layer + 1 < DEPTH)
    return x
```

```python
import contextlib
import numpy as np
import concourse.bass as bass
import concourse.mybir as mybir
from concourse.bass_utils import run_bass_kernel_spmd

F32 = mybir.dt.float32
BF16 = mybir.dt.bfloat16
AF = mybir.ActivationFunctionType
ALU = mybir.AluOpType
AX = mybir.AxisListType

D = 2048
KC = 16
NCTX = 256
NOWN = 2048
NALL = 4352
EPS = 1e-6
MH, MDQK, MDV = 8, 128, 256
AH, ANOPE, AROPE, ADV = 16, 128, 64, 128
KVR = 512
CH = 128
NCHUNK = NALL // CH

C_KM = 0
C_GT = 1024
C_CKV = 1056
C_KR = 1568
C_KRS = 1632
C_VM = 1696
C_QM = 3744
C_OZ = 4768
C_QA = 8864
C_ZA = 12960
C_GM = 15008
NCOLS = 19104

ATT_SHIFT = 8.0


class Tok:
    __slots__ = ("sem", "val", "eng")

    def __init__(self, sem, val, eng):
        self.sem, self.val, self.eng = sem, val, eng


class Buf:
    __slots__ = ("name", "w", "r")

    def __init__(self, name=""):
        self.name = name
        self.w = None
        self.r = {}


class Eng:
    def __init__(self, ctx, name, eng):
        self.ctx, self.name, self.eng = ctx, name, eng
        self.sem = None
        self.n = 0
        self.waited = {}
        self.new_sem()

    def new_sem(self):
        self.sem = self.ctx.es.enter_context(self.ctx.nc.semaphore(f"s_{self.name}_{self.ctx.uid()}"))
        self.n = 0

    def wait(self, tok):
        if tok is None:
            return
        k = id(tok.sem)
        if self.waited.get(k, 0) >= tok.val:
            return
        self.eng.wait_ge(tok.sem, tok.val)
        self.waited[k] = tok.val


class Ctx:
    def __init__(self, nc):
        self.nc = nc
        self.es = contextlib.ExitStack()
        self._uid = 0
        self.pe = Eng(self, "pe", nc.tensor)
        self.act = Eng(self, "act", nc.scalar)
        self.dve = Eng(self, "dve", nc.vector)
        self.pool = Eng(self, "pool", nc.gpsimd)
        self.sp = Eng(self, "sp", nc.sync)
        self.dma_sems = [self.es.enter_context(nc.semaphore(f"dq{i}")) for i in range(32)]
        self.dma_cnt = [0] * 32
        self.dma_last = [None] * 32
        self.dma_i = 0
        self.all_dma_toks = []
        self.fence_scr = None

    def uid(self):
        self._uid += 1
        return self._uid

    def fresh_sems(self):
        for e in (self.pe, self.act, self.dve, self.pool):
            e.new_sem()

    def _deps(self, eng, reads, writes):
        deps = []
        for b in reads:
            if b.w is not None:
                deps.append(b.w)
        for b in writes:
            if b.w is not None and b.w.eng is not eng:
                deps.append(b.w)
            for t in b.r.values():
                if t.eng is not eng:
                    deps.append(t)
        return deps

    def _wait_all(self, eng, deps, pe_skip_self=False):
        for d in deps:
            if pe_skip_self and d.eng is eng:
                continue
            eng.wait(d)

    def _record(self, tok, reads, writes):
        for b in reads:
            k = id(tok.sem)
            b.r[k] = tok
        for b in writes:
            b.w = tok
            b.r = {}

    def op(self, eng, fn, reads=(), writes=()):
        deps = self._deps(eng, reads, writes)
        self._wait_all(eng, deps, pe_skip_self=(eng is self.pe))
        inst = fn()
        if eng is not self.pe and self.fence_scr is not None:
            scr = self.fence_scr[eng.name]
            if eng is self.act:
                inst = self.nc.scalar.copy(out=scr[:, 1:2], in_=scr[:, 0:1])
            elif eng is self.dve:
                inst = self.nc.vector.tensor_copy(out=scr[:, 1:2], in_=scr[:, 0:1])
            else:
                inst = self.nc.gpsimd.tensor_copy(out=scr[:, 1:2], in_=scr[:, 0:1])
        eng.n += 1
        inst.then_inc(eng.sem, 1)
        tok = Tok(eng.sem, eng.n, eng)
        self._record(tok, reads, writes)
        return tok

    def mm_group(self, out_ap, pairs, reads, writes, transpose=False):
        eng = self.pe
        deps = self._deps(eng, reads, writes)
        self._wait_all(eng, deps, pe_skip_self=True)
        n = len(pairs)
        inst = None
        for i, (l, r) in enumerate(pairs):
            inst = self.nc.tensor.matmul(out_ap, lhsT=l, rhs=r, start=(i == 0), stop=(i == n - 1))
        eng.n += 1
        inst.then_inc(eng.sem, 1)
        tok = Tok(eng.sem, eng.n, eng)
        self._record(tok, reads, writes)
        return tok

    def mm1(self, out_ap, l, r, start, stop, reads, writes, signal):
        eng = self.pe
        deps = self._deps(eng, reads, writes)
        self._wait_all(eng, deps, pe_skip_self=True)
        inst = self.nc.tensor.matmul(out_ap, lhsT=l, rhs=r, start=start, stop=stop)
        if signal:
            eng.n += 1
            inst.then_inc(eng.sem, 1)
            tok = Tok(eng.sem, eng.n, eng)
            self._record(tok, reads, writes)
        else:
            tok = Tok(eng.sem, eng.n + 1, eng)
            for b in reads:
                b.r[id(tok.sem)] = tok
        return None

    def dma(self, out, in_, reads=(), writes=(), q=None):
        eng = q or self.sp
        deps = self._deps(eng, reads, writes)
        self._wait_all(eng, deps)
        j = self.dma_i % len(self.dma_sems)
        self.dma_i += 1
        if self.dma_last[j] is not None:
            eng.wait(self.dma_last[j])
        inst = eng.eng.dma_start(out=out, in_=in_)
        self.dma_cnt[j] += 16
        inst.then_inc(self.dma_sems[j], 16)
        tok = Tok(self.dma_sems[j], self.dma_cnt[j], None)
        self.dma_last[j] = tok
        self._record(tok, reads, writes)
        self.all_dma_toks.append(tok)
        return tok

    def sb(self, name, shape, dtype):
        return self.es.enter_context(self.nc.sbuf_tensor(name, list(shape), dtype))


def build(debug=()):
    nc = bass.Bass("TRN2", target_bir_lowering=False)
    outer = contextlib.ExitStack()
    outer.enter_context(nc.allow_low_precision("bf16 matmul operands, fp32 accumulation"))
    outer.enter_context(nc.allow_non_contiguous_dma("small strided tiles"))
    cx = Ctx(nc)
    dbg = set(debug)

    def din(name, shape, dt=F32):
        return nc.dram_tensor(name, list(shape), dt, kind="ExternalInput").ap()

    def dscr(name, shape, dt):
        kind = "ExternalOutput" if name in dbg else "Internal"
        return nc.dram_tensor(name, list(shape), dt, kind=kind).ap()

    xa = din("xa", [NALL, D])
    cT = din("cT", [128, KC, 2])
    ada_w = din("ada_w", [D, 3 * D])
    ada_bT = din("ada_bT", [128, 48])
    norm_gT = din("norm_gT", [128, KC])
    w_in = din("w_in", [D, NCOLS])
    conv_wk = din("conv_wk", [128, MH, 5])
    conv_wq = din("conv_wq", [128, MH, 5])
    conv_bk = din("conv_bk", [128, MH])
    conv_bq = din("conv_bq", [128, MH])
    gate_b = din("gate_b", [32, 1])
    mhgT = din("mhgT", [128, KC])
    qg = din("qg", [128, 4])
    kg = din("kg", [128, 4])
    kvgT = din("kvgT", [128, 4])
    w_uk = din("w_uk", [KVR, AH * ANOPE])
    w_uv = din("w_uv", [KVR, AH * ADV])
    w_pm = din("w_pm", [D, D])
    w_pa = din("w_pa", [D, D])
    w_out = din("w_out", [D, D])
    cosT = din("cosT", [64, NALL])
    sinT = din("sinT", [64, NALL])
    ident_f = din("ident_f", [128, 128])
    masks = din("masks", [128, 2, 128])
    sel = din("sel", [32, 6, 32])
    selb = din("selb", [32, 2])
    hmask = din("hmask", [32, 16])
    eones_d = din("eones", [32, 128])
    out_d = nc.dram_tensor("out", [NOWN, D], F32, kind="ExternalOutput").ap()

    hT_d = dscr("hT_d", [D, NALL], BF16)
    gateb_d = dscr("gateb_d", [1, D], F32)
    rawk_d = dscr("rawk_d", [MH, 128, NALL], F32)
    rawq_d = dscr("rawq_d", [MH, 128, NOWN + 2], F32)
    graw_d = dscr("graw_d", [32, NALL], F32)
    ckv_d = dscr("ckv_d", [KVR, NALL], F32)
    kr_d = dscr("kr_d", [128, NALL], F32)
    vm_d = dscr("vm_d", [NALL, MH * MDV], BF16)
    gmT_d = dscr("gmT_d", [D, NOWN], BF16)
    qaT_d = dscr("qaT_d", [AH, 192, NOWN], BF16)
    szaT_d = dscr("szaT_d", [D, NOWN], BF16)
    ggT_d = dscr("ggT_d", [2 * D, NOWN], BF16)
    kT_d = dscr("kT_d", [MH, 128, NALL], BF16)
    qT_d = dscr("qT_d", [MH, 128, NOWN], BF16)
    hA_d = dscr("hA_d", [NOWN, MH * MDV], F32)
    hmgT_d = dscr("hmgT_d", [D, NOWN], BF16)
    kaT_d = dscr("kaT_d", [AH, 192, NALL], BF16)
    va_d = dscr("va_d", [NALL, AH * ADV], BF16)
    ozT_d = dscr("ozT_d", [D, NOWN], BF16)
    mixT_d = dscr("mixT_d", [D, NOWN], BF16)

    es = cx.es
    banks = [es.enter_context(nc.psum_tensor(f"pb{i}", [128, 512], F32)) for i in range(8)]
    bbuf = [Buf(f"bank{i}") for i in range(8)]

    identf = cx.sb("identf", [128, 128], F32)
    identb = cx.sb("identb", [128, 128], BF16)
    onesb = cx.sb("onesb", [128, 128], BF16)
    modc = cx.sb("modc", [128, 48, 2], F32)
    gsc = cx.sb("gsc", [128, KC, 2], F32)
    B_const = Buf("const")
    B_mod = Buf("mod")
    fscr = {n_: cx.sb(f"fscr_{n_}", [128, 2], F32) for n_ in ("act", "dve", "pool")}
    nc.gpsimd.memset(fscr["pool"][:], 0.0)
    nc.vector.memset(fscr["dve"][:], 0.0)
    nc.vector.memset(fscr["act"][:], 0.0).then_inc(cx.dve.sem, 1)
    cx.dve.n += 1
    cx.act.wait(Tok(cx.dve.sem, cx.dve.n, cx.dve))
    cx.fence_scr = fscr

    cx.dma(identf[:], ident_f[:, :], writes=[B_const])
    cx.op(cx.dve, lambda: nc.vector.tensor_copy(out=identb[:], in_=identf[:]), reads=[B_const], writes=[B_const])
    cx.op(cx.pool, lambda: nc.gpsimd.memset(onesb[:], 1.0), writes=[B_const])

    stage_es = contextlib.ExitStack()

    def stage_begin():
        nonlocal stage_es
        stage_es = contextlib.ExitStack()
        return stage_es

    def st(name, shape, dtype):
        return stage_es.enter_context(nc.sbuf_tensor(name + f"_{cx.uid()}", list(shape), dtype))

    def stage_end():
        toks = []
        for e in (cx.pe, cx.act, cx.dve, cx.pool):
            if e.n > 0:
                toks.append(Tok(e.sem, e.n, e))
        for j, t in enumerate(cx.dma_last):
            if t is not None:
                toks.append(t)
        for e in (cx.pe, cx.act, cx.dve, cx.pool, cx.sp):
            for t in toks:
                e.wait(t)
        stage_es.close()

    stage_begin()
    ct_sb = st("ct", [128, KC, 2], F32)
    sc_sb = st("sc", [128, KC, 2], F32)
    adab = st("adab", [128, 48], F32)
    ngt = st("ngt", [128, KC], F32)
    B_ct = Buf("ct")
    cx.dma(ct_sb[:], cT[:, :, :], writes=[B_ct])
    cx.dma(adab[:], ada_bT[:, :], writes=[B_ct])
    cx.dma(ngt[:], norm_gT[:, :], writes=[B_ct])
    cx.op(cx.act, lambda: nc.scalar.activation(out=sc_sb[:], in_=ct_sb[:], func=AF.Sigmoid), reads=[B_ct], writes=[B_ct])
    cx.op(cx.dve, lambda: nc.vector.tensor_tensor(out=sc_sb[:], in0=sc_sb[:], in1=ct_sb[:], op=ALU.mult),
          reads=[B_ct], writes=[B_ct])
    awt = [st(f"awt{i}", [128, KC, 512], F32) for i in range(2)]
    B_awt = [Buf("awt0"), Buf("awt1")]
    ada_v = ada_w.rearrange("(kc p) n -> p kc n", p=128)
    for cb in range(12):
        sl = cb % 2
        cx.dma(awt[sl][:], ada_v[:, :, cb * 512:(cb + 1) * 512], writes=[B_awt[sl]])
        for sub in range(4):
            j = cb * 4 + sub
            bk = j % 2
            pairs = [(awt[sl][:, kc, sub * 128:(sub + 1) * 128], sc_sb[:, kc, :]) for kc in range(KC)]
            cx.mm_group(banks[bk][:, 0:2], pairs, reads=[B_awt[sl], B_ct], writes=[bbuf[bk]])
            cx.op(cx.dve, lambda j=j, bk=bk: nc.vector.tensor_scalar(
                out=modc[:, j, :], in0=banks[bk][:, 0:2], scalar1=adab[:, j:j + 1], scalar2=None, op0=ALU.add),
                reads=[bbuf[bk], B_ct], writes=[B_mod])
    for i in range(2):
        cx.op(cx.dve, lambda i=i: nc.vector.scalar_tensor_tensor(
            out=gsc[:, :, i], in0=modc[:, 16:32, i], scalar=1.0, in1=ngt[:], op0=ALU.add, op1=ALU.mult),
            reads=[B_mod, B_ct], writes=[B_mod])
    gcol = st("gcol", [128, KC], F32)
    grow = st("grow", [16, 128], F32)
    cx.op(cx.dve, lambda: nc.vector.tensor_copy(out=gcol[:], in_=modc[:, 32:48, 0]), reads=[B_mod], writes=[B_ct])
    cx.op(cx.pe, lambda: nc.tensor.transpose(banks[2][0:16, 0:128], gcol[:], identf[:]), reads=[B_ct, B_const],
          writes=[bbuf[2]])
    cx.op(cx.dve, lambda: nc.vector.tensor_copy(out=grow[:], in_=banks[2][0:16, 0:128]), reads=[bbuf[2]], writes=[B_ct])
    B_gateb = Buf("gateb_d")
    cx.dma(gateb_d.rearrange("o (j p) -> (o j) p", p=128), grow[:], reads=[B_ct], writes=[B_gateb])
    stage_end()

    stage_begin()
    B_hT = Buf("hT_d")
    xt = [st(f"xt{i}", [128, D], F32) for i in range(2)]
    xs = [st(f"xs{i}", [128, D], F32) for i in range(2)]
    sq = st("sqjunk", [128, D], F32)
    hts = [st(f"hts{i}", [128, KC, 128], BF16) for i in range(2)]
    stat = [st(f"stat{i}", [128, 4], F32) for i in range(2)]
    B_xt = [Buf(), Buf()]
    B_xs = [Buf(), Buf()]
    B_sq = Buf()
    B_hts = [Buf(), Buf()]
    B_stat = [Buf(), Buf()]
    ntile = NALL // 128
    cx.dma(xt[0][:], xa[0:128, :], writes=[B_xt[0]])
    for ti in range(ntile):
        sl = ti % 2
        if ti + 1 < ntile:
            cx.dma(xt[1 - sl][:], xa[(ti + 1) * 128:(ti + 2) * 128, :], writes=[B_xt[1 - sl]])
        mi = 1 if ti < 2 else 0
        cx.op(cx.act, lambda sl=sl: nc.scalar.activation(out=sq[:], in_=xt[sl][:], func=AF.Square,
                                                          accum_out=stat[sl][:, 0:1]),
              reads=[B_xt[sl]], writes=[B_sq, B_stat[sl]])
        cx.op(cx.act, lambda sl=sl: nc.scalar.activation(out=stat[sl][:, 1:2], in_=stat[sl][:, 0:1], func=AF.Sqrt,
                                                          scale=1.0 / D, bias=EPS),
              reads=[B_stat[sl]], writes=[B_stat[sl]])
        cx.op(cx.dve, lambda sl=sl: nc.vector.reciprocal(out=stat[sl][:, 2:3], in_=stat[sl][:, 1:2]),
              reads=[B_stat[sl]], writes=[B_stat[sl]])
        cx.op(cx.pool, lambda sl=sl: nc.gpsimd.tensor_scalar(out=xs[sl][:], in0=xt[sl][:], scalar1=stat[sl][:, 2:3],
                                                              scalar2=None, op0=ALU.mult),
              reads=[B_xt[sl], B_stat[sl]], writes=[B_xs[sl]])
        for g in range(4):
            bk = (ti * 4 + g) % 8
            for q in range(4):
                kc = g * 4 + q
                cx.op(cx.pe, lambda kc=kc, bk=bk, q=q, sl=sl: nc.tensor.transpose(
                    banks[bk][:, q * 128:(q + 1) * 128], xs[sl][:, kc * 128:(kc + 1) * 128], identf[:]),
                    reads=[B_xs[sl], B_const], writes=[bbuf[bk]])
            for q in range(4):
                kc = g * 4 + q
                e = cx.act if q % 2 == 0 else cx.dve
                if e is cx.act:
                    cx.op(e, lambda kc=kc, bk=bk, q=q, sl=sl, mi=mi: nc.scalar.activation(
                        out=hts[sl][:, kc, :], in_=banks[bk][:, q * 128:(q + 1) * 128], func=AF.Identity,
                        scale=gsc[:, kc, mi:mi + 1], bias=modc[:, kc, mi:mi + 1]),
                        reads=[bbuf[bk], B_mod], writes=[B_hts[sl]])
                else:
                    cx.op(e, lambda kc=kc, bk=bk, q=q, sl=sl, mi=mi: nc.vector.tensor_scalar(
                        out=hts[sl][:, kc, :], in0=banks[bk][:, q * 128:(q + 1) * 128],
                        scalar1=gsc[:, kc, mi:mi + 1], scalar2=modc[:, kc, mi:mi + 1], op0=ALU.mult, op1=ALU.add),
                        reads=[bbuf[bk], B_mod], writes=[B_hts[sl]])
        cx.dma(hT_d.rearrange("(kc p) a -> p kc a", p=128)[:, :, ti * 128:(ti + 1) * 128], hts[sl][:],
               reads=[B_hts[sl]], writes=[B_hT])
    stage_end()


    B_rawk, B_rawq, B_graw, B_ckv, B_kr, B_vm = Buf(), Buf(), Buf(), Buf(), Buf(), Buf()
    B_gmT, B_qaT, B_szaT, B_ggT = Buf(), Buf(), Buf(), Buf()
    hT_v = hT_d.rearrange("(kc p) a -> p kc a", p=128)
    w_v = w_in.rearrange("(kc p) n -> p kc n", p=128)
    for ps in range(2):
        stage_begin()
        if ps == 0:
            A0, NT = 0, NOWN + NCTX + 2
        else:
            A0, NT = NOWN + NCTX, NOWN
        hres = st("hres", [128, KC, NT], BF16)
        B_h = Buf()
        for kc in range(KC):
            cx.dma(hres[:, kc, :], hT_v[:, kc, A0:A0 + NT], reads=[B_hT], writes=[B_h])
        wf = [st(f"wf{i}", [128, KC, 256], F32) for i in range(2)]
        wb = [st(f"wb{i}", [128, KC, 256], BF16) for i in range(2)]
        B_wf = [Buf(), Buf()]
        B_wb = [Buf(), Buf()]
        tf = [st(f"tf{i}", [128, 512], F32) for i in range(6)]
        tb = [st(f"tb{i}", [128, 512], BF16) for i in range(4)]
        B_tf = [Buf() for _ in range(6)]
        B_tb = [Buf() for _ in range(4)]
        rr = {"tf": 0, "tb": 0, "bk": 0, "w": 0, "ev": 0}

        def nxt(kind, n):
            i = rr[kind] % n
            rr[kind] += 1
            return i

        if ps == 0:
            cos_sb = st("cos_sb", [64, NOWN], F32)
            sin_sb = st("sin_sb", [64, NOWN], F32)
            qg_sb = st("qg_sb", [128, 4], F32)
            B_tab = Buf()
            cx.dma(cos_sb[:], cosT[:, NCTX:NCTX + NOWN], writes=[B_tab])
            cx.dma(sin_sb[:], sinT[:, NCTX:NCTX + NOWN], writes=[B_tab])
            cx.dma(qg_sb[:], qg[:, :], writes=[B_tab])
            cx.op(cx.dve, lambda: nc.vector.tensor_scalar(out=cos_sb[:], in0=cos_sb[:], scalar1=qg_sb[0:64, 1:2],
                                                          scalar2=None, op0=ALU.mult), reads=[B_tab], writes=[B_tab])
            cx.op(cx.dve, lambda: nc.vector.tensor_scalar(out=sin_sb[:], in0=sin_sb[:], scalar1=qg_sb[0:64, 2:3],
                                                          scalar2=None, op0=ALU.mult), reads=[B_tab], writes=[B_tab])

        pending = {}

        def load_group(c0, width):
            i = nxt("w", 2)
            cx.dma(wf[i][:, :, 0:width], w_v[:, :, c0:c0 + width], writes=[B_wf[i]])
            cx.op(cx.pool, lambda: nc.gpsimd.tensor_copy(out=wb[i][:, :, 0:width], in_=wf[i][:, :, 0:width]),
                  reads=[B_wf[i]], writes=[B_wb[i]])
            return i

        def fm(wi, off, M, a0, n):
            bk = nxt("bk", 8)
            pairs = [(wb[wi][:, kc, off:off + M], hres[:, kc, a0 - A0:a0 - A0 + n]) for kc in range(KC)]
            cx.mm_group(banks[bk][0:M, 0:n], pairs, reads=[B_wb[wi], B_h], writes=[bbuf[bk]])
            return bk

        def dump_raw(bk, M, n, dst_ap, dst_buf):
            i = nxt("tf", 6)
            if nxt("ev", 2) == 0:
                cx.op(cx.act, lambda: nc.scalar.copy(out=tf[i][0:M, 0:n], in_=banks[bk][0:M, 0:n]),
                      reads=[bbuf[bk]], writes=[B_tf[i]])
            else:
                cx.op(cx.dve, lambda: nc.vector.tensor_copy(out=tf[i][0:M, 0:n], in_=banks[bk][0:M, 0:n]),
                      reads=[bbuf[bk]], writes=[B_tf[i]])
            cx.dma(dst_ap, tf[i][0:M, 0:n], reads=[B_tf[i]], writes=[dst_buf])

        def blocks(a_lo, a_hi):
            a = a_lo
            while a < a_hi:
                n = min(512, a_hi - a)
                yield a, n
                a += n

        AKV0, AKV1 = (0, NCTX + NOWN) if ps == 0 else (NCTX + NOWN, NALL)
        for g in range(4):
            wi = load_group(C_KM + g * 256, 256)
            for sub in range(2):
                hd = g * 2 + sub
                for a0, n in blocks(AKV0, AKV1):
                    bk = fm(wi, sub * 128, 128, a0, n)
                    dump_raw(bk, 128, n, rawk_d[hd, :, a0:a0 + n], B_rawk)
        wi = load_group(C_GT, 32)
        for a0, n in blocks(AKV0, AKV1):
            bk = fm(wi, 0, 32, a0, n)
            dump_raw(bk, 32, n, graw_d[:, a0:a0 + n], B_graw)
        for g in range(2):
            wi = load_group(C_CKV + g * 256, 256)
            for sub in range(2):
                r0 = (g * 2 + sub) * 128
                for a0, n in blocks(AKV0, AKV1):
                    bk = fm(wi, sub * 128, 128, a0, n)
                    dump_raw(bk, 128, n, ckv_d[r0:r0 + 128, a0:a0 + n], B_ckv)
        wi = load_group(C_KR, 128)
        for a0, n in blocks(AKV0, AKV1):
            bk = fm(wi, 0, 128, a0, n)
            dump_raw(bk, 128, n, kr_d[:, a0:a0 + n], B_kr)
        for g in range(8):
            wi = load_group(C_VM + g * 256, 256)
            for a0 in range(AKV0, AKV1, 128):
                bk = nxt("bk", 8)
                pairs = [(hres[:, kc, a0 - A0:a0 - A0 + 128], wb[wi][:, kc, :]) for kc in range(KC)]
                cx.mm_group(banks[bk][:, 0:256], pairs, reads=[B_wb[wi], B_h], writes=[bbuf[bk]])
                i = nxt("tb", 4)
                if nxt("ev", 2) == 0:
                    cx.op(cx.act, lambda: nc.scalar.copy(out=tb[i][:, 0:256], in_=banks[bk][:, 0:256]),
                          reads=[bbuf[bk]], writes=[B_tb[i]])
                else:
                    cx.op(cx.dve, lambda: nc.vector.tensor_copy(out=tb[i][:, 0:256], in_=banks[bk][:, 0:256]),
                          reads=[bbuf[bk]], writes=[B_tb[i]])
                cx.dma(vm_d[a0:a0 + 128, g * 256:(g + 1) * 256], tb[i][:, 0:256], reads=[B_tb[i]], writes=[B_vm])
        if ps == 0:
            Q0, Q1 = NCTX, NCTX + NOWN
            for g in range(4):
                wi = load_group(C_QM + g * 256, 256)
                for sub in range(2):
                    hd = g * 2 + sub
                    for a0, n in blocks(Q0, Q1 + 2):
                        bk = fm(wi, sub * 128, 128, a0, n)
                        dump_raw(bk, 128, n, rawq_d[hd, :, a0 - Q0:a0 - Q0 + n], B_rawq)
            for j in range(16):
                wi = load_group(C_OZ + j * 256, 256)
                for a0, n in blocks(Q0, Q1):
                    b1 = fm(wi, 0, 128, a0, n)
                    b2 = fm(wi, 128, 128, a0, n)
                    i1, i2 = nxt("tf", 6), nxt("tf", 6)
                    cx.op(cx.act, lambda: nc.scalar.activation(out=tf[i1][:, 0:n], in_=banks[b1][:, 0:n], func=AF.Sigmoid),
                          reads=[bbuf[b1]], writes=[B_tf[i1]])
                    cx.op(cx.act, lambda: nc.scalar.activation(out=tf[i2][:, 0:n], in_=banks[b2][:, 0:n], func=AF.Sigmoid),
                          reads=[bbuf[b2]], writes=[B_tf[i2]])
                    cx.op(cx.dve, lambda: nc.vector.tensor_tensor(out=tf[i2][:, 0:n], in0=banks[b2][:, 0:n],
                                                                  in1=tf[i2][:, 0:n], op=ALU.mult),
                          reads=[bbuf[b2], B_tf[i2]], writes=[B_tf[i2]])
                    ib = nxt("tb", 4)
                    cx.op(cx.dve, lambda: nc.vector.tensor_tensor(out=tb[ib][:, 0:n], in0=tf[i1][:, 0:n],
                                                                  in1=tf[i2][:, 0:n], op=ALU.mult),
                          reads=[B_tf[i1], B_tf[i2]], writes=[B_tb[ib]])
                    cx.dma(gmT_d[j * 128:(j + 1) * 128, a0 - Q0:a0 - Q0 + n], tb[ib][:, 0:n], reads=[B_tb[ib]],
                           writes=[B_gmT])
            for hd in range(AH):
                wi = load_group(C_QA + hd * 256, 256)
                for a0, n in blocks(Q0, Q1):
                    t0 = a0 - Q0
                    bn = fm(wi, 0, 128, a0, n)
                    br = fm(wi, 128, 64, a0, n)
                    bs = fm(wi, 192, 64, a0, n)
                    s1, s2 = nxt("tb", 4), nxt("tb", 4)
                    cx.op(cx.act, lambda: nc.scalar.activation(out=tb[s1][:, 0:n], in_=banks[bn][:, 0:n], func=AF.Square),
                          reads=[bbuf[bn]], writes=[B_tb[s1]])
                    cx.op(cx.act, lambda: nc.scalar.activation(out=tb[s2][0:64, 0:n], in_=banks[br][0:64, 0:n],
                                                               func=AF.Square),
                          reads=[bbuf[br]], writes=[B_tb[s2]])
                    bq = nxt("bk", 8)
                    cx.mm_group(banks[bq][:, 0:n], [(onesb[:, :], tb[s1][:, 0:n]), (onesb[0:64, :], tb[s2][0:64, 0:n])],
                                reads=[B_tb[s1], B_tb[s2], B_const], writes=[bbuf[bq]])
                    ir = nxt("tf", 6)
                    cx.op(cx.act, lambda: nc.scalar.activation(out=tf[ir][:, 0:n], in_=banks[bq][:, 0:n], func=AF.Ln,
                                                               scale=1.0 / 192.0, bias=EPS),
                          reads=[bbuf[bq]], writes=[B_tf[ir]])
                    cx.op(cx.act, lambda: nc.scalar.activation(out=tf[ir][:, 0:n], in_=tf[ir][:, 0:n], func=AF.Exp,
                                                               scale=-0.5),
                          reads=[B_tf[ir]], writes=[B_tf[ir]])
                    o1 = nxt("tb", 4)
                    cx.op(cx.dve, lambda: nc.vector.scalar_tensor_tensor(
                        out=tb[o1][:, 0:n], in0=banks[bn][:, 0:n], scalar=qg_sb[:, 0:1], in1=tf[ir][:, 0:n],
                        op0=ALU.mult, op1=ALU.mult), reads=[bbuf[bn], B_tf[ir], B_tab], writes=[B_tb[o1]])
                    cx.dma(qaT_d[hd, 0:128, t0:t0 + n], tb[o1][:, 0:n], reads=[B_tb[o1]], writes=[B_qaT])
                    ia, ib2 = nxt("tf", 6), nxt("tf", 6)
                    cx.op(cx.dve, lambda: nc.vector.tensor_tensor(out=tf[ia][0:64, 0:n], in0=banks[br][0:64, 0:n],
                                                                  in1=cos_sb[:, t0:t0 + n], op=ALU.mult),
                          reads=[bbuf[br], B_tab], writes=[B_tf[ia]])
                    cx.op(cx.dve, lambda: nc.vector.tensor_tensor(out=tf[ib2][0:64, 0:n], in0=banks[bs][0:64, 0:n],
                                                                  in1=sin_sb[:, t0:t0 + n], op=ALU.mult),
                          reads=[bbuf[bs], B_tab], writes=[B_tf[ib2]])
                    cx.op(cx.pool, lambda: nc.gpsimd.tensor_tensor(out=tf[ia][0:64, 0:n], in0=tf[ia][0:64, 0:n],
                                                                   in1=tf[ib2][0:64, 0:n], op=ALU.add),
                          reads=[B_tf[ia], B_tf[ib2]], writes=[B_tf[ia]])
                    o2 = nxt("tb", 4)
                    cx.op(cx.pool, lambda: nc.gpsimd.tensor_tensor(out=tb[o2][0:64, 0:n], in0=tf[ia][0:64, 0:n],
                                                                   in1=tf[ir][0:64, 0:n], op=ALU.mult),
                          reads=[B_tf[ia], B_tf[ir]], writes=[B_tb[o2]])
                    cx.dma(qaT_d[hd, 128:192, t0:t0 + n], tb[o2][0:64, 0:n], reads=[B_tb[o2]], writes=[B_qaT])
            for g in range(8 + 16):
                is_za = g < 8
                c0 = C_ZA + g * 256 if is_za else C_GM + (g - 8) * 256
                wi = load_group(c0, 256)
                for sub in range(2):
                    for a0, n in blocks(Q0, Q1):
                        t0 = a0 - Q0
                        bk = fm(wi, sub * 128, 128, a0, n)
                        ib = nxt("tb", 4)
                        if is_za:
                            i1 = nxt("tf", 6)
                            cx.op(cx.act, lambda: nc.scalar.activation(out=tf[i1][:, 0:n], in_=banks[bk][:, 0:n],
                                                                       func=AF.Sigmoid),
                                  reads=[bbuf[bk]], writes=[B_tf[i1]])
                            cx.op(cx.dve, lambda: nc.vector.tensor_tensor(out=tb[ib][:, 0:n], in0=banks[bk][:, 0:n],
                                                                          in1=tf[i1][:, 0:n], op=ALU.mult),
                                  reads=[bbuf[bk], B_tf[i1]], writes=[B_tb[ib]])
                            r0 = (g * 2 + sub) * 128
                            cx.dma(szaT_d[r0:r0 + 128, t0:t0 + n], tb[ib][:, 0:n], reads=[B_tb[ib]], writes=[B_szaT])
                        else:
                            cx.op(cx.act, lambda: nc.scalar.activation(out=tb[ib][:, 0:n], in_=banks[bk][:, 0:n],
                                                                       func=AF.Sigmoid),
                                  reads=[bbuf[bk]], writes=[B_tb[ib]])
                            r0 = ((g - 8) * 2 + sub) * 128
                            cx.dma(ggT_d[r0:r0 + 128, t0:t0 + n], tb[ib][:, 0:n], reads=[B_tb[ib]], writes=[B_ggT])
        stage_end()


    def barrier():
        toks = []
        for e in (cx.pe, cx.act, cx.dve, cx.pool):
            if e.n > 0:
                toks.append(Tok(e.sem, e.n, e))
        for t in cx.dma_last:
            if t is not None:
                toks.append(t)
        for e in (cx.pe, cx.act, cx.dve, cx.pool, cx.sp):
            for t in toks:
                e.wait(t)

    stage_begin()
    kTres = st("kTres", [128, MH, NALL], BF16)
    qTres = st("qTres", [128, MH, NOWN], BF16)
    LX = st("LX", [32, NALL], F32)
    YA = st("YA", [32, NALL], F32)
    B_kT, B_qT, B_LX, B_YA = Buf(), Buf(), Buf(), Buf()
    hmask_sb = st("hmask_sb", [32, 16], F32)
    eones = st("eones_sb", [32, 128], F32)
    maskf = st("maskf", [128, 2, 128], F32)
    mhg_sb = st("mhg_sb", [128, KC], F32)
    B_c5 = Buf()
    cx.dma(hmask_sb[:], hmask[:, :], writes=[B_c5])
    cx.dma(eones[:], eones_d[:, :], writes=[B_c5])
    cx.dma(maskf[:], masks[:, :, :], writes=[B_c5])
    cx.dma(mhg_sb[:], mhgT[:, :], writes=[B_c5])
    tmp_es = contextlib.ExitStack()

    def tt(name, shape, dtype):
        return tmp_es.enter_context(nc.sbuf_tensor(name + f"_{cx.uid()}", list(shape), dtype))

    craw = [tt(f"craw{i}", [128, 4 + NCTX], F32) for i in range(2)]
    lraw = [tt(f"lraw{i}", [128, 4 + 4096], F32) for i in range(2)]
    acc = tt("acc", [128, NALL], F32)
    sgm = tt("sgm", [128, NALL], F32)
    cwk = tt("cwk", [128, MH, 5], F32)
    cwq = tt("cwq", [128, MH, 5], F32)
    cbk = tt("cbk", [128, MH], F32)
    cbq = tt("cbq", [128, MH], F32)
    B_raw = [Buf(), Buf()]
    B_acc, B_sgm, B_cw = Buf(), Buf(), Buf()
    cx.dma(cwk[:], conv_wk[:, :, :], writes=[B_cw])
    cx.dma(cwq[:], conv_wq[:, :, :], writes=[B_cw])
    cx.dma(cbk[:], conv_bk[:, :], writes=[B_cw])
    cx.dma(cbq[:], conv_bq[:, :], writes=[B_cw])
    for i in range(2):
        cx.op(cx.pool, lambda i=i: nc.gpsimd.memset(craw[i][:], 0.0), writes=[B_raw[i]])
        cx.op(cx.pool, lambda i=i: nc.gpsimd.memset(lraw[i][:], 0.0), writes=[B_raw[i]])

    def conv_seg(src, dst0, n, wt, bt, hd, sl):
        cx.op(cx.dve, lambda: nc.vector.tensor_scalar(out=acc[:, dst0:dst0 + n], in0=src[:, 0:n], scalar1=wt[:, hd, 0:1],
                                                      scalar2=bt[:, hd:hd + 1], op0=ALU.mult, op1=ALU.add),
              reads=[B_raw[sl], B_cw], writes=[B_acc])
        for j in range(1, 5):
            cx.op(cx.dve, lambda j=j: nc.vector.scalar_tensor_tensor(
                out=acc[:, dst0:dst0 + n], in0=src[:, j:j + n], scalar=wt[:, hd, j:j + 1], in1=acc[:, dst0:dst0 + n],
                op0=ALU.mult, op1=ALU.add), reads=[B_raw[sl], B_cw, B_acc], writes=[B_acc])

    it = 0
    for hd in range(MH):
        sl = it % 2
        it += 1
        cx.dma(craw[sl][:, 2:2 + NCTX], rawk_d[hd, :, 0:NCTX], reads=[B_rawk], writes=[B_raw[sl]])
        cx.dma(lraw[sl][:, 2:2 + 4096], rawk_d[hd, :, NCTX:NALL], reads=[B_rawk], writes=[B_raw[sl]])
        conv_seg(craw[sl], 0, NCTX, cwk, cbk, hd, sl)
        conv_seg(lraw[sl], NCTX, 4096, cwk, cbk, hd, sl)
        cx.op(cx.act, lambda: nc.scalar.activation(out=sgm[:], in_=acc[:], func=AF.Sigmoid), reads=[B_acc], writes=[B_sgm])
        cx.op(cx.pool, lambda hd=hd: nc.gpsimd.tensor_tensor(out=kTres[:, hd, :], in0=acc[:], in1=sgm[:], op=ALU.mult),
              reads=[B_acc, B_sgm], writes=[B_kT])
    for hd in range(MH):
        sl = it % 2
        it += 1
        cx.dma(lraw[sl][:, 2:2 + NOWN + 2], rawq_d[hd, :, :], reads=[B_rawq], writes=[B_raw[sl]])
        conv_seg(lraw[sl], 0, NOWN, cwq, cbq, hd, sl)
        cx.op(cx.act, lambda: nc.scalar.activation(out=sgm[:, 0:NOWN], in_=acc[:, 0:NOWN], func=AF.Sigmoid),
              reads=[B_acc], writes=[B_sgm])
        cx.op(cx.dve, lambda hd=hd: nc.vector.scalar_tensor_tensor(
            out=qTres[:, hd, :], in0=acc[:, 0:NOWN], scalar=float(MDQK ** -0.5), in1=sgm[:, 0:NOWN],
            op0=ALU.mult, op1=ALU.mult), reads=[B_acc, B_sgm], writes=[B_qT])
    barrier()
    tmp_es.close()
    tmp_es = contextlib.ExitStack()
    g_sb = tt("g_sb", [32, NALL], F32)
    sp_sb = tt("sp_sb", [32, NALL], F32)
    cs_sb = tt("cs_sb", [32, NALL], F32)
    rs_sb = tt("rs_sb", [32, NALL], F32)
    ones32 = tt("ones32", [32, 128], F32)
    gb_sb = tt("gb_sb", [32, 1], F32)
    sel_sb = tt("sel_sb", [32, 6, 32], F32)
    selb_sb = tt("selb_sb", [32, 2], F32)
    B_g, B_sp, B_cs, B_rs, B_gc = Buf(), Buf(), Buf(), Buf(), Buf()
    cx.dma(g_sb[:], graw_d[:, :], reads=[B_graw], writes=[B_g])
    cx.dma(gb_sb[:], gate_b[:, :], writes=[B_gc])
    cx.dma(sel_sb[:], sel[:, :, :], writes=[B_gc])
    cx.dma(selb_sb[:], selb[:, :], writes=[B_gc])
    cx.op(cx.pool, lambda: nc.gpsimd.memset(ones32[:], 1.0), writes=[B_gc])
    cx.op(cx.dve, lambda: nc.vector.tensor_scalar(out=g_sb[:], in0=g_sb[:], scalar1=gb_sb[:, 0:1], scalar2=None,
                                                  op0=ALU.add), reads=[B_g, B_gc], writes=[B_g])
    cx.op(cx.act, lambda: nc.scalar.activation(out=sp_sb[:], in_=g_sb[:], func=AF.Exp, scale=-1.0),
          reads=[B_g], writes=[B_sp])
    cx.op(cx.act, lambda: nc.scalar.activation(out=sp_sb[:], in_=sp_sb[:], func=AF.Ln, scale=1.0, bias=1.0),
          reads=[B_sp], writes=[B_sp])
    for c in range(NCHUNK):
        cx.op(cx.dve, lambda c=c: nc.vector.tensor_tensor_scan(
            out=cs_sb[:, c * CH:(c + 1) * CH], data0=ones32[:], data1=sp_sb[:, c * CH:(c + 1) * CH], initial=0.0,
            op0=ALU.mult, op1=ALU.add), reads=[B_sp, B_gc], writes=[B_cs])
    cx.op(cx.dve, lambda: nc.vector.tensor_tensor(out=rs_sb[:], in0=sp_sb[:], in1=cs_sb[:], op=ALU.subtract),
          reads=[B_sp, B_cs], writes=[B_rs])
    cs3 = cs_sb[:].rearrange("p (c t) -> p c t", t=CH)
    rs3 = rs_sb[:].rearrange("p (c t) -> p c t", t=CH)
    cx.op(cx.dve, lambda: nc.vector.tensor_tensor(out=rs3, in0=rs3, in1=cs3[:, :, CH - 1:CH].broadcast_to([32, NCHUNK, CH]),
                                                  op=ALU.add), reads=[B_rs, B_cs], writes=[B_rs])
    a = 0
    while a < NALL:
        n = min(512, NALL - a)
        bk = (a // 512) % 2
        cx.mm_group(banks[bk][0:32, 0:n], [(sel_sb[:, 0, :], g_sb[:, a:a + n]), (sel_sb[:, 1, :], cs_sb[:, a:a + n]),
                                           (sel_sb[:, 2, :], rs_sb[:, a:a + n])],
                    reads=[B_g, B_cs, B_rs, B_gc], writes=[bbuf[bk]])
        cx.op(cx.act, lambda a=a, n=n, bk=bk: nc.scalar.activation(out=LX[:, a:a + n], in_=banks[bk][0:32, 0:n],
                                                                    func=AF.Identity, bias=selb_sb[:, 0:1], scale=1.0),
              reads=[bbuf[bk], B_gc], writes=[B_LX])
        bk2 = 2 + bk
        cx.mm_group(banks[bk2][0:32, 0:n], [(sel_sb[:, 3, :], cs_sb[:, a:a + n]), (sel_sb[:, 4, :], rs_sb[:, a:a + n])],
                    reads=[B_cs, B_rs, B_gc], writes=[bbuf[bk2]])
        cx.op(cx.act, lambda a=a, n=n, bk2=bk2: nc.scalar.activation(out=YA[:, a:a + n], in_=banks[bk2][0:32, 0:n],
                                                                      func=AF.Identity, bias=selb_sb[:, 1:2], scale=1.0),
              reads=[bbuf[bk2], B_gc], writes=[B_YA])
        a += n
    if "LX_dbg" in dbg:
        lx_o = nc.dram_tensor("LX_dbg", [32, NALL], F32, kind="ExternalOutput").ap()
        ya_o = nc.dram_tensor("YA_dbg", [32, NALL], F32, kind="ExternalOutput").ap()
        cs_o = nc.dram_tensor("CS_dbg", [32, NALL], F32, kind="ExternalOutput").ap()
        rs_o = nc.dram_tensor("RS_dbg", [32, NALL], F32, kind="ExternalOutput").ap()
        sp_o = nc.dram_tensor("SP_dbg", [32, NALL], F32, kind="ExternalOutput").ap()
        kt_o = nc.dram_tensor("kT_dbg", [128, MH, NALL], BF16, kind="ExternalOutput").ap()
        qt_o = nc.dram_tensor("qT_dbg", [128, MH, NOWN], BF16, kind="ExternalOutput").ap()
        B_dd = Buf()
        cx.dma(lx_o[:, :], LX[:], reads=[B_LX], writes=[B_dd])
        cx.dma(ya_o[:, :], YA[:], reads=[B_YA], writes=[B_dd])
        cx.dma(cs_o[:, :], cs_sb[:], reads=[B_cs], writes=[B_dd])
        cx.dma(rs_o[:, :], rs_sb[:], reads=[B_rs], writes=[B_dd])
        cx.dma(sp_o[:, :], sp_sb[:], reads=[B_sp], writes=[B_dd])
        cx.dma(kt_o[:, :, :], kTres[:], reads=[B_kT], writes=[B_dd])
        cx.dma(qt_o[:, :, :], qTres[:], reads=[B_qT], writes=[B_dd])
    barrier()
    tmp_es.close()

    Cf = st("Cf", [128, MH, 257], F32)
    Cb = st("Cb", [128, MH, 257], BF16)
    B_Cf = [Buf() for _ in range(MH)]
    B_Cb = [Buf() for _ in range(MH)]
    vp = [st(f"vp{i}", [128, MH, 257], BF16) for i in range(2)]
    B_vp = [Buf(), Buf()]
    for i in range(2):
        cx.op(cx.pool, lambda i=i: nc.gpsimd.memset(vp[i][:], 1.0), writes=[B_vp[i]])
    Hacc = [st(f"Hacc{i}", [128, MH, 256], F32) for i in range(2)]
    B_H = [Buf(), Buf()]
    dexp = [st(f"dexp{i}", [128, 128], F32) for i in range(2)]
    eexp = [st(f"eexp{i}", [128, 128], F32) for i in range(2)]
    stil = [st(f"stil{i}", [128, 128], BF16) for i in range(2)]
    qtil = [st(f"qtil{i}", [128, 128], BF16) for i in range(2)]
    ktil = [st(f"ktil{i}", [128, 128], BF16) for i in range(2)]
    ryh = [st(f"ryh{i}", [32, 128], F32) for i in range(2)]
    rden = [st(f"rden{i}", [128, 2], F32) for i in range(2)]
    B_dexp, B_eexp, B_stil, B_qtil, B_ktil, B_ryh, B_rden = ([Buf(), Buf()] for _ in range(7))
    hA_t = st("hA_t", [128, MH * MDV], F32)
    hsq = st("hsq", [128, MH * MDV], F32)
    hn_b = st("hn_b", [128, MH * MDV], BF16)
    gm_t = st("gm_t", [128, KC, 128], BF16)
    hmg_t = st("hmg_t", [128, KC, 128], BF16)
    nstat = st("nstat", [128, 3, MH], F32)
    B_hA, B_hsq, B_hn, B_gmt, B_hmgt, B_nst = Buf(), Buf(), Buf(), Buf(), Buf(), Buf()
    B_hAd, B_hmgT = Buf(), Buf()
    bankX, bankY, bankZ = [0, 1], [2, 3], [4, 5]
    bX16 = [banks[b][:].bitcast(BF16) for b in bankX]
    vm_v = vm_d.rearrange("a (h v) -> a h v", v=MDV)
    gmT_v = gmT_d.rearrange("(kc p) t -> p kc t", p=128)
    hmgT_v = hmgT_d.rearrange("(kc p) t -> p kc t", p=128)
    vi = 0
    for scan in range(2):
        for hd in range(MH):
            cx.op(cx.pool, lambda hd=hd: nc.gpsimd.memset(Cf[:, hd, :], 0.0), writes=[B_Cf[hd]])
            cx.op(cx.pool, lambda hd=hd: nc.gpsimd.memset(Cb[:, hd, :], 0.0), writes=[B_Cb[hd]])
        if scan == 0:
            order = [(0, False), (1, False)] + [(c, True) for c in range(2, 18)]
        else:
            order = [(1, False), (0, False)] + [(c, False) for c in range(33, 17, -1)] + [(c, True) for c in range(17, 1, -1)]
        tcol = CH - 1 if scan == 0 else 0
        for (c, outp) in order:
            a0 = c * CH
            t0 = a0 - NCTX
            vs = vi % 2
            vi += 1
            cx.dma(vp[vs][:, :, 0:256], vm_v[a0:a0 + CH, :, :], reads=[B_vm], writes=[B_vp[vs]])
            hs = vi % 2
            for hd in range(MH):
                j = scan * 8 + hd
                s = hd % 2
                bx, by, bz = bankX[s], bankY[s], bankZ[s]
                cx.op(cx.pool, lambda: nc.gpsimd.tensor_scalar(out=ryh[s][:], in0=YA[:, a0:a0 + CH],
                                                               scalar1=hmask_sb[:, j:j + 1], scalar2=None, op0=ALU.mult),
                      reads=[B_YA, B_c5], writes=[B_ryh[s]])
                if outp:
                    cx.mm_group(banks[bx][:, 0:128], [(LX[:, a0:a0 + CH], ryh[s][:]), (identf[:, :], maskf[:, scan, :])],
                                reads=[B_LX, B_ryh[s], B_const, B_c5], writes=[bbuf[bx]])
                    cx.mm_group(banks[bx][:, 128:256], [(kTres[:, hd, a0:a0 + CH], qTres[:, hd, t0:t0 + CH])],
                                reads=[B_kT, B_qT], writes=[bbuf[bx]])
                    cx.mm_group(banks[bx][:, 256:384], [(eones[:, :], ryh[s][:])], reads=[B_c5, B_ryh[s]], writes=[bbuf[bx]])
                    cx.op(cx.act, lambda: nc.scalar.activation(out=dexp[s][:], in_=banks[bx][:, 0:128], func=AF.Exp),
                          reads=[bbuf[bx]], writes=[B_dexp[s]])
                    cx.op(cx.act, lambda: nc.scalar.activation(out=eexp[s][:], in_=banks[bx][:, 256:384], func=AF.Exp),
                          reads=[bbuf[bx]], writes=[B_eexp[s]])
                    cx.op(cx.dve, lambda: nc.vector.tensor_tensor(out=stil[s][:], in0=banks[bx][:, 128:256], in1=dexp[s][:],
                                                                  op=ALU.mult),
                          reads=[bbuf[bx], B_dexp[s]], writes=[B_stil[s]])
                    cx.op(cx.dve, lambda: nc.vector.tensor_tensor(out=qtil[s][:], in0=qTres[:, hd, t0:t0 + CH],
                                                                  in1=eexp[s][:], op=ALU.mult),
                          reads=[B_qT, B_eexp[s]], writes=[B_qtil[s]])
                    cx.mm_group(banks[by][:, 0:257], [(stil[s][:], vp[vs][:, hd, :]), (qtil[s][:], Cb[:, hd, :])],
                                reads=[B_stil[s], B_qtil[s], B_vp[vs], B_Cb[hd]], writes=[bbuf[by]])
                    cx.op(cx.act, lambda: nc.scalar.activation(out=rden[s][:, 0:1], in_=banks[by][:, 256:257], func=AF.Abs),
                          reads=[bbuf[by]], writes=[B_rden[s]])
                    cx.op(cx.dve, lambda: nc.vector.tensor_scalar(out=rden[s][:, 0:1], in0=rden[s][:, 0:1], scalar1=1.0,
                                                                  scalar2=None, op0=ALU.max),
                          reads=[B_rden[s]], writes=[B_rden[s]])
                    cx.op(cx.dve, lambda: nc.vector.reciprocal(out=rden[s][:, 1:2], in_=rden[s][:, 0:1]),
                          reads=[B_rden[s]], writes=[B_rden[s]])
                    cx.op(cx.act, lambda: nc.scalar.activation(out=Hacc[hs][:, hd, :], in_=banks[by][:, 0:256], func=AF.Copy,
                                                               scale=rden[s][:, 1:2]),
                          reads=[bbuf[by], B_rden[s]], writes=[B_H[hs]])
                else:
                    cx.mm_group(banks[bx][:, tcol:tcol + 1], [(LX[:, a0:a0 + CH], ryh[s][:, tcol:tcol + 1])],
                                reads=[B_LX, B_ryh[s]], writes=[bbuf[bx]])
                    cx.mm_group(banks[bx][:, 256 + tcol:257 + tcol], [(eones[:, :], ryh[s][:, tcol:tcol + 1])],
                                reads=[B_c5, B_ryh[s]], writes=[bbuf[bx]])
                    cx.op(cx.act, lambda: nc.scalar.activation(out=dexp[s][:, tcol:tcol + 1], in_=banks[bx][:, tcol:tcol + 1],
                                                               func=AF.Exp), reads=[bbuf[bx]], writes=[B_dexp[s]])
                    cx.op(cx.act, lambda: nc.scalar.activation(out=eexp[s][:, tcol:tcol + 1],
                                                               in_=banks[bx][:, 256 + tcol:257 + tcol], func=AF.Exp),
                          reads=[bbuf[bx]], writes=[B_eexp[s]])
                cx.op(cx.pe, lambda: nc.tensor.transpose(bX16[s][:, 768:896], kTres[:, hd, a0:a0 + CH], identb[:]),
                      reads=[B_kT, B_const], writes=[bbuf[bx]])
                cx.op(cx.act, lambda: nc.scalar.activation(out=ktil[s][:], in_=bX16[s][:, 768:896], func=AF.Copy,
                                                           scale=dexp[s][:, tcol:tcol + 1]),
                      reads=[bbuf[bx], B_dexp[s]], writes=[B_ktil[s]])
                cx.mm_group(banks[bz][:, 0:257], [(ktil[s][:], vp[vs][:, hd, :])], reads=[B_ktil[s], B_vp[vs]],
                            writes=[bbuf[bz]])
                cx.op(cx.dve, lambda: nc.vector.scalar_tensor_tensor(
                    out=Cf[:, hd, :], in0=Cf[:, hd, :], scalar=eexp[s][:, tcol:tcol + 1], in1=banks[bz][:, 0:257],
                    op0=ALU.mult, op1=ALU.add), reads=[B_Cf[hd], B_eexp[s], bbuf[bz]], writes=[B_Cf[hd]])
                cx.op(cx.pool, lambda: nc.gpsimd.tensor_copy(out=Cb[:, hd, :], in_=Cf[:, hd, :]),
                      reads=[B_Cf[hd]], writes=[B_Cb[hd]])
                if "s5dbg" in dbg and scan == 0 and c == 13 and hd == 2:
                    B_dd5 = Buf()
                    for nm, tl, bf, shp, dt_ in (("d_dexp", dexp[s][:], B_dexp[s], [128, 128], F32),
                                                 ("d_eexp", eexp[s][:], B_eexp[s], [128, 128], F32),
                                                 ("d_stil", stil[s][:], B_stil[s], [128, 128], BF16),
                                                 ("d_qtil", qtil[s][:], B_qtil[s], [128, 128], BF16),
                                                 ("d_ktil", ktil[s][:], B_ktil[s], [128, 128], BF16),
                                                 ("d_vp", vp[vs][:, hd, :], B_vp[vs], [128, 257], BF16),
                                                 ("d_ryh", ryh[s][:], B_ryh[s], [32, 128], F32),
                                                 ("d_Cf", Cf[:, hd, :], B_Cf[hd], [128, 257], F32)):
                        o_ = nc.dram_tensor(nm, shp, dt_, kind="ExternalOutput").ap()
                        cx.dma(o_, tl, reads=[bf], writes=[B_dd5])
            if not outp:
                continue
            Hf = Hacc[hs][:].rearrange("p h v -> p (h v)")
            if scan == 0:
                cx.dma(hA_d[t0:t0 + CH, :], Hf, reads=[B_H[hs]], writes=[B_hAd])
                continue
            cx.dma(hA_t[:], hA_d[t0:t0 + CH, :], reads=[B_hAd], writes=[B_hA])
            cx.dma(gm_t[:], gmT_v[:, :, t0:t0 + CH], reads=[B_gmT], writes=[B_gmt])
            cx.op(cx.dve, lambda: nc.vector.tensor_tensor(out=hA_t[:], in0=hA_t[:], in1=Hf, op=ALU.add),
                  reads=[B_hA, B_H[hs]], writes=[B_hA])
            cx.op(cx.pool, lambda: nc.gpsimd.tensor_tensor(out=hsq[:], in0=hA_t[:], in1=hA_t[:], op=ALU.mult),
                  reads=[B_hA], writes=[B_hsq])
            cx.op(cx.dve, lambda: nc.vector.tensor_reduce(out=nstat[:, 0, :], in_=hsq[:].rearrange("p (h v) -> p h v", v=MDV),
                                                          axis=AX.X, op=ALU.add), reads=[B_hsq], writes=[B_nst])
            cx.op(cx.act, lambda: nc.scalar.activation(out=nstat[:, 1, :], in_=nstat[:, 0, :], func=AF.Sqrt,
                                                       scale=1.0 / MDV, bias=EPS), reads=[B_nst], writes=[B_nst])
            cx.op(cx.dve, lambda: nc.vector.reciprocal(out=nstat[:, 2, :], in_=nstat[:, 1, :]), reads=[B_nst], writes=[B_nst])
            cx.op(cx.dve, lambda: nc.vector.tensor_tensor(
                out=hn_b[:].rearrange("p (h v) -> p h v", v=MDV), in0=hA_t[:].rearrange("p (h v) -> p h v", v=MDV),
                in1=nstat[:, 2, :].unsqueeze(2).broadcast_to([128, MH, MDV]), op=ALU.mult),
                reads=[B_hA, B_nst], writes=[B_hn])
            for half8 in range(2):
                bt = 6 + half8
                b16 = banks[bt][:].bitcast(BF16)
                for q in range(8):
                    kc = half8 * 8 + q
                    cx.op(cx.pe, lambda kc=kc, q=q, b16=b16: nc.tensor.transpose(
                        b16[:, q * 128:(q + 1) * 128], hn_b[:, kc * 128:(kc + 1) * 128], identb[:]),
                        reads=[B_hn, B_const], writes=[bbuf[bt]])
                for q in range(8):
                    kc = half8 * 8 + q
                    cx.op(cx.dve, lambda kc=kc, q=q, b16=b16: nc.vector.scalar_tensor_tensor(
                        out=hmg_t[:, kc, :], in0=b16[:, q * 128:(q + 1) * 128], scalar=mhg_sb[:, kc:kc + 1],
                        in1=gm_t[:, kc, :], op0=ALU.mult, op1=ALU.mult),
                        reads=[bbuf[bt], B_c5, B_gmt], writes=[B_hmgt])
            cx.dma(hmgT_v[:, :, t0:t0 + CH], hmg_t[:], reads=[B_hmgt], writes=[B_hmgT])
    stage_end()


    stage_begin()
    cnT = st("cnT", [128, 4, NALL], BF16)
    KRg = st("KRg", [64, NALL], F32)
    sqkr = st("sqkr", [64, NALL], BF16)
    cosk = st("cosk", [64, NALL], F32)
    sink = st("sink", [64, NALL], F32)
    kg_sb = st("kg_sb", [128, 4], F32)
    kvg_sb = st("kvg_sb", [128, 4], F32)
    B_cn, B_KR, B_sqkr, B_tk = Buf(), Buf(), Buf(), Buf()
    cx.dma(cosk[:], cosT[:, :], writes=[B_tk])
    cx.dma(sink[:], sinT[:, :], writes=[B_tk])
    cx.dma(kg_sb[:], kg[:, :], writes=[B_tk])
    cx.dma(kvg_sb[:], kvgT[:, :], writes=[B_tk])
    cx.op(cx.dve, lambda: nc.vector.tensor_scalar(out=cosk[:], in0=cosk[:], scalar1=kg_sb[0:64, 1:2], scalar2=None,
                                                  op0=ALU.mult), reads=[B_tk], writes=[B_tk])
    cx.op(cx.dve, lambda: nc.vector.tensor_scalar(out=sink[:], in0=sink[:], scalar1=kg_sb[0:64, 2:3], scalar2=None,
                                                  op0=ALU.mult), reads=[B_tk], writes=[B_tk])
    ck = [st(f"ck{i}", [128, 4, 512], F32) for i in range(2)]
    krt = [st(f"krt{i}", [64, 2, 512], F32) for i in range(2)]
    B_ck = [Buf(), Buf()]
    tf6 = [st(f"tf6_{i}", [128, 512], F32) for i in range(4)]
    tb6 = [st(f"tb6_{i}", [128, 4, 512], BF16) for i in range(2)]
    ob6 = [st(f"ob6_{i}", [128, 512], BF16) for i in range(4)]
    B_tf6 = [Buf() for _ in range(4)]
    B_tb6 = [Buf() for _ in range(2)]
    B_ob6 = [Buf() for _ in range(4)]
    rr6 = {"tf": 0, "ob": 0, "bk": 0, "w": 0}

    def n6(kind, n):
        i = rr6[kind] % n
        rr6[kind] += 1
        return i

    ckv_v = ckv_d.rearrange("(r p) a -> p r a", p=128)
    blks = [(a, min(512, NALL - a)) for a in range(0, NALL, 512)]
    for bi, (a0, n) in enumerate(blks):
        sl = bi % 2
        cx.dma(ck[sl][:, :, 0:n], ckv_v[:, :, a0:a0 + n], reads=[B_ckv], writes=[B_ck[sl]])
        cx.dma(krt[sl][:, 0, 0:n], kr_d[0:64, a0:a0 + n], reads=[B_kr], writes=[B_ck[sl]])
        cx.dma(krt[sl][:, 1, 0:n], kr_d[64:128, a0:a0 + n], reads=[B_kr], writes=[B_ck[sl]])
        cx.op(cx.act, lambda: nc.scalar.activation(out=tb6[sl][:, :, 0:n], in_=ck[sl][:, :, 0:n], func=AF.Square),
              reads=[B_ck[sl]], writes=[B_tb6[sl]])
        bk = n6("bk", 8)
        cx.mm_group(banks[bk][:, 0:n], [(onesb[:, :], tb6[sl][:, r, 0:n]) for r in range(4)],
                    reads=[B_tb6[sl], B_const], writes=[bbuf[bk]])
        ir = n6("tf", 4)
        cx.op(cx.act, lambda: nc.scalar.activation(out=tf6[ir][:, 0:n], in_=banks[bk][:, 0:n], func=AF.Ln,
                                                   scale=1.0 / KVR, bias=EPS), reads=[bbuf[bk]], writes=[B_tf6[ir]])
        cx.op(cx.act, lambda: nc.scalar.activation(out=tf6[ir][:, 0:n], in_=tf6[ir][:, 0:n], func=AF.Exp, scale=-0.5),
              reads=[B_tf6[ir]], writes=[B_tf6[ir]])
        for r in range(4):
            cx.op(cx.dve, lambda r=r: nc.vector.scalar_tensor_tensor(
                out=cnT[:, r, a0:a0 + n], in0=ck[sl][:, r, 0:n], scalar=kvg_sb[:, r:r + 1], in1=tf6[ir][:, 0:n],
                op0=ALU.mult, op1=ALU.mult), reads=[B_ck[sl], B_tf6[ir], B_tk], writes=[B_cn])
        cx.op(cx.act, lambda: nc.scalar.activation(out=sqkr[:, a0:a0 + n], in_=krt[sl][:, 0, 0:n], func=AF.Square),
              reads=[B_ck[sl]], writes=[B_sqkr])
        cx.op(cx.pool, lambda: nc.gpsimd.tensor_tensor(out=KRg[:, a0:a0 + n], in0=krt[sl][:, 0, 0:n], in1=cosk[:, a0:a0 + n],
                                                       op=ALU.mult), reads=[B_ck[sl], B_tk], writes=[B_KR])
        cx.op(cx.pool, lambda: nc.gpsimd.tensor_tensor(out=krt[sl][:, 1, 0:n], in0=krt[sl][:, 1, 0:n], in1=sink[:, a0:a0 + n],
                                                       op=ALU.mult), reads=[B_ck[sl], B_tk], writes=[B_ck[sl]])
        cx.op(cx.pool, lambda: nc.gpsimd.tensor_tensor(out=KRg[:, a0:a0 + n], in0=KRg[:, a0:a0 + n], in1=krt[sl][:, 1, 0:n],
                                                       op=ALU.add), reads=[B_ck[sl], B_KR], writes=[B_KR])
    wu_f = [st(f"wu_f{i}", [128, 4, 512], F32) for i in range(2)]
    wu_b = [st(f"wu_b{i}", [128, 4, 512], BF16) for i in range(2)]
    B_wuf = [Buf(), Buf()]
    B_wub = [Buf(), Buf()]
    B_kaT, B_va = Buf(), Buf()
    wuk_v = w_uk.rearrange("(r p) n -> p r n", p=128)
    wuv_v = w_uv.rearrange("(r p) n -> p r n", p=128)

    def load_wu(view, c0, width):
        i = n6("w", 2)
        cx.dma(wu_f[i][:, :, 0:width], view[:, :, c0:c0 + width], writes=[B_wuf[i]])
        cx.op(cx.pool, lambda: nc.gpsimd.tensor_copy(out=wu_b[i][:, :, 0:width], in_=wu_f[i][:, :, 0:width]),
              reads=[B_wuf[i]], writes=[B_wub[i]])
        return i

    for hd in range(AH):
        wi = load_wu(wuk_v, hd * 128, 128)
        for (a0, n) in blks:
            bk = n6("bk", 8)
            cx.mm_group(banks[bk][:, 0:n], [(wu_b[wi][:, r, 0:128], cnT[:, r, a0:a0 + n]) for r in range(4)],
                        reads=[B_wub[wi], B_cn], writes=[bbuf[bk]])
            s1 = n6("ob", 4)
            cx.op(cx.act, lambda: nc.scalar.activation(out=ob6[s1][:, 0:n], in_=banks[bk][:, 0:n], func=AF.Square),
                  reads=[bbuf[bk]], writes=[B_ob6[s1]])
            bq = n6("bk", 8)
            cx.mm_group(banks[bq][:, 0:n], [(onesb[:, :], ob6[s1][:, 0:n]), (onesb[0:64, :], sqkr[:, a0:a0 + n])],
                        reads=[B_ob6[s1], B_sqkr, B_const], writes=[bbuf[bq]])
            ir = n6("tf", 4)
            cx.op(cx.act, lambda: nc.scalar.activation(out=tf6[ir][:, 0:n], in_=banks[bq][:, 0:n], func=AF.Ln,
                                                       scale=1.0 / 192.0, bias=EPS), reads=[bbuf[bq]], writes=[B_tf6[ir]])
            cx.op(cx.act, lambda: nc.scalar.activation(out=tf6[ir][:, 0:n], in_=tf6[ir][:, 0:n], func=AF.Exp, scale=-0.5),
                  reads=[B_tf6[ir]], writes=[B_tf6[ir]])
            o1 = n6("ob", 4)
            cx.op(cx.dve, lambda: nc.vector.scalar_tensor_tensor(
                out=ob6[o1][:, 0:n], in0=banks[bk][:, 0:n], scalar=kg_sb[:, 0:1], in1=tf6[ir][:, 0:n],
                op0=ALU.mult, op1=ALU.mult), reads=[bbuf[bk], B_tf6[ir], B_tk], writes=[B_ob6[o1]])
            cx.dma(kaT_d[hd, 0:128, a0:a0 + n], ob6[o1][:, 0:n], reads=[B_ob6[o1]], writes=[B_kaT])
            o2 = n6("ob", 4)
            cx.op(cx.pool, lambda: nc.gpsimd.tensor_tensor(out=ob6[o2][0:64, 0:n], in0=KRg[:, a0:a0 + n],
                                                           in1=tf6[ir][0:64, 0:n], op=ALU.mult),
                  reads=[B_KR, B_tf6[ir]], writes=[B_ob6[o2]])
            cx.dma(kaT_d[hd, 128:192, a0:a0 + n], ob6[o2][0:64, 0:n], reads=[B_ob6[o2]], writes=[B_kaT])
    for g4 in range(4):
        wi = load_wu(wuv_v, g4 * 512, 512)
        for a0 in range(0, NALL, 128):
            bk = n6("bk", 8)
            cx.mm_group(banks[bk][:, 0:512], [(cnT[:, r, a0:a0 + 128], wu_b[wi][:, r, :]) for r in range(4)],
                        reads=[B_wub[wi], B_cn], writes=[bbuf[bk]])
            o1 = n6("ob", 4)
            if (a0 // 128) % 2 == 0:
                cx.op(cx.act, lambda: nc.scalar.copy(out=ob6[o1][:, :], in_=banks[bk][:, 0:512]), reads=[bbuf[bk]],
                      writes=[B_ob6[o1]])
            else:
                cx.op(cx.dve, lambda: nc.vector.tensor_copy(out=ob6[o1][:, :], in_=banks[bk][:, 0:512]), reads=[bbuf[bk]],
                      writes=[B_ob6[o1]])
            cx.dma(va_d[a0:a0 + 128, g4 * 512:(g4 + 1) * 512], ob6[o1][:, :], reads=[B_ob6[o1]], writes=[B_va])
    stage_end()

    stage_begin()
    B_ozT = Buf()
    kn_s = [st(f"kn_s{i}", [128, NALL], BF16) for i in range(2)]
    kr_s = [st(f"kr_s{i}", [64, NALL], BF16) for i in range(2)]
    v_s = [st(f"v_s{i}", [128, NCHUNK, 128], BF16) for i in range(2)]
    qn_s = [st(f"qn_s{i}", [128, NOWN], BF16) for i in range(2)]
    qr_s = [st(f"qr_s{i}", [64, NOWN], BF16) for i in range(2)]
    sz_s = [st(f"sz_s{i}", [128, NOWN], BF16) for i in range(2)]
    B_hd = [Buf(), Buf()]
    pt = [st(f"pt{i}", [128, 512], BF16) for i in range(3)]
    B_pt = [Buf() for _ in range(3)]
    rd = st("rd", [128, 512], F32)
    ot = st("ot", [128, 512], F32)
    ozb = [st(f"ozb{i}", [128, 512], BF16) for i in range(2)]
    B_rd, B_ot = Buf(), Buf()
    B_ozb = [Buf(), Buf()]
    va_v = va_d.rearrange("(c p) n -> p c n", p=128)
    sc_scale = float(192.0 ** -0.5)
    pi = 0
    sbk = 0
    for hd in range(AH):
        sl = hd % 2
        cx.dma(kn_s[sl][:], kaT_d[hd, 0:128, :], reads=[B_kaT], writes=[B_hd[sl]])
        cx.dma(kr_s[sl][:], kaT_d[hd, 128:192, :], reads=[B_kaT], writes=[B_hd[sl]])
        cx.dma(v_s[sl][:], va_v[:, :, hd * 128:(hd + 1) * 128], reads=[B_va], writes=[B_hd[sl]])
        cx.dma(qn_s[sl][:], qaT_d[hd, 0:128, :], reads=[B_qaT], writes=[B_hd[sl]])
        cx.dma(qr_s[sl][:], qaT_d[hd, 128:192, :], reads=[B_qaT], writes=[B_hd[sl]])
        cx.dma(sz_s[sl][:], szaT_d[hd * 128:(hd + 1) * 128, :], reads=[B_szaT], writes=[B_hd[sl]])
        for tbk in range(4):
            t0 = tbk * 512
            bo, bd = 6, 7
            for sc in range(NCHUNK):
                bs = sbk % 4
                sbk += 1
                cx.mm_group(banks[bs][:, 0:512], [(kn_s[sl][:, sc * 128:(sc + 1) * 128], qn_s[sl][:, t0:t0 + 512]),
                                                  (kr_s[sl][:, sc * 128:(sc + 1) * 128], qr_s[sl][:, t0:t0 + 512])],
                            reads=[B_hd[sl]], writes=[bbuf[bs]])
                p = pi % 3
                pi += 1
                cx.op(cx.act, lambda: nc.scalar.activation(out=pt[p][:], in_=banks[bs][:, 0:512], func=AF.Exp,
                                                           scale=sc_scale, bias=-ATT_SHIFT),
                      reads=[bbuf[bs]], writes=[B_pt[p]])
                last = sc == NCHUNK - 1
                cx.mm1(banks[bo][:, 0:512], v_s[sl][:, sc, :], pt[p][:], start=(sc == 0), stop=last,
                       reads=[B_pt[p], B_hd[sl]], writes=[bbuf[bo]], signal=last)
                cx.mm1(banks[bd][:, 0:512], onesb[:, :], pt[p][:], start=(sc == 0), stop=last,
                       reads=[B_pt[p], B_const], writes=[bbuf[bd]], signal=True)
            cx.op(cx.dve, lambda: nc.vector.reciprocal(out=rd[:], in_=banks[bd][:, 0:512]), reads=[bbuf[bd]], writes=[B_rd])
            cx.op(cx.dve, lambda: nc.vector.tensor_tensor(out=ot[:], in0=banks[bo][:, 0:512], in1=rd[:], op=ALU.mult),
                  reads=[bbuf[bo], B_rd], writes=[B_ot])
            ob = (hd * 4 + tbk) % 2
            cx.op(cx.pool, lambda: nc.gpsimd.tensor_tensor(out=ozb[ob][:], in0=ot[:], in1=sz_s[sl][:, t0:t0 + 512], op=ALU.mult),
                  reads=[B_ot, B_hd[sl]], writes=[B_ozb[ob]])
            cx.dma(ozT_d[hd * 128:(hd + 1) * 128, t0:t0 + 512], ozb[ob][:], reads=[B_ozb[ob]], writes=[B_ozT])
    stage_end()

    B_mixT = Buf()
    hmg_v = hmgT_d.rearrange("(kc p) t -> p kc t", p=128)
    oz_v = ozT_d.rearrange("(kc p) t -> p kc t", p=128)
    wpm_v = w_pm.rearrange("(kc p) n -> p kc n", p=128)
    wpa_v = w_pa.rearrange("(kc p) n -> p kc n", p=128)
    for th in range(2):
        stage_begin()
        T0 = th * 1024
        hmg_r = st("hmg_r", [128, KC, 1024], BF16)
        oz_r = st("oz_r", [128, KC, 1024], BF16)
        B_res = Buf()
        for kc in range(KC):
            cx.dma(hmg_r[:, kc, :], hmg_v[:, kc, T0:T0 + 1024], reads=[B_hmgT], writes=[B_res])
            cx.dma(oz_r[:, kc, :], oz_v[:, kc, T0:T0 + 1024], reads=[B_ozT], writes=[B_res])
        wf8 = [st(f"wf8_{i}", [128, KC, 128], F32) for i in range(4)]
        wb8 = [st(f"wb8_{i}", [128, KC, 128], BF16) for i in range(4)]
        B_wf8 = [Buf() for _ in range(4)]
        B_wb8 = [Buf() for _ in range(4)]
        gg8 = [st(f"gg8_{i}", [128, 2, 512], BF16) for i in range(2)]
        B_gg8 = [Buf(), Buf()]
        ta8 = [st(f"ta8_{i}", [128, 512], F32) for i in range(2)]
        tb8 = [st(f"tb8_{i}", [128, 512], F32) for i in range(2)]
        mo8 = [st(f"mo8_{i}", [128, 512], BF16) for i in range(2)]
        B_ta8, B_tb8, B_mo8 = [Buf(), Buf()], [Buf(), Buf()], [Buf(), Buf()]
        it8 = 0
        for j in range(KC):
            ws = []
            for wsel, view in ((0, wpm_v), (1, wpa_v)):
                i = (j * 2 + wsel) % 4
                cx.dma(wf8[i][:], view[:, :, j * 128:(j + 1) * 128], writes=[B_wf8[i]])
                cx.op(cx.pool, lambda i=i: nc.gpsimd.tensor_copy(out=wb8[i][:], in_=wf8[i][:]), reads=[B_wf8[i]],
                      writes=[B_wb8[i]])
                ws.append(i)
            for blk in range(2):
                s8 = it8 % 2
                it8 += 1
                t0 = T0 + blk * 512
                cx.dma(gg8[s8][:, 0, :], ggT_d[j * 128:(j + 1) * 128, t0:t0 + 512], reads=[B_ggT], writes=[B_gg8[s8]])
                cx.dma(gg8[s8][:, 1, :], ggT_d[D + j * 128:D + (j + 1) * 128, t0:t0 + 512], reads=[B_ggT], writes=[B_gg8[s8]])
                bm, ba = (it8 % 2) * 2, (it8 % 2) * 2 + 1
                cx.mm_group(banks[bm][:, 0:512], [(wb8[ws[0]][:, kc, :], hmg_r[:, kc, blk * 512:(blk + 1) * 512]) for kc in range(KC)],
                            reads=[B_wb8[ws[0]], B_res], writes=[bbuf[bm]])
                cx.mm_group(banks[ba][:, 0:512], [(wb8[ws[1]][:, kc, :], oz_r[:, kc, blk * 512:(blk + 1) * 512]) for kc in range(KC)],
                            reads=[B_wb8[ws[1]], B_res], writes=[bbuf[ba]])
                cx.op(cx.dve, lambda: nc.vector.tensor_tensor(out=ta8[s8][:], in0=banks[bm][:, 0:512], in1=gg8[s8][:, 0, :],
                                                              op=ALU.mult), reads=[bbuf[bm], B_gg8[s8]], writes=[B_ta8[s8]])
                cx.op(cx.dve, lambda: nc.vector.tensor_tensor(out=tb8[s8][:], in0=banks[ba][:, 0:512], in1=gg8[s8][:, 1, :],
                                                              op=ALU.mult), reads=[bbuf[ba], B_gg8[s8]], writes=[B_tb8[s8]])
                cx.op(cx.pool, lambda: nc.gpsimd.tensor_tensor(out=mo8[s8][:], in0=ta8[s8][:], in1=tb8[s8][:], op=ALU.add),
                      reads=[B_ta8[s8], B_tb8[s8]], writes=[B_mo8[s8]])
                cx.dma(mixT_d[j * 128:(j + 1) * 128, t0:t0 + 512], mo8[s8][:], reads=[B_mo8[s8]], writes=[B_mixT])
        stage_end()

    stage_begin()
    wo_b = st("wo_b", [128, KC, D], BF16)
    gate_bc = st("gate_bc", [128, D], F32)
    B_wo, B_gbc = Buf(), Buf()
    cx.dma(gate_bc[:], gateb_d[0:1, :].broadcast_to([128, D]), reads=[B_gateb], writes=[B_gbc])
    wof = [st(f"wof{i}", [128, KC, 256], F32) for i in range(2)]
    B_wof = [Buf(), Buf()]
    wout_v = w_out.rearrange("(kc p) n -> p kc n", p=128)
    for g8 in range(8):
        i = g8 % 2
        cx.dma(wof[i][:], wout_v[:, :, g8 * 256:(g8 + 1) * 256], writes=[B_wof[i]])
        cx.op(cx.pool, lambda i=i, g8=g8: nc.gpsimd.tensor_copy(out=wo_b[:, :, g8 * 256:(g8 + 1) * 256], in_=wof[i][:]),
              reads=[B_wof[i]], writes=[B_wo])
    mx = [st(f"mx{i}", [128, KC, 128], BF16) for i in range(2)]
    xo = [st(f"xo{i}", [128, D], F32) for i in range(2)]
    yo = [st(f"yo{i}", [128, D], F32) for i in range(2)]
    B_mx, B_xo, B_yo = [Buf(), Buf()], [Buf(), Buf()], [Buf(), Buf()]
    mix_v = mixT_d.rearrange("(kc p) t -> p kc t", p=128)
    B_out = Buf()
    for ti in range(NOWN // 128):
        s9 = ti % 2
        cx.dma(mx[s9][:], mix_v[:, :, ti * 128:(ti + 1) * 128], reads=[B_mixT], writes=[B_mx[s9]])
        cx.dma(xo[s9][:], xa[NCTX + ti * 128:NCTX + (ti + 1) * 128, :], writes=[B_xo[s9]])
        for fb in range(4):
            bk = (ti * 4 + fb) % 8
            cx.mm_group(banks[bk][:, 0:512], [(mx[s9][:, kc, :], wo_b[:, kc, fb * 512:(fb + 1) * 512]) for kc in range(KC)],
                        reads=[B_mx[s9], B_wo], writes=[bbuf[bk]])
            cx.op(cx.dve, lambda fb=fb, bk=bk: nc.vector.tensor_tensor(
                out=yo[s9][:, fb * 512:(fb + 1) * 512], in0=banks[bk][:, 0:512], in1=gate_bc[:, fb * 512:(fb + 1) * 512],
                op=ALU.mult), reads=[bbuf[bk], B_gbc], writes=[B_yo[s9]])
        cx.op(cx.pool, lambda: nc.gpsimd.tensor_tensor(out=yo[s9][:], in0=yo[s9][:], in1=xo[s9][:], op=ALU.add),
              reads=[B_yo[s9], B_xo[s9]], writes=[B_yo[s9]])
        cx.dma(out_d[ti * 128:(ti + 1) * 128, :], yo[s9][:], reads=[B_yo[s9]], writes=[B_out])
    stage_end()

    for t in cx.all_dma_toks[-40:]:
        cx.sp.wait(t)
    cx.es.close()
    outer.close()
    return nc


def _col_perm(half):
    M_QK_W, M_V_W, M_GATE_W = 1024, 2048, 32
    A_Q_W, A_V_W = 3072, 2048
    o_km = 0
    o_vm = o_km + M_QK_W
    o_gt = o_vm + M_V_W
    o_ckv = o_gt + M_GATE_W
    o_kr = o_ckv + KVR
    kvc = o_kr + AROPE
    o_qm = kvc
    o_om = o_qm + M_QK_W
    o_zm = o_om + M_V_W
    o_qa = o_zm + M_V_W
    o_za = o_qa + A_Q_W
    o_gm = o_za + A_V_W
    sw = np.array([ax * 32 + (1 - hf) * 16 + f for ax in range(2) for hf in range(2) for f in range(16)])
    perm = []
    perm += list(range(o_km, o_km + 1024))
    g = np.arange(32).reshape(4, 8)
    if half == 1:
        g = g[[2, 3, 0, 1]]
    perm += list(o_gt + g.reshape(-1))
    perm += list(range(o_ckv, o_ckv + KVR))
    perm += list(range(o_kr, o_kr + 64))
    perm += list(o_kr + sw)
    perm += list(range(o_vm, o_vm + 2048))
    perm += list(range(o_qm, o_qm + 1024))
    for j in range(16):
        perm += list(range(o_om + j * 128, o_om + (j + 1) * 128))
        perm += list(range(o_zm + j * 128, o_zm + (j + 1) * 128))
    for h in range(16):
        b0 = o_qa + h * 192
        perm += list(range(b0, b0 + 128))
        perm += list(range(b0 + 128, b0 + 192))
        perm += list(b0 + 128 + sw)
    perm += list(range(o_za, o_za + 2048))
    perm += list(range(o_gm, o_gm + 4096))
    perm = np.array(perm)
    assert perm.shape[0] == NCOLS
    return perm, sw


def _colT(v, n):
    return np.ascontiguousarray(np.asarray(v, np.float32).reshape(n, 128).T)


def _consts():
    ident = np.eye(128, dtype=np.float32)
    s = np.arange(128)[:, None]
    t = np.arange(128)[None, :]
    masks = np.zeros((128, 2, 128), np.float32)
    masks[:, 0, :] = np.where(s <= t, 0.0, -30000.0)
    masks[:, 1, :] = np.where(s >= t, 0.0, -30000.0)
    sel = np.zeros((32, 6, 32), np.float32)
    for h in range(8):
        sel[h, 0, h] = 1.0
        sel[16 + h, 0, 8 + h] = 1.0
        sel[8 + h, 1, h] = 1.0
        sel[24 + h, 2, 8 + h] = 1.0
        sel[8 + h, 3, 16 + h] = -1.0
        sel[24 + h, 4, 24 + h] = -1.0
    for r in range(16, 32):
        sel[r, 5, :] = 1.0
    selb = np.zeros((32, 2), np.float32)
    selb[16:, 0] = 1.0
    selb[:16, 1] = 1.0
    hmask = np.zeros((32, 16), np.float32)
    for j in range(16):
        hmask[j, j] = 1.0
        hmask[16 + j, j] = 1.0
    return ident, masks, sel, selb, hmask


def _rope_tables(pos):
    row = (pos // 64).astype(np.float32)
    col = (pos % 64).astype(np.float32)
    freqs = (np.float32(10000.0) ** (-np.arange(16, dtype=np.float32) / np.float32(16))).astype(np.float32)
    ang = np.stack([row[:, None] * freqs, col[:, None] * freqs], axis=1)
    cos = np.cos(ang).astype(np.float32)
    sin = np.sin(ang).astype(np.float32)
    cosT = np.ones((64, NALL), np.float32)
    sinT = np.zeros((64, NALL), np.float32)
    for ax in range(2):
        for hf in range(2):
            r0 = ax * 32 + hf * 16
            cosT[r0:r0 + 16, NCTX:] = cos[:, ax, :].T
            sgn = -1.0 if hf == 0 else 1.0
            sinT[r0:r0 + 16, NCTX:] = sgn * sin[:, ax, :].T
    return cosT, sinT


def make_in_maps(inp):
    x = np.asarray(inp["x"], np.float32)
    c = np.asarray(inp["c"], np.float32)
    ctx = np.asarray(inp["ctx"], np.float32)
    c_ctx = np.asarray(inp["c_ctx"], np.float32)
    ident, masks, sel, selb, hmask = _consts()
    w_in0 = np.asarray(inp["w_in"], np.float32)[0]
    conv_w = np.asarray(inp["conv_w"], np.float32)[0]
    conv_b = np.asarray(inp["conv_b"], np.float32)[0]
    gate_b = np.asarray(inp["gate_b"], np.float32)[0]
    qn = np.asarray(inp["q_norm_g"], np.float32)[0]
    kn = np.asarray(inp["k_norm_g"], np.float32)[0]
    shared = {}
    per_half = {}
    for half in range(2):
        perm, sw = _col_perm(half)
        d = {}
        d["w_in"] = np.ascontiguousarray(w_in0[:, perm])
        cw = conv_w if half == 0 else conv_w[::-1]
        d["conv_wq"] = np.ascontiguousarray(cw[:, :1024].reshape(5, MH, 128).transpose(2, 1, 0))
        d["conv_wk"] = np.ascontiguousarray(cw[:, 1024:].reshape(5, MH, 128).transpose(2, 1, 0))
        gb = gate_b.reshape(4, 8)
        if half == 1:
            gb = gb[[2, 3, 0, 1]]
        d["gate_b"] = np.ascontiguousarray(gb.reshape(32, 1))
        pos = np.arange(4096)
        if half == 1:
            pos = pos[::-1]
        d["cosT"], d["sinT"] = _rope_tables(pos)
        per_half[half] = d
    shared["ada_w"] = np.ascontiguousarray(np.asarray(inp["ada_w"], np.float32)[0])
    shared["ada_bT"] = _colT(np.asarray(inp["ada_b"])[0], 48)
    shared["norm_gT"] = _colT(np.asarray(inp["norm_g"])[0], KC)
    shared["conv_bq"] = np.ascontiguousarray(conv_b[:1024].reshape(MH, 128).T)
    shared["conv_bk"] = np.ascontiguousarray(conv_b[1024:].reshape(MH, 128).T)
    shared["mhgT"] = _colT(np.asarray(inp["mh_norm_g"])[0], KC)
    sw = _col_perm(0)[1]
    qg = np.zeros((128, 4), np.float32)
    qg[:, 0] = qn[:128]
    qg[:64, 1] = qn[128:]
    qg[:64, 2] = qn[128:][sw]
    kgm = np.zeros((128, 4), np.float32)
    kgm[:, 0] = kn[:128]
    kgm[:64, 1] = kn[128:]
    kgm[:64, 2] = kn[128:][sw]
    shared["qg"] = qg
    shared["kg"] = kgm
    shared["kvgT"] = _colT(np.asarray(inp["kv_norm_g"])[0], 4)
    shared["w_uk"] = np.ascontiguousarray(np.asarray(inp["w_uk"], np.float32)[0])
    shared["w_uv"] = np.ascontiguousarray(np.asarray(inp["w_uv"], np.float32)[0])
    shared["w_pm"] = np.ascontiguousarray(np.asarray(inp["w_proj_m"], np.float32)[0])
    shared["w_pa"] = np.ascontiguousarray(np.asarray(inp["w_proj_a"], np.float32)[0])
    shared["w_out"] = np.ascontiguousarray(np.asarray(inp["w_out"], np.float32)[0])
    shared["ident_f"] = ident
    shared["masks"] = masks
    shared["sel"] = sel
    shared["selb"] = selb
    shared["hmask"] = hmask
    eo = np.zeros((32, 128), np.float32)
    eo[16:] = 1.0
    shared["eones"] = eo
    maps = []
    for core in range(8):
        b, half = core // 2, core % 2
        m = dict(shared)
        m.update(per_half[half])
        if half == 0:
            xl = x[b]
            cl = ctx[b]
        else:
            xl = x[b][::-1]
            cl = ctx[b][::-1]
        m["xa"] = np.ascontiguousarray(np.concatenate([cl, xl], axis=0))
        cc = np.stack([c[b], c_ctx], axis=1)
        m["cT"] = np.ascontiguousarray(cc.reshape(KC, 128, 2).transpose(1, 0, 2))
        maps.append(m)
    return maps


_NC_CACHE = {}


def kernel(**inputs):
    maps = make_in_maps(inputs)
    if "nc" not in _NC_CACHE:
        _NC_CACHE["nc"] = build()
    nc = _NC_CACHE["nc"]
    res = run_bass_kernel_spmd(nc, maps, core_ids=list(range(8)))
    out = np.zeros((4, 4096, D), np.float32)
    for core in range(8):
        b, half = core // 2, core % 2
        o = np.asarray(res.results[core]["out"], np.float32)
        if half == 0:
            out[b, :2048] = o
        else:
            out[b, 2048:] = o[::-1]
    return out
```

```python
import contextlib
import numpy as np
import concourse.bass as bass
import concourse.mybir as mybir
from concourse.bass_utils import run_bass_kernel_spmd

F32 = mybir.dt.float32
BF16 = mybir.dt.bfloat16
AF = mybir.ActivationFunctionType
ALU = mybir.AluOpType
AX = mybir.AxisListType

D = 2048
KC = 16
NCTX = 256
NOWN = 2048
NALL = 4352
EPS = 1e-6
MH, MDQK, MDV = 8, 128, 256
AH, ANOPE, AROPE, ADV = 16, 128, 64, 128
KVR = 512
CH = 128
NCHUNK = NALL // CH

C_KM = 0
C_GT = 1024
C_CKV = 1056
C_KR = 1568
C_KRS = 1632
C_VM = 1696
C_QM = 3744
C_OZ = 4768
C_QA = 8864
C_ZA = 12960
C_GM = 15008
NCOLS = 19104

ATT_SHIFT = 8.0


class Tok:
    __slots__ = ("sem", "val", "eng")

    def __init__(self, sem, val, eng):
        self.sem, self.val, self.eng = sem, val, eng


class Buf:
    __slots__ = ("name", "w", "r", "wd")

    def __init__(self, name=""):
        self.name = name
        self.w = None
        self.wd = {}
        self.r = {}


class Eng:
    def __init__(self, ctx, name, eng):
        self.ctx, self.name, self.eng = ctx, name, eng
        self.sem = None
        self.n = 0
        self.waited = {}
        self.new_sem()

    def new_sem(self):
        self.sem = self.ctx.es.enter_context(self.ctx.nc.semaphore(f"s_{self.name}_{self.ctx.uid()}"))
        self.n = 0

    def wait(self, tok):
        if tok is None:
            return
        k = id(tok.sem)
        if self.waited.get(k, 0) >= tok.val:
            return
        self.eng.wait_ge(tok.sem, tok.val)
        self.waited[k] = tok.val


class Ctx:
    def __init__(self, nc):
        self.nc = nc
        self.es = contextlib.ExitStack()
        self._uid = 0
        self.pe = Eng(self, "pe", nc.tensor)
        self.act = Eng(self, "act", nc.scalar)
        self.dve = Eng(self, "dve", nc.vector)
        self.pool = Eng(self, "pool", nc.gpsimd)
        self.sp = Eng(self, "sp", nc.sync)
        self.dma_sems = [self.es.enter_context(nc.semaphore(f"dq{i}")) for i in range(32)]
        self.dma_cnt = [0] * 32
        self.dma_last = [None] * 32
        self.dma_i = 0
        self.all_dma_toks = []
        self.fence_scr = None

    def uid(self):
        self._uid += 1
        return self._uid

    def fresh_sems(self):
        for e in (self.pe, self.act, self.dve, self.pool):
            e.new_sem()

    def _deps(self, eng, reads, writes, is_dma=False):
        deps = []
        for b in reads:
            if b.w is not None:
                deps.append(b.w)
            deps.extend(b.wd.values())
        for b in writes:
            if b.w is not None and b.w.eng is not eng:
                deps.append(b.w)
            if not is_dma:
                deps.extend(b.wd.values())
            for t in b.r.values():
                if is_dma or t.eng is not eng:
                    deps.append(t)
        return deps

    def _wait_all(self, eng, deps, pe_skip_self=False):
        for d in deps:
            if pe_skip_self and d.eng is eng:
                continue
            eng.wait(d)

    def _record(self, tok, reads, writes, is_dma=False):
        for b in reads:
            k = id(tok.sem)
            b.r[k] = tok
        for b in writes:
            if is_dma:
                b.wd[id(tok.sem)] = tok
                b.w = None
            else:
                b.w = tok
                b.wd = {}
            b.r = {}

    def op(self, eng, fn, reads=(), writes=()):
        deps = self._deps(eng, reads, writes)
        self._wait_all(eng, deps, pe_skip_self=(eng is self.pe))
        inst = fn()
        if eng is not self.pe and self.fence_scr is not None:
            scr = self.fence_scr[eng.name]
            if eng is self.act:
                inst = self.nc.scalar.copy(out=scr[:, 1:2], in_=scr[:, 0:1])
            elif eng is self.dve:
                inst = self.nc.vector.tensor_copy(out=scr[:, 1:2], in_=scr[:, 0:1])
            else:
                inst = self.nc.gpsimd.tensor_copy(out=scr[:, 1:2], in_=scr[:, 0:1])
        eng.n += 1
        inst.then_inc(eng.sem, 1)
        tok = Tok(eng.sem, eng.n, eng)
        self._record(tok, reads, writes)
        return tok

    def mm_group(self, out_ap, pairs, reads, writes, transpose=False):
        eng = self.pe
        deps = self._deps(eng, reads, writes)
        self._wait_all(eng, deps, pe_skip_self=True)
        n = len(pairs)
        inst = None
        for i, (l, r) in enumerate(pairs):
            inst = self.nc.tensor.matmul(out_ap, lhsT=l, rhs=r, start=(i == 0), stop=(i == n - 1))
        eng.n += 1
        inst.then_inc(eng.sem, 1)
        tok = Tok(eng.sem, eng.n, eng)
        self._record(tok, reads, writes)
        return tok

    def mm1(self, out_ap, l, r, start, stop, reads, writes, signal):
        eng = self.pe
        deps = self._deps(eng, reads, writes)
        self._wait_all(eng, deps, pe_skip_self=True)
        inst = self.nc.tensor.matmul(out_ap, lhsT=l, rhs=r, start=start, stop=stop)
        if signal:
            eng.n += 1
            inst.then_inc(eng.sem, 1)
            tok = Tok(eng.sem, eng.n, eng)
            self._record(tok, reads, writes)
        else:
            tok = Tok(eng.sem, eng.n + 1, eng)
            for b in reads:
                b.r[id(tok.sem)] = tok
        return None

    def dma(self, out, in_, reads=(), writes=(), q=None):
        eng = q or self.sp
        deps = self._deps(eng, reads, writes, is_dma=True)
        self._wait_all(eng, deps)
        j = self.dma_i % len(self.dma_sems)
        self.dma_i += 1
        if self.dma_last[j] is not None:
            eng.wait(self.dma_last[j])
        inst = eng.eng.dma_start(out=out, in_=in_)
        self.dma_cnt[j] += 16
        inst.then_inc(self.dma_sems[j], 16)
        tok = Tok(self.dma_sems[j], self.dma_cnt[j], None)
        self.dma_last[j] = tok
        self._record(tok, reads, writes, is_dma=True)
        self.all_dma_toks.append(tok)
        return tok

    def sb(self, name, shape, dtype):
        return self.es.enter_context(self.nc.sbuf_tensor(name, list(shape), dtype))


def build(debug=()):
    nc = bass.Bass("TRN2", target_bir_lowering=False)
    outer = contextlib.ExitStack()
    outer.enter_context(nc.allow_low_precision("bf16 matmul operands, fp32 accumulation"))
    outer.enter_context(nc.allow_non_contiguous_dma("small strided tiles"))
    cx = Ctx(nc)
    dbg = set(debug)

    def din(name, shape, dt=F32):
        return nc.dram_tensor(name, list(shape), dt, kind="ExternalInput").ap()

    def dscr(name, shape, dt):
        kind = "ExternalOutput" if name in dbg else "Internal"
        return nc.dram_tensor(name, list(shape), dt, kind=kind).ap()

    xa = din("xa", [NALL, D])
    cT = din("cT", [128, KC, 2])
    ada_w = din("ada_w", [D, 3 * D])
    ada_bT = din("ada_bT", [128, 48])
    norm_gT = din("norm_gT", [128, KC])
    w_in = din("w_in", [D, NCOLS])
    conv_wk = din("conv_wk", [128, MH, 5])
    conv_wq = din("conv_wq", [128, MH, 5])
    conv_bk = din("conv_bk", [128, MH])
    conv_bq = din("conv_bq", [128, MH])
    gate_b = din("gate_b", [32, 1])
    mhgT = din("mhgT", [128, KC])
    qg = din("qg", [128, 4])
    kg = din("kg", [128, 4])
    kvgT = din("kvgT", [128, 4])
    w_uk = din("w_uk", [KVR, AH * ANOPE])
    w_uv = din("w_uv", [KVR, AH * ADV])
    w_pm = din("w_pm", [D, D])
    w_pa = din("w_pa", [D, D])
    w_out = din("w_out", [D, D])
    cosT = din("cosT", [64, NALL])
    sinT = din("sinT", [64, NALL])
    ident_f = din("ident_f", [128, 128])
    masks = din("masks", [128, 2, 128])
    sel = din("sel", [32, 6, 32])
    selb = din("selb", [32, 2])
    hmask = din("hmask", [32, 16])
    eones_d = din("eones", [32, 128])
    out_d = nc.dram_tensor("out", [NOWN, D], F32, kind="ExternalOutput").ap()

    hT_d = dscr("hT_d", [D, NALL], BF16)
    gateb_d = dscr("gateb_d", [1, D], F32)
    rawk_d = dscr("rawk_d", [MH, 128, NALL], F32)
    rawq_d = dscr("rawq_d", [MH, 128, NOWN + 2], F32)
    graw_d = dscr("graw_d", [32, NALL], F32)
    ckv_d = dscr("ckv_d", [KVR, NALL], F32)
    kr_d = dscr("kr_d", [128, NALL], F32)
    vm_d = dscr("vm_d", [NALL, MH * MDV], BF16)
    gmT_d = dscr("gmT_d", [D, NOWN], BF16)
    qaT_d = dscr("qaT_d", [AH, 192, NOWN], BF16)
    szaT_d = dscr("szaT_d", [D, NOWN], BF16)
    ggT_d = dscr("ggT_d", [2 * D, NOWN], BF16)
    kT_d = dscr("kT_d", [MH, 128, NALL], BF16)
    qT_d = dscr("qT_d", [MH, 128, NOWN], BF16)
    hA_d = dscr("hA_d", [NOWN, MH * MDV], F32)
    hmgT_d = dscr("hmgT_d", [D, NOWN], BF16)
    kaT_d = dscr("kaT_d", [AH, 192, NALL], BF16)
    va_d = dscr("va_d", [NALL, AH * ADV], BF16)
    ozT_d = dscr("ozT_d", [D, NOWN], BF16)
    mixT_d = dscr("mixT_d", [D, NOWN], BF16)

    es = cx.es
    banks = [es.enter_context(nc.psum_tensor(f"pb{i}", [128, 512], F32)) for i in range(8)]
    bbuf = [Buf(f"bank{i}") for i in range(8)]

    identf = cx.sb("identf", [128, 128], F32)
    identb = cx.sb("identb", [128, 128], BF16)
    onesb = cx.sb("onesb", [128, 128], BF16)
    modc = cx.sb("modc", [128, 48, 2], F32)
    gsc = cx.sb("gsc", [128, KC, 2], F32)
    B_const = Buf("const")
    B_mod = Buf("mod")
    fscr = {n_: cx.sb(f"fscr_{n_}", [128, 2], F32) for n_ in ("act", "dve", "pool")}
    nc.gpsimd.memset(fscr["pool"][:], 0.0)
    nc.gpsimd.memset(fscr["dve"][:], 0.0)
    nc.gpsimd.memset(fscr["act"][:], 0.0).then_inc(cx.pool.sem, 1)
    cx.pool.n += 1
    for e_ in (cx.act, cx.dve, cx.pool):
        e_.wait(Tok(cx.pool.sem, cx.pool.n, cx.pool))
    cx.fence_scr = fscr

    cx.dma(identf[:], ident_f[:, :], writes=[B_const])
    cx.op(cx.dve, lambda: nc.vector.tensor_copy(out=identb[:], in_=identf[:]), reads=[B_const], writes=[B_const])
    cx.op(cx.pool, lambda: nc.gpsimd.memset(onesb[:], 1.0), writes=[B_const])

    stage_es = contextlib.ExitStack()

    def stage_begin():
        nonlocal stage_es
        stage_es = contextlib.ExitStack()
        return stage_es

    def st(name, shape, dtype):
        return stage_es.enter_context(nc.sbuf_tensor(name + f"_{cx.uid()}", list(shape), dtype))

    def stage_end():
        toks = []
        for e in (cx.pe, cx.act, cx.dve, cx.pool):
            if e.n > 0:
                toks.append(Tok(e.sem, e.n, e))
        for j, t in enumerate(cx.dma_last):
            if t is not None:
                toks.append(t)
        for e in (cx.pe, cx.act, cx.dve, cx.pool, cx.sp):
            for t in toks:
                e.wait(t)
        stage_es.close()

    stage_begin()
    ct_sb = st("ct", [128, KC, 2], F32)
    sc_sb = st("sc", [128, KC, 2], F32)
    adab = st("adab", [128, 48], F32)
    ngt = st("ngt", [128, KC], F32)
    B_ct = Buf("ct")
    cx.dma(ct_sb[:], cT[:, :, :], writes=[B_ct])
    cx.dma(adab[:], ada_bT[:, :], writes=[B_ct])
    cx.dma(ngt[:], norm_gT[:, :], writes=[B_ct])
    cx.op(cx.act, lambda: nc.scalar.activation(out=sc_sb[:], in_=ct_sb[:], func=AF.Sigmoid), reads=[B_ct], writes=[B_ct])
    cx.op(cx.dve, lambda: nc.vector.tensor_tensor(out=sc_sb[:], in0=sc_sb[:], in1=ct_sb[:], op=ALU.mult),
          reads=[B_ct], writes=[B_ct])
    awt = [st(f"awt{i}", [128, KC, 512], F32) for i in range(2)]
    B_awt = [Buf("awt0"), Buf("awt1")]
    ada_v = ada_w.rearrange("(kc p) n -> p kc n", p=128)
    for cb in range(12):
        sl = cb % 2
        cx.dma(awt[sl][:], ada_v[:, :, cb * 512:(cb + 1) * 512], writes=[B_awt[sl]])
        for sub in range(4):
            j = cb * 4 + sub
            bk = j % 2
            pairs = [(awt[sl][:, kc, sub * 128:(sub + 1) * 128], sc_sb[:, kc, :]) for kc in range(KC)]
            cx.mm_group(banks[bk][:, 0:2], pairs, reads=[B_awt[sl], B_ct], writes=[bbuf[bk]])
            cx.op(cx.dve, lambda j=j, bk=bk: nc.vector.tensor_scalar(
                out=modc[:, j, :], in0=banks[bk][:, 0:2], scalar1=adab[:, j:j + 1], scalar2=None, op0=ALU.add),
                reads=[bbuf[bk], B_ct], writes=[B_mod])
    for i in range(2):
        cx.op(cx.dve, lambda i=i: nc.vector.scalar_tensor_tensor(
            out=gsc[:, :, i], in0=modc[:, 16:32, i], scalar=1.0, in1=ngt[:], op0=ALU.add, op1=ALU.mult),
            reads=[B_mod, B_ct], writes=[B_mod])
    gcol = st("gcol", [128, KC], F32)
    grow = st("grow", [16, 128], F32)
    cx.op(cx.dve, lambda: nc.vector.tensor_copy(out=gcol[:], in_=modc[:, 32:48, 0]), reads=[B_mod], writes=[B_ct])
    cx.op(cx.pe, lambda: nc.tensor.transpose(banks[2][0:16, 0:128], gcol[:], identf[:]), reads=[B_ct, B_const],
          writes=[bbuf[2]])
    cx.op(cx.dve, lambda: nc.vector.tensor_copy(out=grow[:], in_=banks[2][0:16, 0:128]), reads=[bbuf[2]], writes=[B_ct])
    B_gateb = Buf("gateb_d")
    cx.dma(gateb_d.rearrange("o (j p) -> (o j) p", p=128), grow[:], reads=[B_ct], writes=[B_gateb])
    stage_end()

    stage_begin()
    B_hT = Buf("hT_d")
    xt = [st(f"xt{i}", [128, D], F32) for i in range(2)]
    xs = [st(f"xs{i}", [128, D], F32) for i in range(2)]
    sq = st("sqjunk", [128, D], F32)
    hts = [st(f"hts{i}", [128, KC, 128], BF16) for i in range(2)]
    stat = [st(f"stat{i}", [128, 4], F32) for i in range(2)]
    B_xt = [Buf(), Buf()]
    B_xs = [Buf(), Buf()]
    B_sq = Buf()
    B_hts = [Buf(), Buf()]
    B_stat = [Buf(), Buf()]
    ntile = NALL // 128
    cx.dma(xt[0][:], xa[0:128, :], writes=[B_xt[0]])
    for ti in range(ntile):
        sl = ti % 2
        if ti + 1 < ntile:
            cx.dma(xt[1 - sl][:], xa[(ti + 1) * 128:(ti + 2) * 128, :], writes=[B_xt[1 - sl]])
        mi = 1 if ti < 2 else 0
        cx.op(cx.act, lambda sl=sl: nc.scalar.activation(out=sq[:], in_=xt[sl][:], func=AF.Square,
                                                          accum_out=stat[sl][:, 0:1]),
              reads=[B_xt[sl]], writes=[B_sq, B_stat[sl]])
        cx.op(cx.act, lambda sl=sl: nc.scalar.activation(out=stat[sl][:, 1:2], in_=stat[sl][:, 0:1], func=AF.Sqrt,
                                                          scale=1.0 / D, bias=EPS),
              reads=[B_stat[sl]], writes=[B_stat[sl]])
        cx.op(cx.dve, lambda sl=sl: nc.vector.reciprocal(out=stat[sl][:, 2:3], in_=stat[sl][:, 1:2]),
              reads=[B_stat[sl]], writes=[B_stat[sl]])
        cx.op(cx.dve, lambda sl=sl: nc.vector.tensor_scalar(out=xs[sl][:], in0=xt[sl][:], scalar1=stat[sl][:, 2:3],
                                                             scalar2=None, op0=ALU.mult),
              reads=[B_xt[sl], B_stat[sl]], writes=[B_xs[sl]])
        for g in range(4):
            bk = (ti * 4 + g) % 8
            for q in range(4):
                kc = g * 4 + q
                cx.op(cx.pe, lambda kc=kc, bk=bk, q=q, sl=sl: nc.tensor.transpose(
                    banks[bk][:, q * 128:(q + 1) * 128], xs[sl][:, kc * 128:(kc + 1) * 128], identf[:]),
                    reads=[B_xs[sl], B_const], writes=[bbuf[bk]])
            for q in range(4):
                kc = g * 4 + q
                e = cx.act if q % 2 == 0 else cx.dve
                if e is cx.act:
                    cx.op(e, lambda kc=kc, bk=bk, q=q, sl=sl, mi=mi: nc.scalar.activation(
                        out=hts[sl][:, kc, :], in_=banks[bk][:, q * 128:(q + 1) * 128], func=AF.Identity,
                        scale=gsc[:, kc, mi:mi + 1], bias=modc[:, kc, mi:mi + 1]),
                        reads=[bbuf[bk], B_mod], writes=[B_hts[sl]])
                else:
                    cx.op(e, lambda kc=kc, bk=bk, q=q, sl=sl, mi=mi: nc.vector.tensor_scalar(
                        out=hts[sl][:, kc, :], in0=banks[bk][:, q * 128:(q + 1) * 128],
                        scalar1=gsc[:, kc, mi:mi + 1], scalar2=modc[:, kc, mi:mi + 1], op0=ALU.mult, op1=ALU.add),
                        reads=[bbuf[bk], B_mod], writes=[B_hts[sl]])
        cx.dma(hT_d.rearrange("(kc p) a -> p kc a", p=128)[:, :, ti * 128:(ti + 1) * 128], hts[sl][:],
               reads=[B_hts[sl]], writes=[B_hT])
    stage_end()


    B_rawk, B_rawq, B_graw, B_ckv, B_kr, B_vm = Buf(), Buf(), Buf(), Buf(), Buf(), Buf()
    B_gmT, B_qaT, B_szaT, B_ggT = Buf(), Buf(), Buf(), Buf()
    hT_v = hT_d.rearrange("(kc p) a -> p kc a", p=128)
    w_v = w_in.rearrange("(kc p) n -> p kc n", p=128)
    for ps in range(2):
        stage_begin()
        if ps == 0:
            A0, NT = 0, NOWN + NCTX + 2
        else:
            A0, NT = NOWN + NCTX, NOWN
        hres = st("hres", [128, KC, NT], BF16)
        B_h = Buf()
        for kc in range(KC):
            cx.dma(hres[:, kc, :], hT_v[:, kc, A0:A0 + NT], reads=[B_hT], writes=[B_h])
        wf = [st(f"wf{i}", [128, KC, 256], F32) for i in range(3)]
        wb = [st(f"wb{i}", [128, KC, 256], BF16) for i in range(3)]
        B_wf = [Buf(), Buf(), Buf()]
        B_wb = [Buf(), Buf(), Buf()]
        tf = [st(f"tf{i}", [128, 512], F32) for i in range(6)]
        tb = [st(f"tb{i}", [128, 512], BF16) for i in range(4)]
        B_tf = [Buf() for _ in range(6)]
        B_tb = [Buf() for _ in range(4)]
        rr = {"tf": 0, "tb": 0, "bk": 0, "w": 0, "ev": 0}

        def nxt(kind, n):
            i = rr[kind] % n
            rr[kind] += 1
            return i

        if ps == 0:
            cos_sb = st("cos_sb", [64, NOWN], F32)
            sin_sb = st("sin_sb", [64, NOWN], F32)
            qg_sb = st("qg_sb", [128, 4], F32)
            B_tab = Buf()
            cx.dma(cos_sb[:], cosT[:, NCTX:NCTX + NOWN], writes=[B_tab])
            cx.dma(sin_sb[:], sinT[:, NCTX:NCTX + NOWN], writes=[B_tab])
            cx.dma(qg_sb[:], qg[:, :], writes=[B_tab])
            cx.op(cx.dve, lambda: nc.vector.tensor_scalar(out=cos_sb[:], in0=cos_sb[:], scalar1=qg_sb[0:64, 1:2],
                                                          scalar2=None, op0=ALU.mult), reads=[B_tab], writes=[B_tab])
            cx.op(cx.dve, lambda: nc.vector.tensor_scalar(out=sin_sb[:], in0=sin_sb[:], scalar1=qg_sb[0:64, 2:3],
                                                          scalar2=None, op0=ALU.mult), reads=[B_tab], writes=[B_tab])

        pending = {}

        glist = [(C_KM + g * 256, 256) for g in range(4)] + [(C_GT, 32)] + [(C_CKV + g * 256, 256) for g in range(2)]
        glist += [(C_KR, 128)] + [(C_VM + g * 256, 256) for g in range(8)]
        if ps == 0:
            glist += [(C_QM + g * 256, 256) for g in range(4)] + [(C_OZ + j * 256, 256) for j in range(16)]
            glist += [(C_QA + h_ * 256, 256) for h_ in range(AH)]
            glist += [(C_ZA + g * 256 if g < 8 else C_GM + (g - 8) * 256, 256) for g in range(24)]
        gstate = {"issued": 0, "used": 0}

        def issue_next():
            gi = gstate["issued"]
            if gi >= len(glist):
                return
            c0, width = glist[gi]
            i = gi % 3
            cx.dma(wf[i][:, :, 0:width], w_v[:, :, c0:c0 + width], writes=[B_wf[i]])
            cx.op(cx.pool, lambda: nc.gpsimd.tensor_copy(out=wb[i][:, :, 0:width], in_=wf[i][:, :, 0:width]),
                  reads=[B_wf[i]], writes=[B_wb[i]])
            gstate["issued"] += 1

        def load_group(c0, width):
            gu = gstate["used"]
            assert glist[gu] == (c0, width), (glist[gu], c0, width)
            while gstate["issued"] < min(gu + 2, len(glist)):
                issue_next()
            gstate["used"] += 1
            return gu % 3

        def fm(wi, off, M, a0, n):
            bk = nxt("bk", 8)
            pairs = [(wb[wi][:, kc, off:off + M], hres[:, kc, a0 - A0:a0 - A0 + n]) for kc in range(KC)]
            cx.mm_group(banks[bk][0:M, 0:n], pairs, reads=[B_wb[wi], B_h], writes=[bbuf[bk]])
            return bk

        def dump_raw(bk, M, n, dst_ap, dst_buf):
            i = nxt("tf", 6)
            if nxt("ev", 2) == 0:
                cx.op(cx.act, lambda: nc.scalar.copy(out=tf[i][0:M, 0:n], in_=banks[bk][0:M, 0:n]),
                      reads=[bbuf[bk]], writes=[B_tf[i]])
            else:
                cx.op(cx.dve, lambda: nc.vector.tensor_copy(out=tf[i][0:M, 0:n], in_=banks[bk][0:M, 0:n]),
                      reads=[bbuf[bk]], writes=[B_tf[i]])
            cx.dma(dst_ap, tf[i][0:M, 0:n], reads=[B_tf[i]], writes=[dst_buf])

        def blocks(a_lo, a_hi):
            a = a_lo
            while a < a_hi:
                n = min(512, a_hi - a)
                yield a, n
                a += n

        AKV0, AKV1 = (0, NCTX + NOWN) if ps == 0 else (NCTX + NOWN, NALL)
        for g in range(4):
            wi = load_group(C_KM + g * 256, 256)
            for sub in range(2):
                hd = g * 2 + sub
                for a0, n in blocks(AKV0, AKV1):
                    bk = fm(wi, sub * 128, 128, a0, n)
                    dump_raw(bk, 128, n, rawk_d[hd, :, a0:a0 + n], B_rawk)
        wi = load_group(C_GT, 32)
        for a0, n in blocks(AKV0, AKV1):
            bk = fm(wi, 0, 32, a0, n)
            dump_raw(bk, 32, n, graw_d[:, a0:a0 + n], B_graw)
        for g in range(2):
            wi = load_group(C_CKV + g * 256, 256)
            for sub in range(2):
                r0 = (g * 2 + sub) * 128
                for a0, n in blocks(AKV0, AKV1):
                    bk = fm(wi, sub * 128, 128, a0, n)
                    dump_raw(bk, 128, n, ckv_d[r0:r0 + 128, a0:a0 + n], B_ckv)
        wi = load_group(C_KR, 128)
        for a0, n in blocks(AKV0, AKV1):
            bk = fm(wi, 0, 128, a0, n)
            dump_raw(bk, 128, n, kr_d[:, a0:a0 + n], B_kr)
        for g in range(8):
            wi = load_group(C_VM + g * 256, 256)
            for a0 in range(AKV0, AKV1, 128):
                bk = nxt("bk", 8)
                pairs = [(hres[:, kc, a0 - A0:a0 - A0 + 128], wb[wi][:, kc, :]) for kc in range(KC)]
                cx.mm_group(banks[bk][:, 0:256], pairs, reads=[B_wb[wi], B_h], writes=[bbuf[bk]])
                i = nxt("tb", 4)
                if nxt("ev", 2) == 0:
                    cx.op(cx.act, lambda: nc.scalar.copy(out=tb[i][:, 0:256], in_=banks[bk][:, 0:256]),
                          reads=[bbuf[bk]], writes=[B_tb[i]])
                else:
                    cx.op(cx.dve, lambda: nc.vector.tensor_copy(out=tb[i][:, 0:256], in_=banks[bk][:, 0:256]),
                          reads=[bbuf[bk]], writes=[B_tb[i]])
                cx.dma(vm_d[a0:a0 + 128, g * 256:(g + 1) * 256], tb[i][:, 0:256], reads=[B_tb[i]], writes=[B_vm])
        if ps == 0:
            Q0, Q1 = NCTX, NCTX + NOWN
            for g in range(4):
                wi = load_group(C_QM + g * 256, 256)
                for sub in range(2):
                    hd = g * 2 + sub
                    for a0, n in blocks(Q0, Q1 + 2):
                        bk = fm(wi, sub * 128, 128, a0, n)
                        dump_raw(bk, 128, n, rawq_d[hd, :, a0 - Q0:a0 - Q0 + n], B_rawq)
            for j in range(16):
                wi = load_group(C_OZ + j * 256, 256)
                for a0, n in blocks(Q0, Q1):
                    b1 = fm(wi, 0, 128, a0, n)
                    b2 = fm(wi, 128, 128, a0, n)
                    i1, i2 = nxt("tf", 6), nxt("tf", 6)
                    cx.op(cx.act, lambda: nc.scalar.activation(out=tf[i1][:, 0:n], in_=banks[b1][:, 0:n], func=AF.Sigmoid),
                          reads=[bbuf[b1]], writes=[B_tf[i1]])
                    cx.op(cx.act, lambda: nc.scalar.activation(out=tf[i2][:, 0:n], in_=banks[b2][:, 0:n], func=AF.Sigmoid),
                          reads=[bbuf[b2]], writes=[B_tf[i2]])
                    cx.op(cx.dve, lambda: nc.vector.tensor_tensor(out=tf[i2][:, 0:n], in0=banks[b2][:, 0:n],
                                                                  in1=tf[i2][:, 0:n], op=ALU.mult),
                          reads=[bbuf[b2], B_tf[i2]], writes=[B_tf[i2]])
                    ib = nxt("tb", 4)
                    cx.op(cx.dve, lambda: nc.vector.tensor_tensor(out=tb[ib][:, 0:n], in0=tf[i1][:, 0:n],
                                                                  in1=tf[i2][:, 0:n], op=ALU.mult),
                          reads=[B_tf[i1], B_tf[i2]], writes=[B_tb[ib]])
                    cx.dma(gmT_d[j * 128:(j + 1) * 128, a0 - Q0:a0 - Q0 + n], tb[ib][:, 0:n], reads=[B_tb[ib]],
                           writes=[B_gmT])
            for hd in range(AH):
                wi = load_group(C_QA + hd * 256, 256)
                for a0, n in blocks(Q0, Q1):
                    t0 = a0 - Q0
                    bn = fm(wi, 0, 128, a0, n)
                    br = fm(wi, 128, 64, a0, n)
                    bs = fm(wi, 192, 64, a0, n)
                    s1, s2 = nxt("tb", 4), nxt("tb", 4)
                    cx.op(cx.act, lambda: nc.scalar.activation(out=tb[s1][:, 0:n], in_=banks[bn][:, 0:n], func=AF.Square),
                          reads=[bbuf[bn]], writes=[B_tb[s1]])
                    cx.op(cx.act, lambda: nc.scalar.activation(out=tb[s2][0:64, 0:n], in_=banks[br][0:64, 0:n],
                                                               func=AF.Square),
                          reads=[bbuf[br]], writes=[B_tb[s2]])
                    bq = nxt("bk", 8)
                    cx.mm_group(banks[bq][:, 0:n], [(onesb[:, :], tb[s1][:, 0:n]), (onesb[0:64, :], tb[s2][0:64, 0:n])],
                                reads=[B_tb[s1], B_tb[s2], B_const], writes=[bbuf[bq]])
                    ir = nxt("tf", 6)
                    cx.op(cx.act, lambda: nc.scalar.activation(out=tf[ir][:, 0:n], in_=banks[bq][:, 0:n], func=AF.Ln,
                                                               scale=1.0 / 192.0, bias=EPS),
                          reads=[bbuf[bq]], writes=[B_tf[ir]])
                    cx.op(cx.act, lambda: nc.scalar.activation(out=tf[ir][:, 0:n], in_=tf[ir][:, 0:n], func=AF.Exp,
                                                               scale=-0.5),
                          reads=[B_tf[ir]], writes=[B_tf[ir]])
                    o1 = nxt("tb", 4)
                    cx.op(cx.dve, lambda: nc.vector.scalar_tensor_tensor(
                        out=tb[o1][:, 0:n], in0=banks[bn][:, 0:n], scalar=qg_sb[:, 0:1], in1=tf[ir][:, 0:n],
                        op0=ALU.mult, op1=ALU.mult), reads=[bbuf[bn], B_tf[ir], B_tab], writes=[B_tb[o1]])
                    cx.dma(qaT_d[hd, 0:128, t0:t0 + n], tb[o1][:, 0:n], reads=[B_tb[o1]], writes=[B_qaT])
                    ia, ib2 = nxt("tf", 6), nxt("tf", 6)
                    cx.op(cx.dve, lambda: nc.vector.tensor_tensor(out=tf[ia][0:64, 0:n], in0=banks[br][0:64, 0:n],
                                                                  in1=cos_sb[:, t0:t0 + n], op=ALU.mult),
                          reads=[bbuf[br], B_tab], writes=[B_tf[ia]])
                    cx.op(cx.dve, lambda: nc.vector.tensor_tensor(out=tf[ib2][0:64, 0:n], in0=banks[bs][0:64, 0:n],
                                                                  in1=sin_sb[:, t0:t0 + n], op=ALU.mult),
                          reads=[bbuf[bs], B_tab], writes=[B_tf[ib2]])
                    cx.op(cx.pool, lambda: nc.gpsimd.tensor_tensor(out=tf[ia][0:64, 0:n], in0=tf[ia][0:64, 0:n],
                                                                   in1=tf[ib2][0:64, 0:n], op=ALU.add),
                          reads=[B_tf[ia], B_tf[ib2]], writes=[B_tf[ia]])
                    o2 = nxt("tb", 4)
                    cx.op(cx.pool, lambda: nc.gpsimd.tensor_tensor(out=tb[o2][0:64, 0:n], in0=tf[ia][0:64, 0:n],
                                                                   in1=tf[ir][0:64, 0:n], op=ALU.mult),
                          reads=[B_tf[ia], B_tf[ir]], writes=[B_tb[o2]])
                    cx.dma(qaT_d[hd, 128:192, t0:t0 + n], tb[o2][0:64, 0:n], reads=[B_tb[o2]], writes=[B_qaT])
            for g in range(8 + 16):
                is_za = g < 8
                c0 = C_ZA + g * 256 if is_za else C_GM + (g - 8) * 256
                wi = load_group(c0, 256)
                for sub in range(2):
                    for a0, n in blocks(Q0, Q1):
                        t0 = a0 - Q0
                        bk = fm(wi, sub * 128, 128, a0, n)
                        ib = nxt("tb", 4)
                        if is_za:
                            i1 = nxt("tf", 6)
                            cx.op(cx.act, lambda: nc.scalar.activation(out=tf[i1][:, 0:n], in_=banks[bk][:, 0:n],
                                                                       func=AF.Sigmoid),
                                  reads=[bbuf[bk]], writes=[B_tf[i1]])
                            cx.op(cx.dve, lambda: nc.vector.tensor_tensor(out=tb[ib][:, 0:n], in0=banks[bk][:, 0:n],
                                                                          in1=tf[i1][:, 0:n], op=ALU.mult),
                                  reads=[bbuf[bk], B_tf[i1]], writes=[B_tb[ib]])
                            r0 = (g * 2 + sub) * 128
                            cx.dma(szaT_d[r0:r0 + 128, t0:t0 + n], tb[ib][:, 0:n], reads=[B_tb[ib]], writes=[B_szaT])
                        else:
                            cx.op(cx.act, lambda: nc.scalar.activation(out=tb[ib][:, 0:n], in_=banks[bk][:, 0:n],
                                                                       func=AF.Sigmoid),
                                  reads=[bbuf[bk]], writes=[B_tb[ib]])
                            r0 = ((g - 8) * 2 + sub) * 128
                            cx.dma(ggT_d[r0:r0 + 128, t0:t0 + n], tb[ib][:, 0:n], reads=[B_tb[ib]], writes=[B_ggT])
        stage_end()


    def barrier():
        toks = []
        for e in (cx.pe, cx.act, cx.dve, cx.pool):
            if e.n > 0:
                toks.append(Tok(e.sem, e.n, e))
        for t in cx.dma_last:
            if t is not None:
                toks.append(t)
        for e in (cx.pe, cx.act, cx.dve, cx.pool, cx.sp):
            for t in toks:
                e.wait(t)

    stage_begin()
    kTres = st("kTres", [128, MH, NALL], BF16)
    qTres = st("qTres", [128, MH, NOWN], BF16)
    LX = st("LX", [32, NALL], F32)
    YA = st("YA", [32, NALL], F32)
    B_kT, B_qT, B_LX, B_YA = Buf(), Buf(), Buf(), Buf()
    hmask_sb = st("hmask_sb", [32, 16], F32)
    eones = st("eones_sb", [32, 128], F32)
    maskf = st("maskf", [128, 2, 128], F32)
    mhg_sb = st("mhg_sb", [128, KC], F32)
    B_c5 = Buf()
    cx.dma(hmask_sb[:], hmask[:, :], writes=[B_c5])
    cx.dma(eones[:], eones_d[:, :], writes=[B_c5])
    cx.dma(maskf[:], masks[:, :, :], writes=[B_c5])
    cx.dma(mhg_sb[:], mhgT[:, :], writes=[B_c5])
    tmp_es = contextlib.ExitStack()

    def tt(name, shape, dtype):
        return tmp_es.enter_context(nc.sbuf_tensor(name + f"_{cx.uid()}", list(shape), dtype))

    craw = [tt(f"craw{i}", [128, 4 + NCTX], F32) for i in range(2)]
    lraw = [tt(f"lraw{i}", [128, 4 + 4096], F32) for i in range(2)]
    acc = tt("acc", [128, NALL], F32)
    sgm = tt("sgm", [128, NALL], F32)
    cwk = tt("cwk", [128, MH, 5], F32)
    cwq = tt("cwq", [128, MH, 5], F32)
    cbk = tt("cbk", [128, MH], F32)
    cbq = tt("cbq", [128, MH], F32)
    B_raw = [Buf(), Buf()]
    B_acc, B_sgm, B_cw = Buf(), Buf(), Buf()
    cx.dma(cwk[:], conv_wk[:, :, :], writes=[B_cw])
    cx.dma(cwq[:], conv_wq[:, :, :], writes=[B_cw])
    cx.dma(cbk[:], conv_bk[:, :], writes=[B_cw])
    cx.dma(cbq[:], conv_bq[:, :], writes=[B_cw])
    for i in range(2):
        cx.op(cx.pool, lambda i=i: nc.gpsimd.memset(craw[i][:], 0.0), writes=[B_raw[i]])
        cx.op(cx.pool, lambda i=i: nc.gpsimd.memset(lraw[i][:], 0.0), writes=[B_raw[i]])

    def conv_seg(src, dst0, n, wt, bt, hd, sl):
        cx.op(cx.dve, lambda: nc.vector.tensor_scalar(out=acc[:, dst0:dst0 + n], in0=src[:, 0:n], scalar1=wt[:, hd, 0:1],
                                                      scalar2=bt[:, hd:hd + 1], op0=ALU.mult, op1=ALU.add),
              reads=[B_raw[sl], B_cw], writes=[B_acc])
        for j in range(1, 5):
            cx.op(cx.dve, lambda j=j: nc.vector.scalar_tensor_tensor(
                out=acc[:, dst0:dst0 + n], in0=src[:, j:j + n], scalar=wt[:, hd, j:j + 1], in1=acc[:, dst0:dst0 + n],
                op0=ALU.mult, op1=ALU.add), reads=[B_raw[sl], B_cw, B_acc], writes=[B_acc])

    it = 0
    for hd in range(MH):
        sl = it % 2
        it += 1
        cx.dma(craw[sl][:, 2:2 + NCTX], rawk_d[hd, :, 0:NCTX], reads=[B_rawk], writes=[B_raw[sl]])
        cx.dma(lraw[sl][:, 2:2 + 4096], rawk_d[hd, :, NCTX:NALL], reads=[B_rawk], writes=[B_raw[sl]])
        conv_seg(craw[sl], 0, NCTX, cwk, cbk, hd, sl)
        conv_seg(lraw[sl], NCTX, 4096, cwk, cbk, hd, sl)
        cx.op(cx.act, lambda: nc.scalar.activation(out=sgm[:], in_=acc[:], func=AF.Sigmoid), reads=[B_acc], writes=[B_sgm])
        cx.op(cx.pool, lambda hd=hd: nc.gpsimd.tensor_tensor(out=kTres[:, hd, :], in0=acc[:], in1=sgm[:], op=ALU.mult),
              reads=[B_acc, B_sgm], writes=[B_kT])
    for hd in range(MH):
        sl = it % 2
        it += 1
        cx.dma(lraw[sl][:, 2:2 + NOWN + 2], rawq_d[hd, :, :], reads=[B_rawq], writes=[B_raw[sl]])
        conv_seg(lraw[sl], 0, NOWN, cwq, cbq, hd, sl)
        cx.op(cx.act, lambda: nc.scalar.activation(out=sgm[:, 0:NOWN], in_=acc[:, 0:NOWN], func=AF.Sigmoid),
              reads=[B_acc], writes=[B_sgm])
        cx.op(cx.dve, lambda hd=hd: nc.vector.scalar_tensor_tensor(
            out=qTres[:, hd, :], in0=acc[:, 0:NOWN], scalar=float(MDQK ** -0.5), in1=sgm[:, 0:NOWN],
            op0=ALU.mult, op1=ALU.mult), reads=[B_acc, B_sgm], writes=[B_qT])
    barrier()
    tmp_es.close()
    tmp_es = contextlib.ExitStack()
    g_sb = tt("g_sb", [32, NALL], F32)
    sp_sb = tt("sp_sb", [32, NALL], F32)
    cs_sb = tt("cs_sb", [32, NALL], F32)
    rs_sb = tt("rs_sb", [32, NALL], F32)
    ones32 = tt("ones32", [32, 128], F32)
    gb_sb = tt("gb_sb", [32, 1], F32)
    sel_sb = tt("sel_sb", [32, 6, 32], F32)
    selb_sb = tt("selb_sb", [32, 2], F32)
    B_g, B_sp, B_cs, B_rs, B_gc = Buf(), Buf(), Buf(), Buf(), Buf()
    cx.dma(g_sb[:], graw_d[:, :], reads=[B_graw], writes=[B_g])
    cx.dma(gb_sb[:], gate_b[:, :], writes=[B_gc])
    cx.dma(sel_sb[:], sel[:, :, :], writes=[B_gc])
    cx.dma(selb_sb[:], selb[:, :], writes=[B_gc])
    cx.op(cx.pool, lambda: nc.gpsimd.memset(ones32[:], 1.0), writes=[B_gc])
    cx.op(cx.dve, lambda: nc.vector.tensor_scalar(out=g_sb[:], in0=g_sb[:], scalar1=gb_sb[:, 0:1], scalar2=None,
                                                  op0=ALU.add), reads=[B_g, B_gc], writes=[B_g])
    cx.op(cx.act, lambda: nc.scalar.activation(out=sp_sb[:], in_=g_sb[:], func=AF.Exp, scale=-1.0),
          reads=[B_g], writes=[B_sp])
    cx.op(cx.act, lambda: nc.scalar.activation(out=sp_sb[:], in_=sp_sb[:], func=AF.Ln, scale=1.0, bias=1.0),
          reads=[B_sp], writes=[B_sp])
    for c in range(NCHUNK):
        cx.op(cx.dve, lambda c=c: nc.vector.tensor_tensor_scan(
            out=cs_sb[:, c * CH:(c + 1) * CH], data0=ones32[:], data1=sp_sb[:, c * CH:(c + 1) * CH], initial=0.0,
            op0=ALU.mult, op1=ALU.add), reads=[B_sp, B_gc], writes=[B_cs])
    cx.op(cx.dve, lambda: nc.vector.tensor_tensor(out=rs_sb[:], in0=sp_sb[:], in1=cs_sb[:], op=ALU.subtract),
          reads=[B_sp, B_cs], writes=[B_rs])
    cs3 = cs_sb[:].rearrange("p (c t) -> p c t", t=CH)
    rs3 = rs_sb[:].rearrange("p (c t) -> p c t", t=CH)
    cx.op(cx.dve, lambda: nc.vector.tensor_tensor(out=rs3, in0=rs3, in1=cs3[:, :, CH - 1:CH].broadcast_to([32, NCHUNK, CH]),
                                                  op=ALU.add), reads=[B_rs, B_cs], writes=[B_rs])
    a = 0
    while a < NALL:
        n = min(512, NALL - a)
        bk = (a // 512) % 2
        cx.mm_group(banks[bk][0:32, 0:n], [(sel_sb[:, 0, :], g_sb[:, a:a + n]), (sel_sb[:, 1, :], cs_sb[:, a:a + n]),
                                           (sel_sb[:, 2, :], rs_sb[:, a:a + n])],
                    reads=[B_g, B_cs, B_rs, B_gc], writes=[bbuf[bk]])
        cx.op(cx.act, lambda a=a, n=n, bk=bk: nc.scalar.activation(out=LX[:, a:a + n], in_=banks[bk][0:32, 0:n],
                                                                    func=AF.Identity, bias=selb_sb[:, 0:1], scale=1.0),
              reads=[bbuf[bk], B_gc], writes=[B_LX])
        bk2 = 2 + bk
        cx.mm_group(banks[bk2][0:32, 0:n], [(sel_sb[:, 3, :], cs_sb[:, a:a + n]), (sel_sb[:, 4, :], rs_sb[:, a:a + n])],
                    reads=[B_cs, B_rs, B_gc], writes=[bbuf[bk2]])
        cx.op(cx.act, lambda a=a, n=n, bk2=bk2: nc.scalar.activation(out=YA[:, a:a + n], in_=banks[bk2][0:32, 0:n],
                                                                      func=AF.Identity, bias=selb_sb[:, 1:2], scale=1.0),
              reads=[bbuf[bk2], B_gc], writes=[B_YA])
        a += n
    if "LX_dbg" in dbg:
        lx_o = nc.dram_tensor("LX_dbg", [32, NALL], F32, kind="ExternalOutput").ap()
        ya_o = nc.dram_tensor("YA_dbg", [32, NALL], F32, kind="ExternalOutput").ap()
        cs_o = nc.dram_tensor("CS_dbg", [32, NALL], F32, kind="ExternalOutput").ap()
        rs_o = nc.dram_tensor("RS_dbg", [32, NALL], F32, kind="ExternalOutput").ap()
        sp_o = nc.dram_tensor("SP_dbg", [32, NALL], F32, kind="ExternalOutput").ap()
        kt_o = nc.dram_tensor("kT_dbg", [128, MH, NALL], BF16, kind="ExternalOutput").ap()
        qt_o = nc.dram_tensor("qT_dbg", [128, MH, NOWN], BF16, kind="ExternalOutput").ap()
        B_dd = Buf()
        cx.dma(lx_o[:, :], LX[:], reads=[B_LX], writes=[B_dd])
        cx.dma(ya_o[:, :], YA[:], reads=[B_YA], writes=[B_dd])
        cx.dma(cs_o[:, :], cs_sb[:], reads=[B_cs], writes=[B_dd])
        cx.dma(rs_o[:, :], rs_sb[:], reads=[B_rs], writes=[B_dd])
        cx.dma(sp_o[:, :], sp_sb[:], reads=[B_sp], writes=[B_dd])
        cx.dma(kt_o[:, :, :], kTres[:], reads=[B_kT], writes=[B_dd])
        cx.dma(qt_o[:, :, :], qTres[:], reads=[B_qT], writes=[B_dd])
    barrier()
    tmp_es.close()

    Cf = st("Cf", [128, MH, 257], F32)
    Cb = st("Cb", [128, MH, 257], BF16)
    B_Cf = [Buf(), Buf()]
    B_Cb = [Buf(), Buf()]
    vp = [st(f"vp{i}", [128, MH, 257], BF16) for i in range(2)]
    B_vp = [Buf(), Buf()]
    for i in range(2):
        cx.op(cx.pool, lambda i=i: nc.gpsimd.memset(vp[i][:], 1.0), writes=[B_vp[i]])
    Hacc1 = st("Hacc", [128, MH, 256], F32)
    Hacc = [Hacc1, Hacc1]
    B_H1 = Buf()
    B_H = [B_H1, B_H1]
    dexp4 = [st(f"dexp4_{i}", [128, 4, CH], F32) for i in range(2)]
    eexp4 = [st(f"eexp4_{i}", [128, 4, CH], F32) for i in range(2)]
    stil4 = [st(f"stil4_{i}", [128, 4, CH], BF16) for i in range(2)]
    qtil4 = [st(f"qtil4_{i}", [128, 4, CH], BF16) for i in range(2)]
    ktil4 = [st(f"ktil4_{i}", [128, 4, CH], BF16) for i in range(2)]
    ryh4 = [st(f"ryh4_{i}", [32, 4, CH], F32) for i in range(2)]
    rden4 = [st(f"rden4_{i}", [128, 2, 4], F32) for i in range(2)]
    B_dexp, B_eexp, B_stil, B_qtil, B_ktil, B_ryh, B_rden = ([Buf(), Buf()] for _ in range(7))
    hA_t = st("hA_t", [128, MH * MDV], F32)
    hn_b = st("hn_b", [128, MH * MDV], BF16)
    gm_t = st("gm_t", [128, KC, 128], BF16)
    hmg_t = st("hmg_t", [128, KC, 128], BF16)
    nstat = st("nstat", [128, 3, MH], F32)
    B_hA, B_hsq, B_hn, B_gmt, B_hmgt, B_nst = Buf(), Buf(), Buf(), Buf(), Buf(), Buf()
    B_hAd, B_hmgT = Buf(), Buf()
    bPT, bST, bE, bK = 0, 1, 2, 3
    bO, bD = (4, 5), (6, 7)
    bK16 = banks[bK][:].bitcast(BF16)
    vm_v = vm_d.rearrange("a (h v) -> a h v", v=MDV)
    gmT_v = gmT_d.rearrange("(kc p) t -> p kc t", p=128)
    hmgT_v = hmgT_d.rearrange("(kc p) t -> p kc t", p=128)
    vi = 0
    gcnt = 0
    for scan in range(2):
        for hd in range(MH):
            cx.op(cx.pool, lambda hd=hd: nc.gpsimd.memset(Cf[:, hd, :], 0.0), writes=[B_Cf[hd // 4]])
            cx.op(cx.pool, lambda hd=hd: nc.gpsimd.memset(Cb[:, hd, :], 0.0), writes=[B_Cb[hd // 4]])
        if scan == 0:
            order = [(0, False), (1, False)] + [(c, True) for c in range(2, 18)]
        else:
            order = [(1, False), (0, False)] + [(c, False) for c in range(33, 17, -1)] + [(c, True) for c in range(17, 1, -1)]
        tcol = CH - 1 if scan == 0 else 0
        for (c, outp) in order:
            a0 = c * CH
            t0 = a0 - NCTX
            vs = vi % 2
            vi += 1
            cx.dma(vp[vs][:, :, 0:256], vm_v[a0:a0 + CH, :, :], reads=[B_vm], writes=[B_vp[vs]])
            hs = vi % 2
            for gi in range(2):
                h0 = gi * 4
                gp = gcnt % 2
                gcnt += 1
                BCf, BCb = B_Cf[gi], B_Cb[gi]
                for hl in range(4):
                    j = scan * 8 + h0 + hl
                    cx.op(cx.pool, lambda hl=hl, j=j: nc.gpsimd.tensor_scalar(
                        out=ryh4[gp][:, hl, :], in0=YA[:, a0:a0 + CH], scalar1=hmask_sb[:, j:j + 1], scalar2=None,
                        op0=ALU.mult), reads=[B_YA, B_c5], writes=[B_ryh[gp]])
                if outp:
                    for hl in range(4):
                        cx.mm_group(banks[bPT][:, hl * CH:(hl + 1) * CH],
                                    [(LX[:, a0:a0 + CH], ryh4[gp][:, hl, :]), (identf[:, :], maskf[:, scan, :])],
                                    reads=[B_LX, B_ryh[gp], B_const, B_c5], writes=[bbuf[bPT]])
                    for hl in range(4):
                        cx.mm_group(banks[bST][:, hl * CH:(hl + 1) * CH],
                                    [(kTres[:, h0 + hl, a0:a0 + CH], qTres[:, h0 + hl, t0:t0 + CH])],
                                    reads=[B_kT, B_qT], writes=[bbuf[bST]])
                    for hl in range(4):
                        cx.mm_group(banks[bE][:, hl * CH:(hl + 1) * CH], [(eones[:, :], ryh4[gp][:, hl, :])],
                                    reads=[B_c5, B_ryh[gp]], writes=[bbuf[bE]])
                    cx.op(cx.act, lambda: nc.scalar.activation(out=dexp4[gp][:].rearrange("p h t -> p (h t)"),
                                                               in_=banks[bPT][:, 0:512], func=AF.Exp),
                          reads=[bbuf[bPT]], writes=[B_dexp[gp]])
                    cx.op(cx.act, lambda: nc.scalar.activation(out=eexp4[gp][:].rearrange("p h t -> p (h t)"),
                                                               in_=banks[bE][:, 0:512], func=AF.Exp),
                          reads=[bbuf[bE]], writes=[B_eexp[gp]])
                    cx.op(cx.dve, lambda: nc.vector.tensor_tensor(out=stil4[gp][:].rearrange("p h t -> p (h t)"),
                                                                  in0=banks[bST][:, 0:512],
                                                                  in1=dexp4[gp][:].rearrange("p h t -> p (h t)"), op=ALU.mult),
                          reads=[bbuf[bST], B_dexp[gp]], writes=[B_stil[gp]])
                    cx.op(cx.dve, lambda: nc.vector.tensor_tensor(out=qtil4[gp][:], in0=qTres[:, h0:h0 + 4, t0:t0 + CH],
                                                                  in1=eexp4[gp][:], op=ALU.mult),
                          reads=[B_qT, B_eexp[gp]], writes=[B_qtil[gp]])
                    for hl in range(4):
                        hd = h0 + hl
                        bo = bO[hl // 2]
                        cx.mm_group(banks[bo][:, (hl % 2) * 256:(hl % 2) * 256 + 256],
                                    [(stil4[gp][:, hl, :], vp[vs][:, hd, 0:256]), (qtil4[gp][:, hl, :], Cb[:, hd, 0:256])],
                                    reads=[B_stil[gp], B_qtil[gp], B_vp[vs], BCb], writes=[bbuf[bo]])
                        cx.mm_group(banks[bK][:, 256 + hl:257 + hl],
                                    [(stil4[gp][:, hl, :], vp[vs][:, hd, 256:257]), (qtil4[gp][:, hl, :], Cb[:, hd, 256:257])],
                                    reads=[B_stil[gp], B_qtil[gp], B_vp[vs], BCb], writes=[bbuf[bK]])
                    cx.op(cx.act, lambda: nc.scalar.activation(out=rden4[gp][:, 0, :], in_=banks[bK][:, 256:260], func=AF.Abs),
                          reads=[bbuf[bK]], writes=[B_rden[gp]])
                    cx.op(cx.dve, lambda: nc.vector.tensor_scalar(out=rden4[gp][:, 0, :], in0=rden4[gp][:, 0, :], scalar1=1.0,
                                                                  scalar2=None, op0=ALU.max),
                          reads=[B_rden[gp]], writes=[B_rden[gp]])
                    cx.op(cx.dve, lambda: nc.vector.reciprocal(out=rden4[gp][:, 1, :], in_=rden4[gp][:, 0, :]),
                          reads=[B_rden[gp]], writes=[B_rden[gp]])
                    for b2 in range(2):
                        cx.op(cx.dve, lambda b2=b2: nc.vector.tensor_tensor(
                            out=Hacc[hs][:, h0 + 2 * b2:h0 + 2 * b2 + 2, :],
                            in0=banks[bO[b2]][:, 0:512].rearrange("p (h v) -> p h v", v=256),
                            in1=rden4[gp][:, 1, 2 * b2:2 * b2 + 2].unsqueeze(2).broadcast_to([128, 2, 256]), op=ALU.mult),
                            reads=[bbuf[bO[b2]], B_rden[gp]], writes=[B_H[hs]])
                else:
                    for hl in range(4):
                        cx.mm_group(banks[bPT][:, hl * CH + tcol:hl * CH + tcol + 1],
                                    [(LX[:, a0:a0 + CH], ryh4[gp][:, hl, tcol:tcol + 1])],
                                    reads=[B_LX, B_ryh[gp]], writes=[bbuf[bPT]])
                        cx.mm_group(banks[bE][:, hl * CH + tcol:hl * CH + tcol + 1],
                                    [(eones[:, :], ryh4[gp][:, hl, tcol:tcol + 1])],
                                    reads=[B_c5, B_ryh[gp]], writes=[bbuf[bE]])
                    cx.op(cx.act, lambda: nc.scalar.activation(
                        out=dexp4[gp][:, :, tcol:tcol + 1],
                        in_=banks[bPT][:, 0:512].rearrange("p (h t) -> p h t", t=CH)[:, :, tcol:tcol + 1], func=AF.Exp),
                        reads=[bbuf[bPT]], writes=[B_dexp[gp]])
                    cx.op(cx.act, lambda: nc.scalar.activation(
                        out=eexp4[gp][:, :, tcol:tcol + 1],
                        in_=banks[bE][:, 0:512].rearrange("p (h t) -> p h t", t=CH)[:, :, tcol:tcol + 1], func=AF.Exp),
                        reads=[bbuf[bE]], writes=[B_eexp[gp]])
                for hl in range(4):
                    cx.op(cx.pe, lambda hl=hl: nc.tensor.transpose(bK16[:, hl * CH:(hl + 1) * CH],
                                                                   kTres[:, h0 + hl, a0:a0 + CH], identb[:]),
                          reads=[B_kT, B_const], writes=[bbuf[bK]])
                cx.op(cx.dve, lambda: nc.vector.tensor_tensor(
                    out=ktil4[gp][:], in0=bK16[:, 0:512].rearrange("p (h d) -> p h d", d=CH),
                    in1=dexp4[gp][:, :, tcol:tcol + 1].broadcast_to([128, 4, CH]), op=ALU.mult),
                    reads=[bbuf[bK], B_dexp[gp]], writes=[B_ktil[gp]])
                for hl in range(4):
                    hd = h0 + hl
                    bd = bD[hl // 2]
                    cx.mm_group(banks[bd][:, (hl % 2) * 256:(hl % 2) * 256 + 256], [(ktil4[gp][:, hl, :], vp[vs][:, hd, 0:256])],
                                reads=[B_ktil[gp], B_vp[vs]], writes=[bbuf[bd]])
                    cx.mm_group(banks[bK][:, 260 + hl:261 + hl], [(ktil4[gp][:, hl, :], vp[vs][:, hd, 256:257])],
                                reads=[B_ktil[gp], B_vp[vs]], writes=[bbuf[bK]])
                cx.op(cx.dve, lambda: nc.vector.tensor_tensor(
                    out=Cf[:, h0:h0 + 4, :], in0=Cf[:, h0:h0 + 4, :],
                    in1=eexp4[gp][:, :, tcol:tcol + 1].broadcast_to([128, 4, 257]), op=ALU.mult),
                    reads=[BCf, B_eexp[gp]], writes=[BCf])
                for b2 in range(2):
                    cx.op(cx.dve, lambda b2=b2: nc.vector.tensor_tensor(
                        out=Cf[:, h0 + 2 * b2:h0 + 2 * b2 + 2, 0:256], in0=Cf[:, h0 + 2 * b2:h0 + 2 * b2 + 2, 0:256],
                        in1=banks[bD[b2]][:, 0:512].rearrange("p (h v) -> p h v", v=256), op=ALU.add),
                        reads=[BCf, bbuf[bD[b2]]], writes=[BCf])
                cx.op(cx.dve, lambda: nc.vector.tensor_tensor(
                    out=Cf[:, h0:h0 + 4, 256:257], in0=Cf[:, h0:h0 + 4, 256:257],
                    in1=banks[bK][:, 260:264].unsqueeze(2), op=ALU.add),
                    reads=[BCf, bbuf[bK]], writes=[BCf])
                cx.op(cx.pool, lambda: nc.gpsimd.tensor_copy(out=Cb[:, h0:h0 + 4, :], in_=Cf[:, h0:h0 + 4, :]),
                      reads=[BCf], writes=[BCb])
            if not outp:
                continue
            Hf = Hacc[hs][:].rearrange("p h v -> p (h v)")
            if scan == 0:
                cx.dma(hA_d[t0:t0 + CH, :], Hf, reads=[B_H[hs]], writes=[B_hAd])
                continue
            cx.dma(hA_t[:], hA_d[t0:t0 + CH, :], reads=[B_hAd], writes=[B_hA])
            cx.dma(gm_t[:], gmT_v[:, :, t0:t0 + CH], reads=[B_gmT], writes=[B_gmt])
            cx.op(cx.dve, lambda: nc.vector.tensor_tensor(out=hA_t[:], in0=hA_t[:], in1=Hf, op=ALU.add),
                  reads=[B_hA, B_H[hs]], writes=[B_hA])
            for hd8 in range(MH):
                cx.op(cx.act, lambda hd8=hd8: nc.scalar.activation(
                    out=hn_b[:, hd8 * MDV:(hd8 + 1) * MDV], in_=hA_t[:, hd8 * MDV:(hd8 + 1) * MDV], func=AF.Square,
                    accum_out=nstat[:, 0, hd8:hd8 + 1]), reads=[B_hA], writes=[B_hn, B_nst])
            cx.op(cx.act, lambda: nc.scalar.activation(out=nstat[:, 1, :], in_=nstat[:, 0, :], func=AF.Sqrt,
                                                       scale=1.0 / MDV, bias=EPS), reads=[B_nst], writes=[B_nst])
            cx.op(cx.dve, lambda: nc.vector.reciprocal(out=nstat[:, 2, :], in_=nstat[:, 1, :]), reads=[B_nst], writes=[B_nst])
            cx.op(cx.dve, lambda: nc.vector.tensor_tensor(
                out=hn_b[:].rearrange("p (h v) -> p h v", v=MDV), in0=hA_t[:].rearrange("p (h v) -> p h v", v=MDV),
                in1=nstat[:, 2, :].unsqueeze(2).broadcast_to([128, MH, MDV]), op=ALU.mult),
                reads=[B_hA, B_nst], writes=[B_hn])
            for half8 in range(2):
                bt = 6 + half8
                b16 = banks[bt][:].bitcast(BF16)
                for q in range(8):
                    kc = half8 * 8 + q
                    cx.op(cx.pe, lambda kc=kc, q=q, b16=b16: nc.tensor.transpose(
                        b16[:, q * 128:(q + 1) * 128], hn_b[:, kc * 128:(kc + 1) * 128], identb[:]),
                        reads=[B_hn, B_const], writes=[bbuf[bt]])
                for q in range(8):
                    kc = half8 * 8 + q
                    cx.op(cx.dve, lambda kc=kc, q=q, b16=b16: nc.vector.scalar_tensor_tensor(
                        out=hmg_t[:, kc, :], in0=b16[:, q * 128:(q + 1) * 128], scalar=mhg_sb[:, kc:kc + 1],
                        in1=gm_t[:, kc, :], op0=ALU.mult, op1=ALU.mult),
                        reads=[bbuf[bt], B_c5, B_gmt], writes=[B_hmgt])
            cx.dma(hmgT_v[:, :, t0:t0 + CH], hmg_t[:], reads=[B_hmgt], writes=[B_hmgT])
    stage_end()


    stage_begin()
    cnT = st("cnT", [128, 4, NALL], BF16)
    KRg = st("KRg", [64, NALL], F32)
    sqkr = st("sqkr", [64, NALL], BF16)
    cosk = st("cosk", [64, NALL], F32)
    sink = st("sink", [64, NALL], F32)
    kg_sb = st("kg_sb", [128, 4], F32)
    kvg_sb = st("kvg_sb", [128, 4], F32)
    B_cn, B_KR, B_sqkr, B_tk = Buf(), Buf(), Buf(), Buf()
    cx.dma(cosk[:], cosT[:, :], writes=[B_tk])
    cx.dma(sink[:], sinT[:, :], writes=[B_tk])
    cx.dma(kg_sb[:], kg[:, :], writes=[B_tk])
    cx.dma(kvg_sb[:], kvgT[:, :], writes=[B_tk])
    cx.op(cx.dve, lambda: nc.vector.tensor_scalar(out=cosk[:], in0=cosk[:], scalar1=kg_sb[0:64, 1:2], scalar2=None,
                                                  op0=ALU.mult), reads=[B_tk], writes=[B_tk])
    cx.op(cx.dve, lambda: nc.vector.tensor_scalar(out=sink[:], in0=sink[:], scalar1=kg_sb[0:64, 2:3], scalar2=None,
                                                  op0=ALU.mult), reads=[B_tk], writes=[B_tk])
    ck = [st(f"ck{i}", [128, 4, 512], F32) for i in range(2)]
    krt = [st(f"krt{i}", [64, 2, 512], F32) for i in range(2)]
    B_ck = [Buf(), Buf()]
    tf6 = [st(f"tf6_{i}", [128, 512], F32) for i in range(4)]
    tb6 = [st(f"tb6_{i}", [128, 4, 512], BF16) for i in range(2)]
    ob6 = [st(f"ob6_{i}", [128, 512], BF16) for i in range(4)]
    B_tf6 = [Buf() for _ in range(4)]
    B_tb6 = [Buf() for _ in range(2)]
    B_ob6 = [Buf() for _ in range(4)]
    rr6 = {"tf": 0, "ob": 0, "bk": 0, "w": 0}

    def n6(kind, n):
        i = rr6[kind] % n
        rr6[kind] += 1
        return i

    ckv_v = ckv_d.rearrange("(r p) a -> p r a", p=128)
    blks = [(a, min(512, NALL - a)) for a in range(0, NALL, 512)]
    for bi, (a0, n) in enumerate(blks):
        sl = bi % 2
        cx.dma(ck[sl][:, :, 0:n], ckv_v[:, :, a0:a0 + n], reads=[B_ckv], writes=[B_ck[sl]])
        cx.dma(krt[sl][:, 0, 0:n], kr_d[0:64, a0:a0 + n], reads=[B_kr], writes=[B_ck[sl]])
        cx.dma(krt[sl][:, 1, 0:n], kr_d[64:128, a0:a0 + n], reads=[B_kr], writes=[B_ck[sl]])
        cx.op(cx.act, lambda: nc.scalar.activation(out=tb6[sl][:, :, 0:n], in_=ck[sl][:, :, 0:n], func=AF.Square),
              reads=[B_ck[sl]], writes=[B_tb6[sl]])
        bk = n6("bk", 8)
        cx.mm_group(banks[bk][:, 0:n], [(onesb[:, :], tb6[sl][:, r, 0:n]) for r in range(4)],
                    reads=[B_tb6[sl], B_const], writes=[bbuf[bk]])
        ir = n6("tf", 4)
        cx.op(cx.act, lambda: nc.scalar.activation(out=tf6[ir][:, 0:n], in_=banks[bk][:, 0:n], func=AF.Ln,
                                                   scale=1.0 / KVR, bias=EPS), reads=[bbuf[bk]], writes=[B_tf6[ir]])
        cx.op(cx.act, lambda: nc.scalar.activation(out=tf6[ir][:, 0:n], in_=tf6[ir][:, 0:n], func=AF.Exp, scale=-0.5),
              reads=[B_tf6[ir]], writes=[B_tf6[ir]])
        for r in range(4):
            cx.op(cx.dve, lambda r=r: nc.vector.scalar_tensor_tensor(
                out=cnT[:, r, a0:a0 + n], in0=ck[sl][:, r, 0:n], scalar=kvg_sb[:, r:r + 1], in1=tf6[ir][:, 0:n],
                op0=ALU.mult, op1=ALU.mult), reads=[B_ck[sl], B_tf6[ir], B_tk], writes=[B_cn])
        cx.op(cx.act, lambda: nc.scalar.activation(out=sqkr[:, a0:a0 + n], in_=krt[sl][:, 0, 0:n], func=AF.Square),
              reads=[B_ck[sl]], writes=[B_sqkr])
        cx.op(cx.pool, lambda: nc.gpsimd.tensor_tensor(out=KRg[:, a0:a0 + n], in0=krt[sl][:, 0, 0:n], in1=cosk[:, a0:a0 + n],
                                                       op=ALU.mult), reads=[B_ck[sl], B_tk], writes=[B_KR])
        cx.op(cx.pool, lambda: nc.gpsimd.tensor_tensor(out=krt[sl][:, 1, 0:n], in0=krt[sl][:, 1, 0:n], in1=sink[:, a0:a0 + n],
                                                       op=ALU.mult), reads=[B_ck[sl], B_tk], writes=[B_ck[sl]])
        cx.op(cx.pool, lambda: nc.gpsimd.tensor_tensor(out=KRg[:, a0:a0 + n], in0=KRg[:, a0:a0 + n], in1=krt[sl][:, 1, 0:n],
                                                       op=ALU.add), reads=[B_ck[sl], B_KR], writes=[B_KR])
    wu_f = [st(f"wu_f{i}", [128, 4, 512], F32) for i in range(3)]
    wu_b = [st(f"wu_b{i}", [128, 4, 512], BF16) for i in range(3)]
    B_wuf = [Buf(), Buf(), Buf()]
    B_wub = [Buf(), Buf(), Buf()]
    B_kaT, B_va = Buf(), Buf()
    wuk_v = w_uk.rearrange("(r p) n -> p r n", p=128)
    wuv_v = w_uv.rearrange("(r p) n -> p r n", p=128)

    wlist6 = [(0, hd_ * 128, 128) for hd_ in range(AH)] + [(1, g4_ * 512, 512) for g4_ in range(4)]
    w6s = {"issued": 0, "used": 0}

    def issue_wu():
        gi = w6s["issued"]
        if gi >= len(wlist6):
            return
        which, c0, width = wlist6[gi]
        view = wuk_v if which == 0 else wuv_v
        i = gi % 3
        cx.dma(wu_f[i][:, :, 0:width], view[:, :, c0:c0 + width], writes=[B_wuf[i]])
        cx.op(cx.pool, lambda: nc.gpsimd.tensor_copy(out=wu_b[i][:, :, 0:width], in_=wu_f[i][:, :, 0:width]),
              reads=[B_wuf[i]], writes=[B_wub[i]])
        w6s["issued"] += 1

    def load_wu(view, c0, width):
        gu = w6s["used"]
        assert wlist6[gu][1:] == (c0, width)
        while w6s["issued"] < min(gu + 2, len(wlist6)):
            issue_wu()
        w6s["used"] += 1
        return gu % 3

    for hd in range(AH):
        wi = load_wu(wuk_v, hd * 128, 128)
        for (a0, n) in blks:
            bk = n6("bk", 8)
            cx.mm_group(banks[bk][:, 0:n], [(wu_b[wi][:, r, 0:128], cnT[:, r, a0:a0 + n]) for r in range(4)],
                        reads=[B_wub[wi], B_cn], writes=[bbuf[bk]])
            s1 = n6("ob", 4)
            cx.op(cx.act, lambda: nc.scalar.activation(out=ob6[s1][:, 0:n], in_=banks[bk][:, 0:n], func=AF.Square),
                  reads=[bbuf[bk]], writes=[B_ob6[s1]])
            bq = n6("bk", 8)
            cx.mm_group(banks[bq][:, 0:n], [(onesb[:, :], ob6[s1][:, 0:n]), (onesb[0:64, :], sqkr[:, a0:a0 + n])],
                        reads=[B_ob6[s1], B_sqkr, B_const], writes=[bbuf[bq]])
            ir = n6("tf", 4)
            cx.op(cx.act, lambda: nc.scalar.activation(out=tf6[ir][:, 0:n], in_=banks[bq][:, 0:n], func=AF.Ln,
                                                       scale=1.0 / 192.0, bias=EPS), reads=[bbuf[bq]], writes=[B_tf6[ir]])
            cx.op(cx.act, lambda: nc.scalar.activation(out=tf6[ir][:, 0:n], in_=tf6[ir][:, 0:n], func=AF.Exp, scale=-0.5),
                  reads=[B_tf6[ir]], writes=[B_tf6[ir]])
            o1 = n6("ob", 4)
            cx.op(cx.dve, lambda: nc.vector.scalar_tensor_tensor(
                out=ob6[o1][:, 0:n], in0=banks[bk][:, 0:n], scalar=kg_sb[:, 0:1], in1=tf6[ir][:, 0:n],
                op0=ALU.mult, op1=ALU.mult), reads=[bbuf[bk], B_tf6[ir], B_tk], writes=[B_ob6[o1]])
            cx.dma(kaT_d[hd, 0:128, a0:a0 + n], ob6[o1][:, 0:n], reads=[B_ob6[o1]], writes=[B_kaT])
            o2 = n6("ob", 4)
            cx.op(cx.pool, lambda: nc.gpsimd.tensor_tensor(out=ob6[o2][0:64, 0:n], in0=KRg[:, a0:a0 + n],
                                                           in1=tf6[ir][0:64, 0:n], op=ALU.mult),
                  reads=[B_KR, B_tf6[ir]], writes=[B_ob6[o2]])
            cx.dma(kaT_d[hd, 128:192, a0:a0 + n], ob6[o2][0:64, 0:n], reads=[B_ob6[o2]], writes=[B_kaT])
    for g4 in range(4):
        wi = load_wu(wuv_v, g4 * 512, 512)
        for a0 in range(0, NALL, 128):
            bk = n6("bk", 8)
            cx.mm_group(banks[bk][:, 0:512], [(cnT[:, r, a0:a0 + 128], wu_b[wi][:, r, :]) for r in range(4)],
                        reads=[B_wub[wi], B_cn], writes=[bbuf[bk]])
            o1 = n6("ob", 4)
            if (a0 // 128) % 2 == 0:
                cx.op(cx.act, lambda: nc.scalar.copy(out=ob6[o1][:, :], in_=banks[bk][:, 0:512]), reads=[bbuf[bk]],
                      writes=[B_ob6[o1]])
            else:
                cx.op(cx.dve, lambda: nc.vector.tensor_copy(out=ob6[o1][:, :], in_=banks[bk][:, 0:512]), reads=[bbuf[bk]],
                      writes=[B_ob6[o1]])
            cx.dma(va_d[a0:a0 + 128, g4 * 512:(g4 + 1) * 512], ob6[o1][:, :], reads=[B_ob6[o1]], writes=[B_va])
    stage_end()

    stage_begin()
    B_ozT = Buf()
    kn_s = [st(f"kn_s{i}", [128, NALL], BF16) for i in range(2)]
    kr_s = [st(f"kr_s{i}", [64, NALL], BF16) for i in range(2)]
    v_s = [st(f"v_s{i}", [128, NCHUNK, 128], BF16) for i in range(2)]
    qn_s = [st(f"qn_s{i}", [128, NOWN], BF16) for i in range(2)]
    qr_s = [st(f"qr_s{i}", [64, NOWN], BF16) for i in range(2)]
    sz_s = [st(f"sz_s{i}", [128, NOWN], BF16) for i in range(2)]
    B_hd = [Buf(), Buf()]
    pt = [st(f"pt{i}", [128, 512], BF16) for i in range(3)]
    B_pt = [Buf() for _ in range(3)]
    rd = st("rd", [128, 512], F32)
    ot = st("ot", [128, 512], F32)
    ozb = [st(f"ozb{i}", [128, 512], BF16) for i in range(2)]
    B_rd, B_ot = Buf(), Buf()
    B_ozb = [Buf(), Buf()]
    va_v = va_d.rearrange("(c p) n -> p c n", p=128)
    sc_scale = float(192.0 ** -0.5)
    pi = 0
    sbk = 0
    blkc = 0

    def load_head(hd):
        sl = hd % 2
        cx.dma(kn_s[sl][:], kaT_d[hd, 0:128, :], reads=[B_kaT], writes=[B_hd[sl]])
        cx.dma(kr_s[sl][:], kaT_d[hd, 128:192, :], reads=[B_kaT], writes=[B_hd[sl]])
        cx.dma(v_s[sl][:], va_v[:, :, hd * 128:(hd + 1) * 128], reads=[B_va], writes=[B_hd[sl]])
        cx.dma(qn_s[sl][:], qaT_d[hd, 0:128, :], reads=[B_qaT], writes=[B_hd[sl]])
        cx.dma(qr_s[sl][:], qaT_d[hd, 128:192, :], reads=[B_qaT], writes=[B_hd[sl]])
        cx.dma(sz_s[sl][:], szaT_d[hd * 128:(hd + 1) * 128, :], reads=[B_szaT], writes=[B_hd[sl]])

    load_head(0)
    for hd in range(AH):
        sl = hd % 2
        if hd + 1 < AH:
            load_head(hd + 1)
        for tbk in range(4):
            t0 = tbk * 512
            bo, bd = (4, 5) if blkc % 2 == 0 else (6, 7)
            blkc += 1

            def emit_s(sc):
                nonlocal sbk
                bs = sbk % 4
                sbk += 1
                cx.mm_group(banks[bs][:, 0:512], [(kn_s[sl][:, sc * 128:(sc + 1) * 128], qn_s[sl][:, t0:t0 + 512]),
                                                  (kr_s[sl][:, sc * 128:(sc + 1) * 128], qr_s[sl][:, t0:t0 + 512])],
                            reads=[B_hd[sl]], writes=[bbuf[bs]])
                return bs

            bs_next = emit_s(0)
            for sc in range(NCHUNK):
                bs = bs_next
                if sc + 1 < NCHUNK:
                    bs_next = emit_s(sc + 1)
                p = pi % 3
                pi += 1
                cx.op(cx.act, lambda: nc.scalar.activation(out=pt[p][:], in_=banks[bs][:, 0:512], func=AF.Exp,
                                                           scale=sc_scale, bias=-ATT_SHIFT),
                      reads=[bbuf[bs]], writes=[B_pt[p]])
                last = sc == NCHUNK - 1
                cx.mm1(banks[bo][:, 0:512], v_s[sl][:, sc, :], pt[p][:], start=(sc == 0), stop=last,
                       reads=[B_pt[p], B_hd[sl]], writes=[bbuf[bo]], signal=last)
                cx.mm1(banks[bd][:, 0:512], onesb[:, :], pt[p][:], start=(sc == 0), stop=last,
                       reads=[B_pt[p], B_const], writes=[bbuf[bd]], signal=True)
            cx.op(cx.dve, lambda: nc.vector.reciprocal(out=rd[:], in_=banks[bd][:, 0:512]), reads=[bbuf[bd]], writes=[B_rd])
            cx.op(cx.dve, lambda: nc.vector.tensor_tensor(out=ot[:], in0=banks[bo][:, 0:512], in1=rd[:], op=ALU.mult),
                  reads=[bbuf[bo], B_rd], writes=[B_ot])
            ob = (hd * 4 + tbk) % 2
            cx.op(cx.pool, lambda: nc.gpsimd.tensor_tensor(out=ozb[ob][:], in0=ot[:], in1=sz_s[sl][:, t0:t0 + 512], op=ALU.mult),
                  reads=[B_ot, B_hd[sl]], writes=[B_ozb[ob]])
            cx.dma(ozT_d[hd * 128:(hd + 1) * 128, t0:t0 + 512], ozb[ob][:], reads=[B_ozb[ob]], writes=[B_ozT])
    stage_end()

    B_mixT = Buf()
    hmg_v = hmgT_d.rearrange("(kc p) t -> p kc t", p=128)
    oz_v = ozT_d.rearrange("(kc p) t -> p kc t", p=128)
    wpm_v = w_pm.rearrange("(kc p) n -> p kc n", p=128)
    wpa_v = w_pa.rearrange("(kc p) n -> p kc n", p=128)
    for th in range(2):
        stage_begin()
        T0 = th * 1024
        hmg_r = st("hmg_r", [128, KC, 1024], BF16)
        oz_r = st("oz_r", [128, KC, 1024], BF16)
        B_res = Buf()
        for kc in range(KC):
            cx.dma(hmg_r[:, kc, :], hmg_v[:, kc, T0:T0 + 1024], reads=[B_hmgT], writes=[B_res])
            cx.dma(oz_r[:, kc, :], oz_v[:, kc, T0:T0 + 1024], reads=[B_ozT], writes=[B_res])
        wf8 = [st(f"wf8_{i}", [128, KC, 128], F32) for i in range(4)]
        wb8 = [st(f"wb8_{i}", [128, KC, 128], BF16) for i in range(4)]
        B_wf8 = [Buf() for _ in range(4)]
        B_wb8 = [Buf() for _ in range(4)]
        gg8 = [st(f"gg8_{i}", [128, 2, 512], BF16) for i in range(2)]
        B_gg8 = [Buf(), Buf()]
        ta8 = [st(f"ta8_{i}", [128, 512], F32) for i in range(2)]
        tb8 = [st(f"tb8_{i}", [128, 512], F32) for i in range(2)]
        mo8 = [st(f"mo8_{i}", [128, 512], BF16) for i in range(2)]
        B_ta8, B_tb8, B_mo8 = [Buf(), Buf()], [Buf(), Buf()], [Buf(), Buf()]
        it8 = 0
        def load_w8(j):
            for wsel, view in ((0, wpm_v), (1, wpa_v)):
                i = (j * 2 + wsel) % 4
                cx.dma(wf8[i][:], view[:, :, j * 128:(j + 1) * 128], writes=[B_wf8[i]])
                cx.op(cx.pool, lambda i=i: nc.gpsimd.tensor_copy(out=wb8[i][:], in_=wf8[i][:]), reads=[B_wf8[i]],
                      writes=[B_wb8[i]])

        load_w8(0)
        for j in range(KC):
            if j + 1 < KC:
                load_w8(j + 1)
            ws = [(j * 2) % 4, (j * 2 + 1) % 4]
            for blk in range(2):
                s8 = it8 % 2
                it8 += 1
                t0 = T0 + blk * 512
                cx.dma(gg8[s8][:, 0, :], ggT_d[j * 128:(j + 1) * 128, t0:t0 + 512], reads=[B_ggT], writes=[B_gg8[s8]])
                cx.dma(gg8[s8][:, 1, :], ggT_d[D + j * 128:D + (j + 1) * 128, t0:t0 + 512], reads=[B_ggT], writes=[B_gg8[s8]])
                bm, ba = (it8 % 2) * 2, (it8 % 2) * 2 + 1
                cx.mm_group(banks[bm][:, 0:512], [(wb8[ws[0]][:, kc, :], hmg_r[:, kc, blk * 512:(blk + 1) * 512]) for kc in range(KC)],
                            reads=[B_wb8[ws[0]], B_res], writes=[bbuf[bm]])
                cx.mm_group(banks[ba][:, 0:512], [(wb8[ws[1]][:, kc, :], oz_r[:, kc, blk * 512:(blk + 1) * 512]) for kc in range(KC)],
                            reads=[B_wb8[ws[1]], B_res], writes=[bbuf[ba]])
                cx.op(cx.dve, lambda: nc.vector.tensor_tensor(out=ta8[s8][:], in0=banks[bm][:, 0:512], in1=gg8[s8][:, 0, :],
                                                              op=ALU.mult), reads=[bbuf[bm], B_gg8[s8]], writes=[B_ta8[s8]])
                cx.op(cx.dve, lambda: nc.vector.tensor_tensor(out=tb8[s8][:], in0=banks[ba][:, 0:512], in1=gg8[s8][:, 1, :],
                                                              op=ALU.mult), reads=[bbuf[ba], B_gg8[s8]], writes=[B_tb8[s8]])
                cx.op(cx.pool, lambda: nc.gpsimd.tensor_tensor(out=mo8[s8][:], in0=ta8[s8][:], in1=tb8[s8][:], op=ALU.add),
                      reads=[B_ta8[s8], B_tb8[s8]], writes=[B_mo8[s8]])
                cx.dma(mixT_d[j * 128:(j + 1) * 128, t0:t0 + 512], mo8[s8][:], reads=[B_mo8[s8]], writes=[B_mixT])
        stage_end()

    stage_begin()
    wo_b = st("wo_b", [128, KC, D], BF16)
    gate_bc = st("gate_bc", [128, D], F32)
    B_wo, B_gbc = Buf(), Buf()
    cx.dma(gate_bc[:], gateb_d[0:1, :].broadcast_to([128, D]), reads=[B_gateb], writes=[B_gbc])
    wof = [st(f"wof{i}", [128, KC, 256], F32) for i in range(2)]
    B_wof = [Buf(), Buf()]
    wout_v = w_out.rearrange("(kc p) n -> p kc n", p=128)
    for g8 in range(8):
        i = g8 % 2
        cx.dma(wof[i][:], wout_v[:, :, g8 * 256:(g8 + 1) * 256], writes=[B_wof[i]])
        cx.op(cx.pool, lambda i=i, g8=g8: nc.gpsimd.tensor_copy(out=wo_b[:, :, g8 * 256:(g8 + 1) * 256], in_=wof[i][:]),
              reads=[B_wof[i]], writes=[B_wo])
    mx = [st(f"mx{i}", [128, KC, 128], BF16) for i in range(2)]
    xo = [st(f"xo{i}", [128, D], F32) for i in range(2)]
    yo = [st(f"yo{i}", [128, D], F32) for i in range(2)]
    B_mx, B_xo, B_yo = [Buf(), Buf()], [Buf(), Buf()], [Buf(), Buf()]
    mix_v = mixT_d.rearrange("(kc p) t -> p kc t", p=128)
    B_out = Buf()
    for ti in range(NOWN // 128):
        s9 = ti % 2
        cx.dma(mx[s9][:], mix_v[:, :, ti * 128:(ti + 1) * 128], reads=[B_mixT], writes=[B_mx[s9]])
        cx.dma(xo[s9][:], xa[NCTX + ti * 128:NCTX + (ti + 1) * 128, :], writes=[B_xo[s9]])
        for fb in range(4):
            bk = (ti * 4 + fb) % 8
            cx.mm_group(banks[bk][:, 0:512], [(mx[s9][:, kc, :], wo_b[:, kc, fb * 512:(fb + 1) * 512]) for kc in range(KC)],
                        reads=[B_mx[s9], B_wo], writes=[bbuf[bk]])
            cx.op(cx.dve, lambda fb=fb, bk=bk: nc.vector.tensor_tensor(
                out=yo[s9][:, fb * 512:(fb + 1) * 512], in0=banks[bk][:, 0:512], in1=gate_bc[:, fb * 512:(fb + 1) * 512],
                op=ALU.mult), reads=[bbuf[bk], B_gbc], writes=[B_yo[s9]])
        cx.op(cx.pool, lambda: nc.gpsimd.tensor_tensor(out=yo[s9][:], in0=yo[s9][:], in1=xo[s9][:], op=ALU.add),
              reads=[B_yo[s9], B_xo[s9]], writes=[B_yo[s9]])
        cx.dma(out_d[ti * 128:(ti + 1) * 128, :], yo[s9][:], reads=[B_yo[s9]], writes=[B_out])
    stage_end()

    for t in cx.all_dma_toks[-40:]:
        cx.sp.wait(t)
    cx.es.close()
    outer.close()
    return nc


def _col_perm(half):
    M_QK_W, M_V_W, M_GATE_W = 1024, 2048, 32
    A_Q_W, A_V_W = 3072, 2048
    o_km = 0
    o_vm = o_km + M_QK_W
    o_gt = o_vm + M_V_W
    o_ckv = o_gt + M_GATE_W
    o_kr = o_ckv + KVR
    kvc = o_kr + AROPE
    o_qm = kvc
    o_om = o_qm + M_QK_W
    o_zm = o_om + M_V_W
    o_qa = o_zm + M_V_W
    o_za = o_qa + A_Q_W
    o_gm = o_za + A_V_W
    sw = np.array([ax * 32 + (1 - hf) * 16 + f for ax in range(2) for hf in range(2) for f in range(16)])
    perm = []
    perm += list(range(o_km, o_km + 1024))
    g = np.arange(32).reshape(4, 8)
    if half == 1:
        g = g[[2, 3, 0, 1]]
    perm += list(o_gt + g.reshape(-1))
    perm += list(range(o_ckv, o_ckv + KVR))
    perm += list(range(o_kr, o_kr + 64))
    perm += list(o_kr + sw)
    perm += list(range(o_vm, o_vm + 2048))
    perm += list(range(o_qm, o_qm + 1024))
    for j in range(16):
        perm += list(range(o_om + j * 128, o_om + (j + 1) * 128))
        perm += list(range(o_zm + j * 128, o_zm + (j + 1) * 128))
    for h in range(16):
        b0 = o_qa + h * 192
        perm += list(range(b0, b0 + 128))
        perm += list(range(b0 + 128, b0 + 192))
        perm += list(b0 + 128 + sw)
    perm += list(range(o_za, o_za + 2048))
    perm += list(range(o_gm, o_gm + 4096))
    perm = np.array(perm)
    assert perm.shape[0] == NCOLS
    return perm, sw


def _colT(v, n):
    return np.ascontiguousarray(np.asarray(v, np.float32).reshape(n, 128).T)


def _consts():
    ident = np.eye(128, dtype=np.float32)
    s = np.arange(128)[:, None]
    t = np.arange(128)[None, :]
    masks = np.zeros((128, 2, 128), np.float32)
    masks[:, 0, :] = np.where(s <= t, 0.0, -30000.0)
    masks[:, 1, :] = np.where(s >= t, 0.0, -30000.0)
    sel = np.zeros((32, 6, 32), np.float32)
    for h in range(8):
        sel[h, 0, h] = 1.0
        sel[16 + h, 0, 8 + h] = 1.0
        sel[8 + h, 1, h] = 1.0
        sel[24 + h, 2, 8 + h] = 1.0
        sel[8 + h, 3, 16 + h] = -1.0
        sel[24 + h, 4, 24 + h] = -1.0
    for r in range(16, 32):
        sel[r, 5, :] = 1.0
    selb = np.zeros((32, 2), np.float32)
    selb[16:, 0] = 1.0
    selb[:16, 1] = 1.0
    hmask = np.zeros((32, 16), np.float32)
    for j in range(16):
        hmask[j, j] = 1.0
        hmask[16 + j, j] = 1.0
    return ident, masks, sel, selb, hmask


def _rope_tables(pos):
    row = (pos // 64).astype(np.float32)
    col = (pos % 64).astype(np.float32)
    freqs = (np.float32(10000.0) ** (-np.arange(16, dtype=np.float32) / np.float32(16))).astype(np.float32)
    ang = np.stack([row[:, None] * freqs, col[:, None] * freqs], axis=1)
    cos = np.cos(ang).astype(np.float32)
    sin = np.sin(ang).astype(np.float32)
    cosT = np.ones((64, NALL), np.float32)
    sinT = np.zeros((64, NALL), np.float32)
    for ax in range(2):
        for hf in range(2):
            r0 = ax * 32 + hf * 16
            cosT[r0:r0 + 16, NCTX:] = cos[:, ax, :].T
            sgn = -1.0 if hf == 0 else 1.0
            sinT[r0:r0 + 16, NCTX:] = sgn * sin[:, ax, :].T
    return cosT, sinT


def make_in_maps(inp):
    x = np.asarray(inp["x"], np.float32)
    c = np.asarray(inp["c"], np.float32)
    ctx = np.asarray(inp["ctx"], np.float32)
    c_ctx = np.asarray(inp["c_ctx"], np.float32)
    ident, masks, sel, selb, hmask = _consts()
    w_in0 = np.asarray(inp["w_in"], np.float32)[0]
    conv_w = np.asarray(inp["conv_w"], np.float32)[0]
    conv_b = np.asarray(inp["conv_b"], np.float32)[0]
    gate_b = np.asarray(inp["gate_b"], np.float32)[0]
    qn = np.asarray(inp["q_norm_g"], np.float32)[0]
    kn = np.asarray(inp["k_norm_g"], np.float32)[0]
    shared = {}
    per_half = {}
    for half in range(2):
        perm, sw = _col_perm(half)
        d = {}
        d["w_in"] = np.ascontiguousarray(w_in0[:, perm])
        cw = conv_w if half == 0 else conv_w[::-1]
        d["conv_wq"] = np.ascontiguousarray(cw[:, :1024].reshape(5, MH, 128).transpose(2, 1, 0))
        d["conv_wk"] = np.ascontiguousarray(cw[:, 1024:].reshape(5, MH, 128).transpose(2, 1, 0))
        gb = gate_b.reshape(4, 8)
        if half == 1:
            gb = gb[[2, 3, 0, 1]]
        d["gate_b"] = np.ascontiguousarray(gb.reshape(32, 1))
        pos = np.arange(4096)
        if half == 1:
            pos = pos[::-1]
        d["cosT"], d["sinT"] = _rope_tables(pos)
        per_half[half] = d
    shared["ada_w"] = np.ascontiguousarray(np.asarray(inp["ada_w"], np.float32)[0])
    shared["ada_bT"] = _colT(np.asarray(inp["ada_b"])[0], 48)
    shared["norm_gT"] = _colT(np.asarray(inp["norm_g"])[0], KC)
    shared["conv_bq"] = np.ascontiguousarray(conv_b[:1024].reshape(MH, 128).T)
    shared["conv_bk"] = np.ascontiguousarray(conv_b[1024:].reshape(MH, 128).T)
    shared["mhgT"] = _colT(np.asarray(inp["mh_norm_g"])[0], KC)
    sw = _col_perm(0)[1]
    qg = np.zeros((128, 4), np.float32)
    qg[:, 0] = qn[:128]
    qg[:64, 1] = qn[128:]
    qg[:64, 2] = qn[128:][sw]
    kgm = np.zeros((128, 4), np.float32)
    kgm[:, 0] = kn[:128]
    kgm[:64, 1] = kn[128:]
    kgm[:64, 2] = kn[128:][sw]
    shared["qg"] = qg
    shared["kg"] = kgm
    shared["kvgT"] = _colT(np.asarray(inp["kv_norm_g"])[0], 4)
    shared["w_uk"] = np.ascontiguousarray(np.asarray(inp["w_uk"], np.float32)[0])
    shared["w_uv"] = np.ascontiguousarray(np.asarray(inp["w_uv"], np.float32)[0])
    shared["w_pm"] = np.ascontiguousarray(np.asarray(inp["w_proj_m"], np.float32)[0])
    shared["w_pa"] = np.ascontiguousarray(np.asarray(inp["w_proj_a"], np.float32)[0])
    shared["w_out"] = np.ascontiguousarray(np.asarray(inp["w_out"], np.float32)[0])
    shared["ident_f"] = ident
    shared["masks"] = masks
    shared["sel"] = sel
    shared["selb"] = selb
    shared["hmask"] = hmask
    eo = np.zeros((32, 128), np.float32)
    eo[16:] = 1.0
    shared["eones"] = eo
    maps = []
    for core in range(8):
        b, half = core // 2, core % 2
        m = dict(shared)
        m.update(per_half[half])
        if half == 0:
            xl = x[b]
            cl = ctx[b]
        else:
            xl = x[b][::-1]
            cl = ctx[b][::-1]
        m["xa"] = np.ascontiguousarray(np.concatenate([cl, xl], axis=0))
        cc = np.stack([c[b], c_ctx], axis=1)
        m["cT"] = np.ascontiguousarray(cc.reshape(KC, 128, 2).transpose(1, 0, 2))
        maps.append(m)
    return maps


_NC_CACHE = {}


def kernel(**inputs):
    maps = make_in_maps(inputs)
    if "nc" not in _NC_CACHE:
        _NC_CACHE["nc"] = build()
    nc = _NC_CACHE["nc"]
    res = run_bass_kernel_spmd(nc, maps, core_ids=list(range(8)))
    out = np.zeros((4, 4096, D), np.float32)
    for core in range(8):
        b, half = core // 2, core % 2
        o = np.asarray(res.results[core]["out"], np.float32)
        if half == 0:
            out[b, :2048] = o
        else:
            out[b, 2048:] = o[::-1]
    return out
```

```python
import contextlib
import numpy as np
import concourse.bass as bass
import concourse.mybir as mybir
from concourse.bass_utils import run_bass_kernel_spmd

F32 = mybir.dt.float32
BF16 = mybir.dt.bfloat16
AF = mybir.ActivationFunctionType
ALU = mybir.AluOpType
AX = mybir.AxisListType

D = 2048
KC = 16
NCTX = 256
NOWN = 2048
NALL = 4352
EPS = 1e-6
MH, MDQK, MDV = 8, 128, 256
AH, ANOPE, AROPE, ADV = 16, 128, 64, 128
KVR = 512
CH = 128
NCHUNK = NALL // CH

C_KM = 0
C_GT = 1024
C_CKV = 1056
C_KR = 1568
C_KRS = 1632
C_VM = 1696
C_QM = 3744
C_OZ = 4768
C_QA = 8864
C_ZA = 12960
C_GM = 15008
NCOLS = 19104

ATT_SHIFT = 8.0


class Tok:
    __slots__ = ("sem", "val", "eng")

    def __init__(self, sem, val, eng):
        self.sem, self.val, self.eng = sem, val, eng


class Buf:
    __slots__ = ("name", "w", "r", "wd")

    def __init__(self, name=""):
        self.name = name
        self.w = None
        self.wd = {}
        self.r = {}


class Eng:
    def __init__(self, ctx, name, eng):
        self.ctx, self.name, self.eng = ctx, name, eng
        self.sem = None
        self.n = 0
        self.waited = {}
        self.new_sem()

    def new_sem(self):
        self.sem = self.ctx.es.enter_context(self.ctx.nc.semaphore(f"s_{self.name}_{self.ctx.uid()}"))
        self.n = 0

    def wait(self, tok):
        if tok is None:
            return
        k = id(tok.sem)
        if self.waited.get(k, 0) >= tok.val:
            return
        self.eng.wait_ge(tok.sem, tok.val)
        self.waited[k] = tok.val


class Ctx:
    def __init__(self, nc):
        self.nc = nc
        self.es = contextlib.ExitStack()
        self._uid = 0
        self.pe = Eng(self, "pe", nc.tensor)
        self.act = Eng(self, "act", nc.scalar)
        self.dve = Eng(self, "dve", nc.vector)
        self.pool = Eng(self, "pool", nc.gpsimd)
        self.sp = Eng(self, "sp", nc.sync)
        self.dma_sems = [self.es.enter_context(nc.semaphore(f"dq{i}")) for i in range(32)]
        self.dma_cnt = [0] * 32
        self.dma_last = [None] * 32
        self.dma_i = 0
        self.all_dma_toks = []
        self.fence_scr = None

    def uid(self):
        self._uid += 1
        return self._uid

    def fresh_sems(self):
        for e in (self.pe, self.act, self.dve, self.pool):
            e.new_sem()

    def _deps(self, eng, reads, writes, is_dma=False):
        deps = []
        for b in reads:
            if b.w is not None:
                deps.append(b.w)
            deps.extend(b.wd.values())
        for b in writes:
            if b.w is not None and b.w.eng is not eng:
                deps.append(b.w)
            if not is_dma:
                deps.extend(b.wd.values())
            for t in b.r.values():
                if is_dma or t.eng is not eng:
                    deps.append(t)
        return deps

    def _wait_all(self, eng, deps, pe_skip_self=False):
        for d in deps:
            if pe_skip_self and d.eng is eng:
                continue
            eng.wait(d)

    def _record(self, tok, reads, writes, is_dma=False):
        for b in reads:
            k = id(tok.sem)
            b.r[k] = tok
        for b in writes:
            if is_dma:
                b.wd[id(tok.sem)] = tok
                b.w = None
            else:
                b.w = tok
                b.wd = {}
            b.r = {}

    def op(self, eng, fn, reads=(), writes=()):
        deps = self._deps(eng, reads, writes)
        self._wait_all(eng, deps, pe_skip_self=(eng is self.pe))
        inst = fn()
        if eng is not self.pe and self.fence_scr is not None:
            scr = self.fence_scr[eng.name]
            if eng is self.act:
                inst = self.nc.scalar.copy(out=scr[:, 1:2], in_=scr[:, 0:1])
            elif eng is self.dve:
                inst = self.nc.vector.tensor_copy(out=scr[:, 1:2], in_=scr[:, 0:1])
            else:
                inst = self.nc.gpsimd.tensor_copy(out=scr[:, 1:2], in_=scr[:, 0:1])
        eng.n += 1
        inst.then_inc(eng.sem, 1)
        tok = Tok(eng.sem, eng.n, eng)
        self._record(tok, reads, writes)
        return tok

    def mm_group(self, out_ap, pairs, reads, writes, transpose=False):
        eng = self.pe
        deps = self._deps(eng, reads, writes)
        self._wait_all(eng, deps, pe_skip_self=True)
        n = len(pairs)
        inst = None
        for i, (l, r) in enumerate(pairs):
            inst = self.nc.tensor.matmul(out_ap, lhsT=l, rhs=r, start=(i == 0), stop=(i == n - 1))
        eng.n += 1
        inst.then_inc(eng.sem, 1)
        tok = Tok(eng.sem, eng.n, eng)
        self._record(tok, reads, writes)
        return tok

    def mm1(self, out_ap, l, r, start, stop, reads, writes, signal):
        eng = self.pe
        deps = self._deps(eng, reads, writes)
        self._wait_all(eng, deps, pe_skip_self=True)
        inst = self.nc.tensor.matmul(out_ap, lhsT=l, rhs=r, start=start, stop=stop)
        if signal:
            eng.n += 1
            inst.then_inc(eng.sem, 1)
            tok = Tok(eng.sem, eng.n, eng)
            self._record(tok, reads, writes)
        else:
            tok = Tok(eng.sem, eng.n + 1, eng)
            for b in reads:
                b.r[id(tok.sem)] = tok
        return None

    def dma(self, out, in_, reads=(), writes=(), q=None):
        eng = q or self.sp
        deps = self._deps(eng, reads, writes, is_dma=True)
        self._wait_all(eng, deps)
        j = self.dma_i % len(self.dma_sems)
        self.dma_i += 1
        if self.dma_last[j] is not None:
            eng.wait(self.dma_last[j])
        inst = eng.eng.dma_start(out=out, in_=in_)
        self.dma_cnt[j] += 16
        inst.then_inc(self.dma_sems[j], 16)
        tok = Tok(self.dma_sems[j], self.dma_cnt[j], None)
        self.dma_last[j] = tok
        self._record(tok, reads, writes, is_dma=True)
        self.all_dma_toks.append(tok)
        return tok

    def sb(self, name, shape, dtype):
        return self.es.enter_context(self.nc.sbuf_tensor(name, list(shape), dtype))


def build(debug=()):
    nc = bass.Bass("TRN2", target_bir_lowering=False)
    outer = contextlib.ExitStack()
    outer.enter_context(nc.allow_low_precision("bf16 matmul operands, fp32 accumulation"))
    outer.enter_context(nc.allow_non_contiguous_dma("small strided tiles"))
    cx = Ctx(nc)
    dbg = set(debug)

    def din(name, shape, dt=F32):
        return nc.dram_tensor(name, list(shape), dt, kind="ExternalInput").ap()

    def dscr(name, shape, dt):
        kind = "ExternalOutput" if name in dbg else "Internal"
        return nc.dram_tensor(name, list(shape), dt, kind=kind).ap()

    xa = din("xa", [NALL, D])
    cT = din("cT", [128, KC, 2])
    ada_w = din("ada_w", [D, 3 * D])
    ada_bT = din("ada_bT", [128, 48])
    norm_gT = din("norm_gT", [128, KC])
    w_in = din("w_in", [D, NCOLS])
    conv_wk = din("conv_wk", [128, MH, 5])
    conv_wq = din("conv_wq", [128, MH, 5])
    conv_bk = din("conv_bk", [128, MH])
    conv_bq = din("conv_bq", [128, MH])
    gate_b = din("gate_b", [32, 1])
    mhgT = din("mhgT", [128, KC])
    qg = din("qg", [128, 4])
    kg = din("kg", [128, 4])
    kvgT = din("kvgT", [128, 4])
    w_uk = din("w_uk", [KVR, AH * ANOPE])
    w_uv = din("w_uv", [KVR, AH * ADV])
    w_pm = din("w_pm", [D, D])
    w_pa = din("w_pa", [D, D])
    w_out = din("w_out", [D, D])
    cosT = din("cosT", [64, NALL])
    sinT = din("sinT", [64, NALL])
    ident_f = din("ident_f", [128, 128])
    masks = din("masks", [128, 2, 128])
    sel = din("sel", [32, 6, 32])
    selb = din("selb", [32, 2])
    hmask = din("hmask", [32, 16])
    eones_d = din("eones", [32, 128])
    out_d = nc.dram_tensor("out", [NOWN, D], F32, kind="ExternalOutput").ap()

    hT_d = dscr("hT_d", [D, NALL], BF16)
    gateb_d = dscr("gateb_d", [1, D], F32)
    rawk_d = dscr("rawk_d", [MH, 128, NALL], F32)
    rawq_d = dscr("rawq_d", [MH, 128, NOWN + 2], F32)
    graw_d = dscr("graw_d", [32, NALL], F32)
    ckv_d = dscr("ckv_d", [KVR, NALL], F32)
    kr_d = dscr("kr_d", [128, NALL], F32)
    vm_d = dscr("vm_d", [NALL, MH * MDV], BF16)
    gmT_d = dscr("gmT_d", [D, NOWN], BF16)
    qaT_d = dscr("qaT_d", [AH, 192, NOWN], BF16)
    szaT_d = dscr("szaT_d", [D, NOWN], BF16)
    ggT_d = dscr("ggT_d", [2 * D, NOWN], BF16)
    kT_d = dscr("kT_d", [MH, 128, NALL], BF16)
    qT_d = dscr("qT_d", [MH, 128, NOWN], BF16)
    hA_d = dscr("hA_d", [NOWN, MH * MDV], F32)
    hmgT_d = dscr("hmgT_d", [D, NOWN], BF16)
    kaT_d = dscr("kaT_d", [AH, 192, NALL], BF16)
    va_d = dscr("va_d", [NALL, AH * ADV], BF16)
    ozT_d = dscr("ozT_d", [D, NOWN], BF16)
    mixT_d = dscr("mixT_d", [D, NOWN], BF16)

    es = cx.es
    banks = [es.enter_context(nc.psum_tensor(f"pb{i}", [128, 512], F32)) for i in range(8)]
    bbuf = [Buf(f"bank{i}") for i in range(8)]

    identf = cx.sb("identf", [128, 128], F32)
    identb = cx.sb("identb", [128, 128], BF16)
    onesb = cx.sb("onesb", [128, 128], BF16)
    modc = cx.sb("modc", [128, 48, 2], F32)
    gsc = cx.sb("gsc", [128, KC, 2], F32)
    B_const = Buf("const")
    B_mod = Buf("mod")
    fscr = {n_: cx.sb(f"fscr_{n_}", [128, 2], F32) for n_ in ("act", "dve", "pool")}
    nc.gpsimd.memset(fscr["pool"][:], 0.0)
    nc.gpsimd.memset(fscr["dve"][:], 0.0)
    nc.gpsimd.memset(fscr["act"][:], 0.0).then_inc(cx.pool.sem, 1)
    cx.pool.n += 1
    for e_ in (cx.act, cx.dve, cx.pool):
        e_.wait(Tok(cx.pool.sem, cx.pool.n, cx.pool))
    cx.fence_scr = fscr

    cx.dma(identf[:], ident_f[:, :], writes=[B_const])
    cx.op(cx.dve, lambda: nc.vector.tensor_copy(out=identb[:], in_=identf[:]), reads=[B_const], writes=[B_const])
    cx.op(cx.pool, lambda: nc.gpsimd.memset(onesb[:], 1.0), writes=[B_const])

    stage_es = contextlib.ExitStack()

    def stage_begin():
        nonlocal stage_es
        stage_es = contextlib.ExitStack()
        return stage_es

    def st(name, shape, dtype):
        return stage_es.enter_context(nc.sbuf_tensor(name + f"_{cx.uid()}", list(shape), dtype))

    def stage_end():
        toks = []
        for e in (cx.pe, cx.act, cx.dve, cx.pool):
            if e.n > 0:
                toks.append(Tok(e.sem, e.n, e))
        for j, t in enumerate(cx.dma_last):
            if t is not None:
                toks.append(t)
        for e in (cx.pe, cx.act, cx.dve, cx.pool, cx.sp):
            for t in toks:
                e.wait(t)
        stage_es.close()

    stage_begin()
    ct_sb = st("ct", [128, KC, 2], F32)
    sc_sb = st("sc", [128, KC, 2], F32)
    adab = st("adab", [128, 48], F32)
    ngt = st("ngt", [128, KC], F32)
    B_ct = Buf("ct")
    cx.dma(ct_sb[:], cT[:, :, :], writes=[B_ct])
    cx.dma(adab[:], ada_bT[:, :], writes=[B_ct])
    cx.dma(ngt[:], norm_gT[:, :], writes=[B_ct])
    cx.op(cx.act, lambda: nc.scalar.activation(out=sc_sb[:], in_=ct_sb[:], func=AF.Sigmoid), reads=[B_ct], writes=[B_ct])
    cx.op(cx.dve, lambda: nc.vector.tensor_tensor(out=sc_sb[:], in0=sc_sb[:], in1=ct_sb[:], op=ALU.mult),
          reads=[B_ct], writes=[B_ct])
    awt = [st(f"awt{i}", [128, KC, 512], F32) for i in range(2)]
    B_awt = [Buf("awt0"), Buf("awt1")]
    ada_v = ada_w.rearrange("(kc p) n -> p kc n", p=128)
    for cb in range(12):
        sl = cb % 2
        cx.dma(awt[sl][:], ada_v[:, :, cb * 512:(cb + 1) * 512], writes=[B_awt[sl]])
        for sub in range(4):
            j = cb * 4 + sub
            bk = j % 2
            pairs = [(awt[sl][:, kc, sub * 128:(sub + 1) * 128], sc_sb[:, kc, :]) for kc in range(KC)]
            cx.mm_group(banks[bk][:, 0:2], pairs, reads=[B_awt[sl], B_ct], writes=[bbuf[bk]])
            cx.op(cx.dve, lambda j=j, bk=bk: nc.vector.tensor_scalar(
                out=modc[:, j, :], in0=banks[bk][:, 0:2], scalar1=adab[:, j:j + 1], scalar2=None, op0=ALU.add),
                reads=[bbuf[bk], B_ct], writes=[B_mod])
    for i in range(2):
        cx.op(cx.dve, lambda i=i: nc.vector.scalar_tensor_tensor(
            out=gsc[:, :, i], in0=modc[:, 16:32, i], scalar=1.0, in1=ngt[:], op0=ALU.add, op1=ALU.mult),
            reads=[B_mod, B_ct], writes=[B_mod])
    gcol = st("gcol", [128, KC], F32)
    grow = st("grow", [16, 128], F32)
    cx.op(cx.dve, lambda: nc.vector.tensor_copy(out=gcol[:], in_=modc[:, 32:48, 0]), reads=[B_mod], writes=[B_ct])
    cx.op(cx.pe, lambda: nc.tensor.transpose(banks[2][0:16, 0:128], gcol[:], identf[:]), reads=[B_ct, B_const],
          writes=[bbuf[2]])
    cx.op(cx.dve, lambda: nc.vector.tensor_copy(out=grow[:], in_=banks[2][0:16, 0:128]), reads=[bbuf[2]], writes=[B_ct])
    B_gateb = Buf("gateb_d")
    cx.dma(gateb_d.rearrange("o (j p) -> (o j) p", p=128), grow[:], reads=[B_ct], writes=[B_gateb])
    stage_end()

    stage_begin()
    B_hT = Buf("hT_d")
    xt = [st(f"xt{i}", [128, D], F32) for i in range(2)]
    xs = [st(f"xs{i}", [128, D], F32) for i in range(2)]
    sq = st("sqjunk", [128, D], F32)
    hts = [st(f"hts{i}", [128, KC, 128], BF16) for i in range(2)]
    stat = [st(f"stat{i}", [128, 4], F32) for i in range(2)]
    B_xt = [Buf(), Buf()]
    B_xs = [Buf(), Buf()]
    B_sq = Buf()
    B_hts = [Buf(), Buf()]
    B_stat = [Buf(), Buf()]
    ntile = NALL // 128
    cx.dma(xt[0][:], xa[0:128, :], writes=[B_xt[0]])
    for ti in range(ntile):
        sl = ti % 2
        if ti + 1 < ntile:
            cx.dma(xt[1 - sl][:], xa[(ti + 1) * 128:(ti + 2) * 128, :], writes=[B_xt[1 - sl]])
        mi = 1 if ti < 2 else 0
        cx.op(cx.act, lambda sl=sl: nc.scalar.activation(out=sq[:], in_=xt[sl][:], func=AF.Square,
                                                          accum_out=stat[sl][:, 0:1]),
              reads=[B_xt[sl]], writes=[B_sq, B_stat[sl]])
        cx.op(cx.act, lambda sl=sl: nc.scalar.activation(out=stat[sl][:, 1:2], in_=stat[sl][:, 0:1], func=AF.Sqrt,
                                                          scale=1.0 / D, bias=EPS),
              reads=[B_stat[sl]], writes=[B_stat[sl]])
        cx.op(cx.dve, lambda sl=sl: nc.vector.reciprocal(out=stat[sl][:, 2:3], in_=stat[sl][:, 1:2]),
              reads=[B_stat[sl]], writes=[B_stat[sl]])
        cx.op(cx.dve, lambda sl=sl: nc.vector.tensor_scalar(out=xs[sl][:], in0=xt[sl][:], scalar1=stat[sl][:, 2:3],
                                                             scalar2=None, op0=ALU.mult),
              reads=[B_xt[sl], B_stat[sl]], writes=[B_xs[sl]])
        for g in range(4):
            bk = (ti * 4 + g) % 8
            for q in range(4):
                kc = g * 4 + q
                cx.op(cx.pe, lambda kc=kc, bk=bk, q=q, sl=sl: nc.tensor.transpose(
                    banks[bk][:, q * 128:(q + 1) * 128], xs[sl][:, kc * 128:(kc + 1) * 128], identf[:]),
                    reads=[B_xs[sl], B_const], writes=[bbuf[bk]])
            for q in range(4):
                kc = g * 4 + q
                e = cx.act if q % 2 == 0 else cx.dve
                if e is cx.act:
                    cx.op(e, lambda kc=kc, bk=bk, q=q, sl=sl, mi=mi: nc.scalar.activation(
                        out=hts[sl][:, kc, :], in_=banks[bk][:, q * 128:(q + 1) * 128], func=AF.Identity,
                        scale=gsc[:, kc, mi:mi + 1], bias=modc[:, kc, mi:mi + 1]),
                        reads=[bbuf[bk], B_mod], writes=[B_hts[sl]])
                else:
                    cx.op(e, lambda kc=kc, bk=bk, q=q, sl=sl, mi=mi: nc.vector.tensor_scalar(
                        out=hts[sl][:, kc, :], in0=banks[bk][:, q * 128:(q + 1) * 128],
                        scalar1=gsc[:, kc, mi:mi + 1], scalar2=modc[:, kc, mi:mi + 1], op0=ALU.mult, op1=ALU.add),
                        reads=[bbuf[bk], B_mod], writes=[B_hts[sl]])
        cx.dma(hT_d.rearrange("(kc p) a -> p kc a", p=128)[:, :, ti * 128:(ti + 1) * 128], hts[sl][:],
               reads=[B_hts[sl]], writes=[B_hT])
    stage_end()


    B_rawk, B_rawq, B_graw, B_ckv, B_kr, B_vm = Buf(), Buf(), Buf(), Buf(), Buf(), Buf()
    B_gmT, B_qaT, B_szaT, B_ggT = Buf(), Buf(), Buf(), Buf()
    hT_v = hT_d.rearrange("(kc p) a -> p kc a", p=128)
    w_v = w_in.rearrange("(kc p) n -> p kc n", p=128)
    for ps in range(2):
        stage_begin()
        if ps == 0:
            A0, NT = 0, NOWN + NCTX + 2
        else:
            A0, NT = NOWN + NCTX, NOWN
        hres = st("hres", [128, KC, NT], BF16)
        B_h = Buf()
        for kc in range(KC):
            cx.dma(hres[:, kc, :], hT_v[:, kc, A0:A0 + NT], reads=[B_hT], writes=[B_h])
        wf = [st(f"wf{i}", [128, KC, 256], F32) for i in range(3)]
        wb = [st(f"wb{i}", [128, KC, 256], BF16) for i in range(3)]
        B_wf = [Buf(), Buf(), Buf()]
        B_wb = [Buf(), Buf(), Buf()]
        tf = [st(f"tf{i}", [128, 512], F32) for i in range(6)]
        tb = [st(f"tb{i}", [128, 512], BF16) for i in range(4)]
        B_tf = [Buf() for _ in range(6)]
        B_tb = [Buf() for _ in range(4)]
        rr = {"tf": 0, "tb": 0, "bk": 0, "w": 0, "ev": 0}

        def nxt(kind, n):
            i = rr[kind] % n
            rr[kind] += 1
            return i

        if ps == 0:
            cos_sb = st("cos_sb", [64, NOWN], F32)
            sin_sb = st("sin_sb", [64, NOWN], F32)
            qg_sb = st("qg_sb", [128, 4], F32)
            B_tab = Buf()
            cx.dma(cos_sb[:], cosT[:, NCTX:NCTX + NOWN], writes=[B_tab])
            cx.dma(sin_sb[:], sinT[:, NCTX:NCTX + NOWN], writes=[B_tab])
            cx.dma(qg_sb[:], qg[:, :], writes=[B_tab])
            cx.op(cx.dve, lambda: nc.vector.tensor_scalar(out=cos_sb[:], in0=cos_sb[:], scalar1=qg_sb[0:64, 1:2],
                                                          scalar2=None, op0=ALU.mult), reads=[B_tab], writes=[B_tab])
            cx.op(cx.dve, lambda: nc.vector.tensor_scalar(out=sin_sb[:], in0=sin_sb[:], scalar1=qg_sb[0:64, 2:3],
                                                          scalar2=None, op0=ALU.mult), reads=[B_tab], writes=[B_tab])

        pending = {}

        glist = [(C_KM + g * 256, 256) for g in range(4)] + [(C_GT, 32)] + [(C_CKV + g * 256, 256) for g in range(2)]
        glist += [(C_KR, 128)] + [(C_VM + g * 256, 256) for g in range(8)]
        if ps == 0:
            glist += [(C_QM + g * 256, 256) for g in range(4)] + [(C_OZ + j * 256, 256) for j in range(16)]
            glist += [(C_QA + h_ * 256, 256) for h_ in range(AH)]
            glist += [(C_ZA + g * 256 if g < 8 else C_GM + (g - 8) * 256, 256) for g in range(24)]
        gstate = {"issued": 0, "used": 0}

        def issue_next():
            gi = gstate["issued"]
            if gi >= len(glist):
                return
            c0, width = glist[gi]
            i = gi % 3
            cx.dma(wf[i][:, :, 0:width], w_v[:, :, c0:c0 + width], writes=[B_wf[i]])
            cx.op(cx.pool, lambda: nc.gpsimd.tensor_copy(out=wb[i][:, :, 0:width], in_=wf[i][:, :, 0:width]),
                  reads=[B_wf[i]], writes=[B_wb[i]])
            gstate["issued"] += 1

        def load_group(c0, width):
            gu = gstate["used"]
            assert glist[gu] == (c0, width), (glist[gu], c0, width)
            while gstate["issued"] < min(gu + 2, len(glist)):
                issue_next()
            gstate["used"] += 1
            return gu % 3

        def fm(wi, off, M, a0, n):
            bk = nxt("bk", 8)
            pairs = [(wb[wi][:, kc, off:off + M], hres[:, kc, a0 - A0:a0 - A0 + n]) for kc in range(KC)]
            cx.mm_group(banks[bk][0:M, 0:n], pairs, reads=[B_wb[wi], B_h], writes=[bbuf[bk]])
            return bk

        def dump_raw(bk, M, n, dst_ap, dst_buf):
            i = nxt("tf", 6)
            if nxt("ev", 2) == 0:
                cx.op(cx.act, lambda: nc.scalar.copy(out=tf[i][0:M, 0:n], in_=banks[bk][0:M, 0:n]),
                      reads=[bbuf[bk]], writes=[B_tf[i]])
            else:
                cx.op(cx.dve, lambda: nc.vector.tensor_copy(out=tf[i][0:M, 0:n], in_=banks[bk][0:M, 0:n]),
                      reads=[bbuf[bk]], writes=[B_tf[i]])
            cx.dma(dst_ap, tf[i][0:M, 0:n], reads=[B_tf[i]], writes=[dst_buf])

        def blocks(a_lo, a_hi):
            a = a_lo
            while a < a_hi:
                n = min(512, a_hi - a)
                yield a, n
                a += n

        AKV0, AKV1 = (0, NCTX + NOWN) if ps == 0 else (NCTX + NOWN, NALL)
        for g in range(4):
            wi = load_group(C_KM + g * 256, 256)
            for sub in range(2):
                hd = g * 2 + sub
                for a0, n in blocks(AKV0, AKV1):
                    bk = fm(wi, sub * 128, 128, a0, n)
                    dump_raw(bk, 128, n, rawk_d[hd, :, a0:a0 + n], B_rawk)
        wi = load_group(C_GT, 32)
        for a0, n in blocks(AKV0, AKV1):
            bk = fm(wi, 0, 32, a0, n)
            dump_raw(bk, 32, n, graw_d[:, a0:a0 + n], B_graw)
        for g in range(2):
            wi = load_group(C_CKV + g * 256, 256)
            for sub in range(2):
                r0 = (g * 2 + sub) * 128
                for a0, n in blocks(AKV0, AKV1):
                    bk = fm(wi, sub * 128, 128, a0, n)
                    dump_raw(bk, 128, n, ckv_d[r0:r0 + 128, a0:a0 + n], B_ckv)
        wi = load_group(C_KR, 128)
        for a0, n in blocks(AKV0, AKV1):
            bk = fm(wi, 0, 128, a0, n)
            dump_raw(bk, 128, n, kr_d[:, a0:a0 + n], B_kr)
        for g in range(8):
            wi = load_group(C_VM + g * 256, 256)
            for a0 in range(AKV0, AKV1, 128):
                bk = nxt("bk", 8)
                pairs = [(hres[:, kc, a0 - A0:a0 - A0 + 128], wb[wi][:, kc, :]) for kc in range(KC)]
                cx.mm_group(banks[bk][:, 0:256], pairs, reads=[B_wb[wi], B_h], writes=[bbuf[bk]])
                i = nxt("tb", 4)
                if nxt("ev", 2) == 0:
                    cx.op(cx.act, lambda: nc.scalar.copy(out=tb[i][:, 0:256], in_=banks[bk][:, 0:256]),
                          reads=[bbuf[bk]], writes=[B_tb[i]])
                else:
                    cx.op(cx.dve, lambda: nc.vector.tensor_copy(out=tb[i][:, 0:256], in_=banks[bk][:, 0:256]),
                          reads=[bbuf[bk]], writes=[B_tb[i]])
                cx.dma(vm_d[a0:a0 + 128, g * 256:(g + 1) * 256], tb[i][:, 0:256], reads=[B_tb[i]], writes=[B_vm])
        if ps == 0:
            Q0, Q1 = NCTX, NCTX + NOWN
            for g in range(4):
                wi = load_group(C_QM + g * 256, 256)
                for sub in range(2):
                    hd = g * 2 + sub
                    for a0, n in blocks(Q0, Q1 + 2):
                        bk = fm(wi, sub * 128, 128, a0, n)
                        dump_raw(bk, 128, n, rawq_d[hd, :, a0 - Q0:a0 - Q0 + n], B_rawq)
            for j in range(16):
                wi = load_group(C_OZ + j * 256, 256)
                for a0, n in blocks(Q0, Q1):
                    b1 = fm(wi, 0, 128, a0, n)
                    b2 = fm(wi, 128, 128, a0, n)
                    i1, i2 = nxt("tf", 6), nxt("tf", 6)
                    cx.op(cx.act, lambda: nc.scalar.activation(out=tf[i1][:, 0:n], in_=banks[b1][:, 0:n], func=AF.Sigmoid),
                          reads=[bbuf[b1]], writes=[B_tf[i1]])
                    cx.op(cx.act, lambda: nc.scalar.activation(out=tf[i2][:, 0:n], in_=banks[b2][:, 0:n], func=AF.Sigmoid),
                          reads=[bbuf[b2]], writes=[B_tf[i2]])
                    cx.op(cx.dve, lambda: nc.vector.tensor_tensor(out=tf[i2][:, 0:n], in0=banks[b2][:, 0:n],
                                                                  in1=tf[i2][:, 0:n], op=ALU.mult),
                          reads=[bbuf[b2], B_tf[i2]], writes=[B_tf[i2]])
                    ib = nxt("tb", 4)
                    cx.op(cx.dve, lambda: nc.vector.tensor_tensor(out=tb[ib][:, 0:n], in0=tf[i1][:, 0:n],
                                                                  in1=tf[i2][:, 0:n], op=ALU.mult),
                          reads=[B_tf[i1], B_tf[i2]], writes=[B_tb[ib]])
                    cx.dma(gmT_d[j * 128:(j + 1) * 128, a0 - Q0:a0 - Q0 + n], tb[ib][:, 0:n], reads=[B_tb[ib]],
                           writes=[B_gmT])
            for hd in range(AH):
                wi = load_group(C_QA + hd * 256, 256)
                for a0, n in blocks(Q0, Q1):
                    t0 = a0 - Q0
                    bn = fm(wi, 0, 128, a0, n)
                    br = fm(wi, 128, 64, a0, n)
                    bs = fm(wi, 192, 64, a0, n)
                    s1, s2 = nxt("tb", 4), nxt("tb", 4)
                    cx.op(cx.act, lambda: nc.scalar.activation(out=tb[s1][:, 0:n], in_=banks[bn][:, 0:n], func=AF.Square),
                          reads=[bbuf[bn]], writes=[B_tb[s1]])
                    cx.op(cx.act, lambda: nc.scalar.activation(out=tb[s2][0:64, 0:n], in_=banks[br][0:64, 0:n],
                                                               func=AF.Square),
                          reads=[bbuf[br]], writes=[B_tb[s2]])
                    bq = nxt("bk", 8)
                    cx.mm_group(banks[bq][:, 0:n], [(onesb[:, :], tb[s1][:, 0:n]), (onesb[0:64, :], tb[s2][0:64, 0:n])],
                                reads=[B_tb[s1], B_tb[s2], B_const], writes=[bbuf[bq]])
                    ir = nxt("tf", 6)
                    cx.op(cx.act, lambda: nc.scalar.activation(out=tf[ir][:, 0:n], in_=banks[bq][:, 0:n], func=AF.Ln,
                                                               scale=1.0 / 192.0, bias=EPS),
                          reads=[bbuf[bq]], writes=[B_tf[ir]])
                    cx.op(cx.act, lambda: nc.scalar.activation(out=tf[ir][:, 0:n], in_=tf[ir][:, 0:n], func=AF.Exp,
                                                               scale=-0.5),
                          reads=[B_tf[ir]], writes=[B_tf[ir]])
                    o1 = nxt("tb", 4)
                    cx.op(cx.dve, lambda: nc.vector.scalar_tensor_tensor(
                        out=tb[o1][:, 0:n], in0=banks[bn][:, 0:n], scalar=qg_sb[:, 0:1], in1=tf[ir][:, 0:n],
                        op0=ALU.mult, op1=ALU.mult), reads=[bbuf[bn], B_tf[ir], B_tab], writes=[B_tb[o1]])
                    cx.dma(qaT_d[hd, 0:128, t0:t0 + n], tb[o1][:, 0:n], reads=[B_tb[o1]], writes=[B_qaT])
                    ia, ib2 = nxt("tf", 6), nxt("tf", 6)
                    cx.op(cx.dve, lambda: nc.vector.tensor_tensor(out=tf[ia][0:64, 0:n], in0=banks[br][0:64, 0:n],
                                                                  in1=cos_sb[:, t0:t0 + n], op=ALU.mult),
                          reads=[bbuf[br], B_tab], writes=[B_tf[ia]])
                    cx.op(cx.dve, lambda: nc.vector.tensor_tensor(out=tf[ib2][0:64, 0:n], in0=banks[bs][0:64, 0:n],
                                                                  in1=sin_sb[:, t0:t0 + n], op=ALU.mult),
                          reads=[bbuf[bs], B_tab], writes=[B_tf[ib2]])
                    cx.op(cx.pool, lambda: nc.gpsimd.tensor_tensor(out=tf[ia][0:64, 0:n], in0=tf[ia][0:64, 0:n],
                                                                   in1=tf[ib2][0:64, 0:n], op=ALU.add),
                          reads=[B_tf[ia], B_tf[ib2]], writes=[B_tf[ia]])
                    o2 = nxt("tb", 4)
                    cx.op(cx.pool, lambda: nc.gpsimd.tensor_tensor(out=tb[o2][0:64, 0:n], in0=tf[ia][0:64, 0:n],
                                                                   in1=tf[ir][0:64, 0:n], op=ALU.mult),
                          reads=[B_tf[ia], B_tf[ir]], writes=[B_tb[o2]])
                    cx.dma(qaT_d[hd, 128:192, t0:t0 + n], tb[o2][0:64, 0:n], reads=[B_tb[o2]], writes=[B_qaT])
            for g in range(8 + 16):
                is_za = g < 8
                c0 = C_ZA + g * 256 if is_za else C_GM + (g - 8) * 256
                wi = load_group(c0, 256)
                for sub in range(2):
                    for a0, n in blocks(Q0, Q1):
                        t0 = a0 - Q0
                        bk = fm(wi, sub * 128, 128, a0, n)
                        ib = nxt("tb", 4)
                        if is_za:
                            i1 = nxt("tf", 6)
                            cx.op(cx.act, lambda: nc.scalar.activation(out=tf[i1][:, 0:n], in_=banks[bk][:, 0:n],
                                                                       func=AF.Sigmoid),
                                  reads=[bbuf[bk]], writes=[B_tf[i1]])
                            cx.op(cx.dve, lambda: nc.vector.tensor_tensor(out=tb[ib][:, 0:n], in0=banks[bk][:, 0:n],
                                                                          in1=tf[i1][:, 0:n], op=ALU.mult),
                                  reads=[bbuf[bk], B_tf[i1]], writes=[B_tb[ib]])
                            r0 = (g * 2 + sub) * 128
                            cx.dma(szaT_d[r0:r0 + 128, t0:t0 + n], tb[ib][:, 0:n], reads=[B_tb[ib]], writes=[B_szaT])
                        else:
                            cx.op(cx.act, lambda: nc.scalar.activation(out=tb[ib][:, 0:n], in_=banks[bk][:, 0:n],
                                                                       func=AF.Sigmoid),
                                  reads=[bbuf[bk]], writes=[B_tb[ib]])
                            r0 = ((g - 8) * 2 + sub) * 128
                            cx.dma(ggT_d[r0:r0 + 128, t0:t0 + n], tb[ib][:, 0:n], reads=[B_tb[ib]], writes=[B_ggT])
        stage_end()


    def barrier():
        toks = []
        for e in (cx.pe, cx.act, cx.dve, cx.pool):
            if e.n > 0:
                toks.append(Tok(e.sem, e.n, e))
        for t in cx.dma_last:
            if t is not None:
                toks.append(t)
        for e in (cx.pe, cx.act, cx.dve, cx.pool, cx.sp):
            for t in toks:
                e.wait(t)

    stage_begin()
    kTres = st("kTres", [128, MH, NALL], BF16)
    qTres = st("qTres", [128, MH, NOWN], BF16)
    LX = st("LX", [32, NALL], F32)
    YA = st("YA", [32, NALL], F32)
    B_kT, B_qT, B_LX, B_YA = Buf(), Buf(), Buf(), Buf()
    hmask_sb = st("hmask_sb", [32, 16], F32)
    eones = st("eones_sb", [32, 128], F32)
    maskf = st("maskf", [128, 2, 128], F32)
    mhg_sb = st("mhg_sb", [128, KC], F32)
    B_c5 = Buf()
    cx.dma(hmask_sb[:], hmask[:, :], writes=[B_c5])
    cx.dma(eones[:], eones_d[:, :], writes=[B_c5])
    cx.dma(maskf[:], masks[:, :, :], writes=[B_c5])
    cx.dma(mhg_sb[:], mhgT[:, :], writes=[B_c5])
    tmp_es = contextlib.ExitStack()

    def tt(name, shape, dtype):
        return tmp_es.enter_context(nc.sbuf_tensor(name + f"_{cx.uid()}", list(shape), dtype))

    craw = [tt(f"craw{i}", [128, 4 + NCTX], F32) for i in range(2)]
    lraw = [tt(f"lraw{i}", [128, 4 + 4096], F32) for i in range(2)]
    acc = tt("acc", [128, NALL], F32)
    sgm = tt("sgm", [128, NALL], F32)
    cwk = tt("cwk", [128, MH, 5], F32)
    cwq = tt("cwq", [128, MH, 5], F32)
    cbk = tt("cbk", [128, MH], F32)
    cbq = tt("cbq", [128, MH], F32)
    B_raw = [Buf(), Buf()]
    B_acc, B_sgm, B_cw = Buf(), Buf(), Buf()
    cx.dma(cwk[:], conv_wk[:, :, :], writes=[B_cw])
    cx.dma(cwq[:], conv_wq[:, :, :], writes=[B_cw])
    cx.dma(cbk[:], conv_bk[:, :], writes=[B_cw])
    cx.dma(cbq[:], conv_bq[:, :], writes=[B_cw])
    for i in range(2):
        cx.op(cx.pool, lambda i=i: nc.gpsimd.memset(craw[i][:], 0.0), writes=[B_raw[i]])
        cx.op(cx.pool, lambda i=i: nc.gpsimd.memset(lraw[i][:], 0.0), writes=[B_raw[i]])

    def conv_seg(src, dst0, n, wt, bt, hd, sl):
        cx.op(cx.dve, lambda: nc.vector.tensor_scalar(out=acc[:, dst0:dst0 + n], in0=src[:, 0:n], scalar1=wt[:, hd, 0:1],
                                                      scalar2=bt[:, hd:hd + 1], op0=ALU.mult, op1=ALU.add),
              reads=[B_raw[sl], B_cw], writes=[B_acc])
        for j in range(1, 5):
            cx.op(cx.dve, lambda j=j: nc.vector.scalar_tensor_tensor(
                out=acc[:, dst0:dst0 + n], in0=src[:, j:j + n], scalar=wt[:, hd, j:j + 1], in1=acc[:, dst0:dst0 + n],
                op0=ALU.mult, op1=ALU.add), reads=[B_raw[sl], B_cw, B_acc], writes=[B_acc])

    it = 0
    for hd in range(MH):
        sl = it % 2
        it += 1
        cx.dma(craw[sl][:, 2:2 + NCTX], rawk_d[hd, :, 0:NCTX], reads=[B_rawk], writes=[B_raw[sl]])
        cx.dma(lraw[sl][:, 2:2 + 4096], rawk_d[hd, :, NCTX:NALL], reads=[B_rawk], writes=[B_raw[sl]])
        conv_seg(craw[sl], 0, NCTX, cwk, cbk, hd, sl)
        conv_seg(lraw[sl], NCTX, 4096, cwk, cbk, hd, sl)
        cx.op(cx.act, lambda: nc.scalar.activation(out=sgm[:], in_=acc[:], func=AF.Sigmoid), reads=[B_acc], writes=[B_sgm])
        cx.op(cx.pool, lambda hd=hd: nc.gpsimd.tensor_tensor(out=kTres[:, hd, :], in0=acc[:], in1=sgm[:], op=ALU.mult),
              reads=[B_acc, B_sgm], writes=[B_kT])
    for hd in range(MH):
        sl = it % 2
        it += 1
        cx.dma(lraw[sl][:, 2:2 + NOWN + 2], rawq_d[hd, :, :], reads=[B_rawq], writes=[B_raw[sl]])
        conv_seg(lraw[sl], 0, NOWN, cwq, cbq, hd, sl)
        cx.op(cx.act, lambda: nc.scalar.activation(out=sgm[:, 0:NOWN], in_=acc[:, 0:NOWN], func=AF.Sigmoid),
              reads=[B_acc], writes=[B_sgm])
        cx.op(cx.dve, lambda hd=hd: nc.vector.scalar_tensor_tensor(
            out=qTres[:, hd, :], in0=acc[:, 0:NOWN], scalar=float(MDQK ** -0.5), in1=sgm[:, 0:NOWN],
            op0=ALU.mult, op1=ALU.mult), reads=[B_acc, B_sgm], writes=[B_qT])
    barrier()
    tmp_es.close()
    tmp_es = contextlib.ExitStack()
    g_sb = tt("g_sb", [32, NALL], F32)
    sp_sb = tt("sp_sb", [32, NALL], F32)
    cs_sb = tt("cs_sb", [32, NALL], F32)
    rs_sb = tt("rs_sb", [32, NALL], F32)
    ones32 = tt("ones32", [32, 128], F32)
    gb_sb = tt("gb_sb", [32, 1], F32)
    sel_sb = tt("sel_sb", [32, 6, 32], F32)
    selb_sb = tt("selb_sb", [32, 2], F32)
    B_g, B_sp, B_cs, B_rs, B_gc = Buf(), Buf(), Buf(), Buf(), Buf()
    cx.dma(g_sb[:], graw_d[:, :], reads=[B_graw], writes=[B_g])
    cx.dma(gb_sb[:], gate_b[:, :], writes=[B_gc])
    cx.dma(sel_sb[:], sel[:, :, :], writes=[B_gc])
    cx.dma(selb_sb[:], selb[:, :], writes=[B_gc])
    cx.op(cx.pool, lambda: nc.gpsimd.memset(ones32[:], 1.0), writes=[B_gc])
    cx.op(cx.dve, lambda: nc.vector.tensor_scalar(out=g_sb[:], in0=g_sb[:], scalar1=gb_sb[:, 0:1], scalar2=None,
                                                  op0=ALU.add), reads=[B_g, B_gc], writes=[B_g])
    cx.op(cx.act, lambda: nc.scalar.activation(out=sp_sb[:], in_=g_sb[:], func=AF.Exp, scale=-1.0),
          reads=[B_g], writes=[B_sp])
    cx.op(cx.act, lambda: nc.scalar.activation(out=sp_sb[:], in_=sp_sb[:], func=AF.Ln, scale=1.0, bias=1.0),
          reads=[B_sp], writes=[B_sp])
    for c in range(NCHUNK):
        cx.op(cx.dve, lambda c=c: nc.vector.tensor_tensor_scan(
            out=cs_sb[:, c * CH:(c + 1) * CH], data0=ones32[:], data1=sp_sb[:, c * CH:(c + 1) * CH], initial=0.0,
            op0=ALU.mult, op1=ALU.add), reads=[B_sp, B_gc], writes=[B_cs])
    cx.op(cx.dve, lambda: nc.vector.tensor_tensor(out=rs_sb[:], in0=sp_sb[:], in1=cs_sb[:], op=ALU.subtract),
          reads=[B_sp, B_cs], writes=[B_rs])
    cs3 = cs_sb[:].rearrange("p (c t) -> p c t", t=CH)
    rs3 = rs_sb[:].rearrange("p (c t) -> p c t", t=CH)
    cx.op(cx.dve, lambda: nc.vector.tensor_tensor(out=rs3, in0=rs3, in1=cs3[:, :, CH - 1:CH].broadcast_to([32, NCHUNK, CH]),
                                                  op=ALU.add), reads=[B_rs, B_cs], writes=[B_rs])
    a = 0
    while a < NALL:
        n = min(512, NALL - a)
        bk = (a // 512) % 2
        cx.mm_group(banks[bk][0:32, 0:n], [(sel_sb[:, 0, :], g_sb[:, a:a + n]), (sel_sb[:, 1, :], cs_sb[:, a:a + n]),
                                           (sel_sb[:, 2, :], rs_sb[:, a:a + n])],
                    reads=[B_g, B_cs, B_rs, B_gc], writes=[bbuf[bk]])
        cx.op(cx.act, lambda a=a, n=n, bk=bk: nc.scalar.activation(out=LX[:, a:a + n], in_=banks[bk][0:32, 0:n],
                                                                    func=AF.Identity, bias=selb_sb[:, 0:1], scale=1.0),
              reads=[bbuf[bk], B_gc], writes=[B_LX])
        bk2 = 2 + bk
        cx.mm_group(banks[bk2][0:32, 0:n], [(sel_sb[:, 3, :], cs_sb[:, a:a + n]), (sel_sb[:, 4, :], rs_sb[:, a:a + n])],
                    reads=[B_cs, B_rs, B_gc], writes=[bbuf[bk2]])
        cx.op(cx.act, lambda a=a, n=n, bk2=bk2: nc.scalar.activation(out=YA[:, a:a + n], in_=banks[bk2][0:32, 0:n],
                                                                      func=AF.Identity, bias=selb_sb[:, 1:2], scale=1.0),
              reads=[bbuf[bk2], B_gc], writes=[B_YA])
        a += n
    if "LX_dbg" in dbg:
        lx_o = nc.dram_tensor("LX_dbg", [32, NALL], F32, kind="ExternalOutput").ap()
        ya_o = nc.dram_tensor("YA_dbg", [32, NALL], F32, kind="ExternalOutput").ap()
        cs_o = nc.dram_tensor("CS_dbg", [32, NALL], F32, kind="ExternalOutput").ap()
        rs_o = nc.dram_tensor("RS_dbg", [32, NALL], F32, kind="ExternalOutput").ap()
        sp_o = nc.dram_tensor("SP_dbg", [32, NALL], F32, kind="ExternalOutput").ap()
        kt_o = nc.dram_tensor("kT_dbg", [128, MH, NALL], BF16, kind="ExternalOutput").ap()
        qt_o = nc.dram_tensor("qT_dbg", [128, MH, NOWN], BF16, kind="ExternalOutput").ap()
        B_dd = Buf()
        cx.dma(lx_o[:, :], LX[:], reads=[B_LX], writes=[B_dd])
        cx.dma(ya_o[:, :], YA[:], reads=[B_YA], writes=[B_dd])
        cx.dma(cs_o[:, :], cs_sb[:], reads=[B_cs], writes=[B_dd])
        cx.dma(rs_o[:, :], rs_sb[:], reads=[B_rs], writes=[B_dd])
        cx.dma(sp_o[:, :], sp_sb[:], reads=[B_sp], writes=[B_dd])
        cx.dma(kt_o[:, :, :], kTres[:], reads=[B_kT], writes=[B_dd])
        cx.dma(qt_o[:, :, :], qTres[:], reads=[B_qT], writes=[B_dd])
    barrier()
    tmp_es.close()

    Cf = st("Cf", [128, MH, 257], F32)
    Cb = st("Cb", [128, MH, 257], BF16)
    B_Cf = [Buf(), Buf()]
    B_Cb = [Buf(), Buf()]
    vp = [st(f"vp{i}", [128, MH, 257], BF16) for i in range(2)]
    B_vp = [Buf(), Buf()]
    for i in range(2):
        cx.op(cx.pool, lambda i=i: nc.gpsimd.memset(vp[i][:], 1.0), writes=[B_vp[i]])
    Hacc1 = st("Hacc", [128, MH, 256], F32)
    Hacc = [Hacc1, Hacc1]
    B_H1 = Buf()
    B_H = [B_H1, B_H1]
    dexp4 = [st(f"dexp4_{i}", [128, 4, CH], F32) for i in range(2)]
    eexp4 = [st(f"eexp4_{i}", [128, 4, CH], F32) for i in range(2)]
    stil4 = [st(f"stil4_{i}", [128, 4, CH], BF16) for i in range(2)]
    qtil4 = [st(f"qtil4_{i}", [128, 4, CH], BF16) for i in range(2)]
    ktil4 = [st(f"ktil4_{i}", [128, 4, CH], BF16) for i in range(2)]
    ryh4 = [st(f"ryh4_{i}", [32, 4, CH], F32) for i in range(2)]
    rden4 = [st(f"rden4_{i}", [128, 2, 4], F32) for i in range(2)]
    B_dexp, B_eexp, B_stil, B_qtil, B_ktil, B_ryh, B_rden = ([Buf(), Buf()] for _ in range(7))
    hA_t = st("hA_t", [128, MH * MDV], F32)
    hn_b = st("hn_b", [128, MH * MDV], BF16)
    gm_t = st("gm_t", [128, KC, 128], BF16)
    hmg_t = st("hmg_t", [128, KC, 128], BF16)
    nstat = st("nstat", [128, 3, MH], F32)
    B_hA, B_hsq, B_hn, B_gmt, B_hmgt, B_nst = Buf(), Buf(), Buf(), Buf(), Buf(), Buf()
    B_hAd, B_hmgT = Buf(), Buf()
    bPT, bST, bE, bK = 0, 1, 2, 3
    bO, bD = (4, 5), (6, 7)
    bK16 = banks[bK][:].bitcast(BF16)
    vm_v = vm_d.rearrange("a (h v) -> a h v", v=MDV)
    gmT_v = gmT_d.rearrange("(kc p) t -> p kc t", p=128)
    hmgT_v = hmgT_d.rearrange("(kc p) t -> p kc t", p=128)
    vi = 0
    gcnt = 0
    for scan in range(2):
        for hd in range(MH):
            cx.op(cx.pool, lambda hd=hd: nc.gpsimd.memset(Cf[:, hd, :], 0.0), writes=[B_Cf[hd // 4]])
            cx.op(cx.pool, lambda hd=hd: nc.gpsimd.memset(Cb[:, hd, :], 0.0), writes=[B_Cb[hd // 4]])
        if scan == 0:
            order = [(0, False), (1, False)] + [(c, True) for c in range(2, 18)]
        else:
            order = [(1, False), (0, False)] + [(c, False) for c in range(33, 17, -1)] + [(c, True) for c in range(17, 1, -1)]
        tcol = CH - 1 if scan == 0 else 0
        for (c, outp) in order:
            a0 = c * CH
            t0 = a0 - NCTX
            vs = vi % 2
            vi += 1
            cx.dma(vp[vs][:, :, 0:256], vm_v[a0:a0 + CH, :, :], reads=[B_vm], writes=[B_vp[vs]])
            hs = vi % 2
            for gi in range(2):
                h0 = gi * 4
                gp = gcnt % 2
                gcnt += 1
                BCf, BCb = B_Cf[gi], B_Cb[gi]
                j0 = scan * 8 + h0
                cx.op(cx.dve, lambda: nc.vector.tensor_tensor(
                    out=ryh4[gp][:], in0=YA[:, a0:a0 + CH].unsqueeze(1).broadcast_to([32, 4, CH]),
                    in1=hmask_sb[:, j0:j0 + 4].unsqueeze(2).broadcast_to([32, 4, CH]), op=ALU.mult),
                    reads=[B_YA, B_c5], writes=[B_ryh[gp]])
                if outp:
                    for hl in range(4):
                        cx.mm_group(banks[bPT][:, hl * CH:(hl + 1) * CH],
                                    [(LX[:, a0:a0 + CH], ryh4[gp][:, hl, :]), (identf[:, :], maskf[:, scan, :])],
                                    reads=[B_LX, B_ryh[gp], B_const, B_c5], writes=[bbuf[bPT]])
                    for hl in range(4):
                        cx.mm_group(banks[bST][:, hl * CH:(hl + 1) * CH],
                                    [(kTres[:, h0 + hl, a0:a0 + CH], qTres[:, h0 + hl, t0:t0 + CH])],
                                    reads=[B_kT, B_qT], writes=[bbuf[bST]])
                    for hl in range(4):
                        cx.mm_group(banks[bE][:, hl * CH:(hl + 1) * CH], [(eones[:, :], ryh4[gp][:, hl, :])],
                                    reads=[B_c5, B_ryh[gp]], writes=[bbuf[bE]])
                    cx.op(cx.act, lambda: nc.scalar.activation(out=dexp4[gp][:].rearrange("p h t -> p (h t)"),
                                                               in_=banks[bPT][:, 0:512], func=AF.Exp),
                          reads=[bbuf[bPT]], writes=[B_dexp[gp]])
                    cx.op(cx.act, lambda: nc.scalar.activation(out=eexp4[gp][:].rearrange("p h t -> p (h t)"),
                                                               in_=banks[bE][:, 0:512], func=AF.Exp),
                          reads=[bbuf[bE]], writes=[B_eexp[gp]])
                    cx.op(cx.dve, lambda: nc.vector.tensor_tensor(out=stil4[gp][:].rearrange("p h t -> p (h t)"),
                                                                  in0=banks[bST][:, 0:512],
                                                                  in1=dexp4[gp][:].rearrange("p h t -> p (h t)"), op=ALU.mult),
                          reads=[bbuf[bST], B_dexp[gp]], writes=[B_stil[gp]])
                    cx.op(cx.dve, lambda: nc.vector.tensor_tensor(out=qtil4[gp][:], in0=qTres[:, h0:h0 + 4, t0:t0 + CH],
                                                                  in1=eexp4[gp][:], op=ALU.mult),
                          reads=[B_qT, B_eexp[gp]], writes=[B_qtil[gp]])
                    for hl in range(4):
                        hd = h0 + hl
                        bo = bO[hl // 2]
                        cx.mm_group(banks[bo][:, (hl % 2) * 256:(hl % 2) * 256 + 256],
                                    [(stil4[gp][:, hl, :], vp[vs][:, hd, 0:256]), (qtil4[gp][:, hl, :], Cb[:, hd, 0:256])],
                                    reads=[B_stil[gp], B_qtil[gp], B_vp[vs], BCb], writes=[bbuf[bo]])
                        cx.mm_group(banks[bK][:, 256 + hl:257 + hl],
                                    [(stil4[gp][:, hl, :], vp[vs][:, hd, 256:257]), (qtil4[gp][:, hl, :], Cb[:, hd, 256:257])],
                                    reads=[B_stil[gp], B_qtil[gp], B_vp[vs], BCb], writes=[bbuf[bK]])
                    cx.op(cx.act, lambda: nc.scalar.activation(out=rden4[gp][:, 0, :], in_=banks[bK][:, 256:260], func=AF.Abs),
                          reads=[bbuf[bK]], writes=[B_rden[gp]])
                    cx.op(cx.dve, lambda: nc.vector.tensor_scalar(out=rden4[gp][:, 0, :], in0=rden4[gp][:, 0, :], scalar1=1.0,
                                                                  scalar2=None, op0=ALU.max),
                          reads=[B_rden[gp]], writes=[B_rden[gp]])
                    cx.op(cx.dve, lambda: nc.vector.reciprocal(out=rden4[gp][:, 1, :], in_=rden4[gp][:, 0, :]),
                          reads=[B_rden[gp]], writes=[B_rden[gp]])
                    for b2 in range(2):
                        cx.op(cx.dve, lambda b2=b2: nc.vector.tensor_tensor(
                            out=Hacc[hs][:, h0 + 2 * b2:h0 + 2 * b2 + 2, :],
                            in0=banks[bO[b2]][:, 0:512].rearrange("p (h v) -> p h v", v=256),
                            in1=rden4[gp][:, 1, 2 * b2:2 * b2 + 2].unsqueeze(2).broadcast_to([128, 2, 256]), op=ALU.mult),
                            reads=[bbuf[bO[b2]], B_rden[gp]], writes=[B_H[hs]])
                else:
                    for hl in range(4):
                        cx.mm_group(banks[bPT][:, hl * CH + tcol:hl * CH + tcol + 1],
                                    [(LX[:, a0:a0 + CH], ryh4[gp][:, hl, tcol:tcol + 1])],
                                    reads=[B_LX, B_ryh[gp]], writes=[bbuf[bPT]])
                        cx.mm_group(banks[bE][:, hl * CH + tcol:hl * CH + tcol + 1],
                                    [(eones[:, :], ryh4[gp][:, hl, tcol:tcol + 1])],
                                    reads=[B_c5, B_ryh[gp]], writes=[bbuf[bE]])
                    cx.op(cx.act, lambda: nc.scalar.activation(
                        out=dexp4[gp][:, :, tcol:tcol + 1],
                        in_=banks[bPT][:, 0:512].rearrange("p (h t) -> p h t", t=CH)[:, :, tcol:tcol + 1], func=AF.Exp),
                        reads=[bbuf[bPT]], writes=[B_dexp[gp]])
                    cx.op(cx.act, lambda: nc.scalar.activation(
                        out=eexp4[gp][:, :, tcol:tcol + 1],
                        in_=banks[bE][:, 0:512].rearrange("p (h t) -> p h t", t=CH)[:, :, tcol:tcol + 1], func=AF.Exp),
                        reads=[bbuf[bE]], writes=[B_eexp[gp]])
                for hl in range(4):
                    cx.op(cx.pe, lambda hl=hl: nc.tensor.transpose(bK16[:, hl * CH:(hl + 1) * CH],
                                                                   kTres[:, h0 + hl, a0:a0 + CH], identb[:]),
                          reads=[B_kT, B_const], writes=[bbuf[bK]])
                cx.op(cx.dve, lambda: nc.vector.tensor_tensor(
                    out=ktil4[gp][:], in0=bK16[:, 0:512].rearrange("p (h d) -> p h d", d=CH),
                    in1=dexp4[gp][:, :, tcol:tcol + 1].broadcast_to([128, 4, CH]), op=ALU.mult),
                    reads=[bbuf[bK], B_dexp[gp]], writes=[B_ktil[gp]])
                for hl in range(4):
                    hd = h0 + hl
                    bd = bD[hl // 2]
                    cx.mm_group(banks[bd][:, (hl % 2) * 256:(hl % 2) * 256 + 256], [(ktil4[gp][:, hl, :], vp[vs][:, hd, 0:256])],
                                reads=[B_ktil[gp], B_vp[vs]], writes=[bbuf[bd]])
                    cx.mm_group(banks[bK][:, 260 + hl:261 + hl], [(ktil4[gp][:, hl, :], vp[vs][:, hd, 256:257])],
                                reads=[B_ktil[gp], B_vp[vs]], writes=[bbuf[bK]])
                cx.op(cx.dve, lambda: nc.vector.tensor_tensor(
                    out=Cf[:, h0:h0 + 4, :], in0=Cf[:, h0:h0 + 4, :],
                    in1=eexp4[gp][:, :, tcol:tcol + 1].broadcast_to([128, 4, 257]), op=ALU.mult),
                    reads=[BCf, B_eexp[gp]], writes=[BCf])
                for b2 in range(2):
                    cx.op(cx.dve, lambda b2=b2: nc.vector.tensor_tensor(
                        out=Cf[:, h0 + 2 * b2:h0 + 2 * b2 + 2, 0:256], in0=Cf[:, h0 + 2 * b2:h0 + 2 * b2 + 2, 0:256],
                        in1=banks[bD[b2]][:, 0:512].rearrange("p (h v) -> p h v", v=256), op=ALU.add),
                        reads=[BCf, bbuf[bD[b2]]], writes=[BCf])
                cx.op(cx.dve, lambda: nc.vector.tensor_tensor(
                    out=Cf[:, h0:h0 + 4, 256:257], in0=Cf[:, h0:h0 + 4, 256:257],
                    in1=banks[bK][:, 260:264].unsqueeze(2), op=ALU.add),
                    reads=[BCf, bbuf[bK]], writes=[BCf])
                cx.op(cx.pool, lambda: nc.gpsimd.tensor_copy(out=Cb[:, h0:h0 + 4, :], in_=Cf[:, h0:h0 + 4, :]),
                      reads=[BCf], writes=[BCb])
            if not outp:
                continue
            Hf = Hacc[hs][:].rearrange("p h v -> p (h v)")
            if scan == 0:
                cx.dma(hA_d[t0:t0 + CH, :], Hf, reads=[B_H[hs]], writes=[B_hAd])
                continue
            cx.dma(hA_t[:], hA_d[t0:t0 + CH, :], reads=[B_hAd], writes=[B_hA])
            cx.dma(gm_t[:], gmT_v[:, :, t0:t0 + CH], reads=[B_gmT], writes=[B_gmt])
            cx.op(cx.dve, lambda: nc.vector.tensor_tensor(out=hA_t[:], in0=hA_t[:], in1=Hf, op=ALU.add),
                  reads=[B_hA, B_H[hs]], writes=[B_hA])
            for hd8 in range(MH):
                cx.op(cx.act, lambda hd8=hd8: nc.scalar.activation(
                    out=hn_b[:, hd8 * MDV:(hd8 + 1) * MDV], in_=hA_t[:, hd8 * MDV:(hd8 + 1) * MDV], func=AF.Square,
                    accum_out=nstat[:, 0, hd8:hd8 + 1]), reads=[B_hA], writes=[B_hn, B_nst])
            cx.op(cx.act, lambda: nc.scalar.activation(out=nstat[:, 1, :], in_=nstat[:, 0, :], func=AF.Sqrt,
                                                       scale=1.0 / MDV, bias=EPS), reads=[B_nst], writes=[B_nst])
            cx.op(cx.dve, lambda: nc.vector.reciprocal(out=nstat[:, 2, :], in_=nstat[:, 1, :]), reads=[B_nst], writes=[B_nst])
            cx.op(cx.dve, lambda: nc.vector.tensor_tensor(
                out=hn_b[:].rearrange("p (h v) -> p h v", v=MDV), in0=hA_t[:].rearrange("p (h v) -> p h v", v=MDV),
                in1=nstat[:, 2, :].unsqueeze(2).broadcast_to([128, MH, MDV]), op=ALU.mult),
                reads=[B_hA, B_nst], writes=[B_hn])
            for half8 in range(2):
                bt = 6 + half8
                b16 = banks[bt][:].bitcast(BF16)
                for q in range(8):
                    kc = half8 * 8 + q
                    cx.op(cx.pe, lambda kc=kc, q=q, b16=b16: nc.tensor.transpose(
                        b16[:, q * 128:(q + 1) * 128], hn_b[:, kc * 128:(kc + 1) * 128], identb[:]),
                        reads=[B_hn, B_const], writes=[bbuf[bt]])
                for q in range(8):
                    kc = half8 * 8 + q
                    cx.op(cx.dve, lambda kc=kc, q=q, b16=b16: nc.vector.scalar_tensor_tensor(
                        out=hmg_t[:, kc, :], in0=b16[:, q * 128:(q + 1) * 128], scalar=mhg_sb[:, kc:kc + 1],
                        in1=gm_t[:, kc, :], op0=ALU.mult, op1=ALU.mult),
                        reads=[bbuf[bt], B_c5, B_gmt], writes=[B_hmgt])
            cx.dma(hmgT_v[:, :, t0:t0 + CH], hmg_t[:], reads=[B_hmgt], writes=[B_hmgT])
    stage_end()


    stage_begin()
    cnT = st("cnT", [128, 4, NALL], BF16)
    KRg = st("KRg", [64, NALL], F32)
    sqkr = st("sqkr", [64, NALL], BF16)
    cosk = st("cosk", [64, NALL], F32)
    sink = st("sink", [64, NALL], F32)
    kg_sb = st("kg_sb", [128, 4], F32)
    kvg_sb = st("kvg_sb", [128, 4], F32)
    B_cn, B_KR, B_sqkr, B_tk = Buf(), Buf(), Buf(), Buf()
    cx.dma(cosk[:], cosT[:, :], writes=[B_tk])
    cx.dma(sink[:], sinT[:, :], writes=[B_tk])
    cx.dma(kg_sb[:], kg[:, :], writes=[B_tk])
    cx.dma(kvg_sb[:], kvgT[:, :], writes=[B_tk])
    cx.op(cx.dve, lambda: nc.vector.tensor_scalar(out=cosk[:], in0=cosk[:], scalar1=kg_sb[0:64, 1:2], scalar2=None,
                                                  op0=ALU.mult), reads=[B_tk], writes=[B_tk])
    cx.op(cx.dve, lambda: nc.vector.tensor_scalar(out=sink[:], in0=sink[:], scalar1=kg_sb[0:64, 2:3], scalar2=None,
                                                  op0=ALU.mult), reads=[B_tk], writes=[B_tk])
    ck = [st(f"ck{i}", [128, 4, 512], F32) for i in range(2)]
    krt = [st(f"krt{i}", [64, 2, 512], F32) for i in range(2)]
    B_ck = [Buf(), Buf()]
    tf6 = [st(f"tf6_{i}", [128, 512], F32) for i in range(4)]
    tb6 = [st(f"tb6_{i}", [128, 4, 512], BF16) for i in range(2)]
    ob6 = [st(f"ob6_{i}", [128, 512], BF16) for i in range(4)]
    B_tf6 = [Buf() for _ in range(4)]
    B_tb6 = [Buf() for _ in range(2)]
    B_ob6 = [Buf() for _ in range(4)]
    rr6 = {"tf": 0, "ob": 0, "bk": 0, "w": 0}

    def n6(kind, n):
        i = rr6[kind] % n
        rr6[kind] += 1
        return i

    ckv_v = ckv_d.rearrange("(r p) a -> p r a", p=128)
    blks = [(a, min(512, NALL - a)) for a in range(0, NALL, 512)]
    for bi, (a0, n) in enumerate(blks):
        sl = bi % 2
        cx.dma(ck[sl][:, :, 0:n], ckv_v[:, :, a0:a0 + n], reads=[B_ckv], writes=[B_ck[sl]])
        cx.dma(krt[sl][:, 0, 0:n], kr_d[0:64, a0:a0 + n], reads=[B_kr], writes=[B_ck[sl]])
        cx.dma(krt[sl][:, 1, 0:n], kr_d[64:128, a0:a0 + n], reads=[B_kr], writes=[B_ck[sl]])
        cx.op(cx.act, lambda: nc.scalar.activation(out=tb6[sl][:, :, 0:n], in_=ck[sl][:, :, 0:n], func=AF.Square),
              reads=[B_ck[sl]], writes=[B_tb6[sl]])
        bk = n6("bk", 8)
        cx.mm_group(banks[bk][:, 0:n], [(onesb[:, :], tb6[sl][:, r, 0:n]) for r in range(4)],
                    reads=[B_tb6[sl], B_const], writes=[bbuf[bk]])
        ir = n6("tf", 4)
        cx.op(cx.act, lambda: nc.scalar.activation(out=tf6[ir][:, 0:n], in_=banks[bk][:, 0:n], func=AF.Ln,
                                                   scale=1.0 / KVR, bias=EPS), reads=[bbuf[bk]], writes=[B_tf6[ir]])
        cx.op(cx.act, lambda: nc.scalar.activation(out=tf6[ir][:, 0:n], in_=tf6[ir][:, 0:n], func=AF.Exp, scale=-0.5),
              reads=[B_tf6[ir]], writes=[B_tf6[ir]])
        for r in range(4):
            cx.op(cx.dve, lambda r=r: nc.vector.scalar_tensor_tensor(
                out=cnT[:, r, a0:a0 + n], in0=ck[sl][:, r, 0:n], scalar=kvg_sb[:, r:r + 1], in1=tf6[ir][:, 0:n],
                op0=ALU.mult, op1=ALU.mult), reads=[B_ck[sl], B_tf6[ir], B_tk], writes=[B_cn])
        cx.op(cx.act, lambda: nc.scalar.activation(out=sqkr[:, a0:a0 + n], in_=krt[sl][:, 0, 0:n], func=AF.Square),
              reads=[B_ck[sl]], writes=[B_sqkr])
        cx.op(cx.pool, lambda: nc.gpsimd.tensor_tensor(out=KRg[:, a0:a0 + n], in0=krt[sl][:, 0, 0:n], in1=cosk[:, a0:a0 + n],
                                                       op=ALU.mult), reads=[B_ck[sl], B_tk], writes=[B_KR])
        cx.op(cx.pool, lambda: nc.gpsimd.tensor_tensor(out=krt[sl][:, 1, 0:n], in0=krt[sl][:, 1, 0:n], in1=sink[:, a0:a0 + n],
                                                       op=ALU.mult), reads=[B_ck[sl], B_tk], writes=[B_ck[sl]])
        cx.op(cx.pool, lambda: nc.gpsimd.tensor_tensor(out=KRg[:, a0:a0 + n], in0=KRg[:, a0:a0 + n], in1=krt[sl][:, 1, 0:n],
                                                       op=ALU.add), reads=[B_ck[sl], B_KR], writes=[B_KR])
    wu_f = [st(f"wu_f{i}", [128, 4, 512], F32) for i in range(3)]
    wu_b = [st(f"wu_b{i}", [128, 4, 512], BF16) for i in range(3)]
    B_wuf = [Buf(), Buf(), Buf()]
    B_wub = [Buf(), Buf(), Buf()]
    B_kaT, B_va = Buf(), Buf()
    wuk_v = w_uk.rearrange("(r p) n -> p r n", p=128)
    wuv_v = w_uv.rearrange("(r p) n -> p r n", p=128)

    wlist6 = [(0, hd_ * 128, 128) for hd_ in range(AH)] + [(1, g4_ * 512, 512) for g4_ in range(4)]
    w6s = {"issued": 0, "used": 0}

    def issue_wu():
        gi = w6s["issued"]
        if gi >= len(wlist6):
            return
        which, c0, width = wlist6[gi]
        view = wuk_v if which == 0 else wuv_v
        i = gi % 3
        cx.dma(wu_f[i][:, :, 0:width], view[:, :, c0:c0 + width], writes=[B_wuf[i]])
        cx.op(cx.pool, lambda: nc.gpsimd.tensor_copy(out=wu_b[i][:, :, 0:width], in_=wu_f[i][:, :, 0:width]),
              reads=[B_wuf[i]], writes=[B_wub[i]])
        w6s["issued"] += 1

    def load_wu(view, c0, width):
        gu = w6s["used"]
        assert wlist6[gu][1:] == (c0, width)
        while w6s["issued"] < min(gu + 2, len(wlist6)):
            issue_wu()
        w6s["used"] += 1
        return gu % 3

    for hd in range(AH):
        wi = load_wu(wuk_v, hd * 128, 128)
        for (a0, n) in blks:
            bk = n6("bk", 8)
            cx.mm_group(banks[bk][:, 0:n], [(wu_b[wi][:, r, 0:128], cnT[:, r, a0:a0 + n]) for r in range(4)],
                        reads=[B_wub[wi], B_cn], writes=[bbuf[bk]])
            s1 = n6("ob", 4)
            cx.op(cx.act, lambda: nc.scalar.activation(out=ob6[s1][:, 0:n], in_=banks[bk][:, 0:n], func=AF.Square),
                  reads=[bbuf[bk]], writes=[B_ob6[s1]])
            bq = n6("bk", 8)
            cx.mm_group(banks[bq][:, 0:n], [(onesb[:, :], ob6[s1][:, 0:n]), (onesb[0:64, :], sqkr[:, a0:a0 + n])],
                        reads=[B_ob6[s1], B_sqkr, B_const], writes=[bbuf[bq]])
            ir = n6("tf", 4)
            cx.op(cx.act, lambda: nc.scalar.activation(out=tf6[ir][:, 0:n], in_=banks[bq][:, 0:n], func=AF.Ln,
                                                       scale=1.0 / 192.0, bias=EPS), reads=[bbuf[bq]], writes=[B_tf6[ir]])
            cx.op(cx.act, lambda: nc.scalar.activation(out=tf6[ir][:, 0:n], in_=tf6[ir][:, 0:n], func=AF.Exp, scale=-0.5),
                  reads=[B_tf6[ir]], writes=[B_tf6[ir]])
            o1 = n6("ob", 4)
            cx.op(cx.dve, lambda: nc.vector.scalar_tensor_tensor(
                out=ob6[o1][:, 0:n], in0=banks[bk][:, 0:n], scalar=kg_sb[:, 0:1], in1=tf6[ir][:, 0:n],
                op0=ALU.mult, op1=ALU.mult), reads=[bbuf[bk], B_tf6[ir], B_tk], writes=[B_ob6[o1]])
            cx.dma(kaT_d[hd, 0:128, a0:a0 + n], ob6[o1][:, 0:n], reads=[B_ob6[o1]], writes=[B_kaT])
            o2 = n6("ob", 4)
            cx.op(cx.pool, lambda: nc.gpsimd.tensor_tensor(out=ob6[o2][0:64, 0:n], in0=KRg[:, a0:a0 + n],
                                                           in1=tf6[ir][0:64, 0:n], op=ALU.mult),
                  reads=[B_KR, B_tf6[ir]], writes=[B_ob6[o2]])
            cx.dma(kaT_d[hd, 128:192, a0:a0 + n], ob6[o2][0:64, 0:n], reads=[B_ob6[o2]], writes=[B_kaT])
    for g4 in range(4):
        wi = load_wu(wuv_v, g4 * 512, 512)
        for a0 in range(0, NALL, 128):
            bk = n6("bk", 8)
            cx.mm_group(banks[bk][:, 0:512], [(cnT[:, r, a0:a0 + 128], wu_b[wi][:, r, :]) for r in range(4)],
                        reads=[B_wub[wi], B_cn], writes=[bbuf[bk]])
            o1 = n6("ob", 4)
            if (a0 // 128) % 2 == 0:
                cx.op(cx.act, lambda: nc.scalar.copy(out=ob6[o1][:, :], in_=banks[bk][:, 0:512]), reads=[bbuf[bk]],
                      writes=[B_ob6[o1]])
            else:
                cx.op(cx.dve, lambda: nc.vector.tensor_copy(out=ob6[o1][:, :], in_=banks[bk][:, 0:512]), reads=[bbuf[bk]],
                      writes=[B_ob6[o1]])
            cx.dma(va_d[a0:a0 + 128, g4 * 512:(g4 + 1) * 512], ob6[o1][:, :], reads=[B_ob6[o1]], writes=[B_va])
    stage_end()

    stage_begin()
    B_ozT = Buf()
    kn_s = [st(f"kn_s{i}", [128, NALL], BF16) for i in range(2)]
    kr_s = [st(f"kr_s{i}", [64, NALL], BF16) for i in range(2)]
    v_s = [st(f"v_s{i}", [128, NCHUNK, 128], BF16) for i in range(2)]
    qn_s = [st(f"qn_s{i}", [128, NOWN], BF16) for i in range(2)]
    qr_s = [st(f"qr_s{i}", [64, NOWN], BF16) for i in range(2)]
    sz_s = [st(f"sz_s{i}", [128, NOWN], BF16) for i in range(2)]
    B_hd = [Buf(), Buf()]
    pt = [st(f"pt{i}", [128, 512], BF16) for i in range(3)]
    B_pt = [Buf() for _ in range(3)]
    rd = st("rd", [128, 512], F32)
    ot = st("ot", [128, 512], F32)
    ozb = [st(f"ozb{i}", [128, 512], BF16) for i in range(2)]
    B_rd, B_ot = Buf(), Buf()
    B_ozb = [Buf(), Buf()]
    va_v = va_d.rearrange("(c p) n -> p c n", p=128)
    sc_scale = float(192.0 ** -0.5)
    pi = 0
    sbk = 0
    blkc = 0

    def load_head(hd):
        sl = hd % 2
        cx.dma(kn_s[sl][:], kaT_d[hd, 0:128, :], reads=[B_kaT], writes=[B_hd[sl]])
        cx.dma(kr_s[sl][:], kaT_d[hd, 128:192, :], reads=[B_kaT], writes=[B_hd[sl]])
        cx.dma(v_s[sl][:], va_v[:, :, hd * 128:(hd + 1) * 128], reads=[B_va], writes=[B_hd[sl]])
        cx.dma(qn_s[sl][:], qaT_d[hd, 0:128, :], reads=[B_qaT], writes=[B_hd[sl]])
        cx.dma(qr_s[sl][:], qaT_d[hd, 128:192, :], reads=[B_qaT], writes=[B_hd[sl]])
        cx.dma(sz_s[sl][:], szaT_d[hd * 128:(hd + 1) * 128, :], reads=[B_szaT], writes=[B_hd[sl]])

    load_head(0)
    for hd in range(AH):
        sl = hd % 2
        if hd + 1 < AH:
            load_head(hd + 1)
        for tbk in range(4):
            t0 = tbk * 512
            bo, bd = (4, 5) if blkc % 2 == 0 else (6, 7)
            blkc += 1

            def emit_s(sc):
                nonlocal sbk
                bs = sbk % 4
                sbk += 1
                cx.mm_group(banks[bs][:, 0:512], [(kn_s[sl][:, sc * 128:(sc + 1) * 128], qn_s[sl][:, t0:t0 + 512]),
                                                  (kr_s[sl][:, sc * 128:(sc + 1) * 128], qr_s[sl][:, t0:t0 + 512])],
                            reads=[B_hd[sl]], writes=[bbuf[bs]])
                return bs

            bs_next = emit_s(0)
            for sc in range(NCHUNK):
                bs = bs_next
                if sc + 1 < NCHUNK:
                    bs_next = emit_s(sc + 1)
                p = pi % 3
                pi += 1
                cx.op(cx.act, lambda: nc.scalar.activation(out=pt[p][:], in_=banks[bs][:, 0:512], func=AF.Exp,
                                                           scale=sc_scale, bias=-ATT_SHIFT),
                      reads=[bbuf[bs]], writes=[B_pt[p]])
                last = sc == NCHUNK - 1
                cx.mm1(banks[bo][:, 0:512], v_s[sl][:, sc, :], pt[p][:], start=(sc == 0), stop=last,
                       reads=[B_pt[p], B_hd[sl]], writes=[bbuf[bo]], signal=last)
                cx.mm1(banks[bd][:, 0:512], onesb[:, :], pt[p][:], start=(sc == 0), stop=last,
                       reads=[B_pt[p], B_const], writes=[bbuf[bd]], signal=True)
            cx.op(cx.dve, lambda: nc.vector.reciprocal(out=rd[:], in_=banks[bd][:, 0:512]), reads=[bbuf[bd]], writes=[B_rd])
            cx.op(cx.dve, lambda: nc.vector.tensor_tensor(out=ot[:], in0=banks[bo][:, 0:512], in1=rd[:], op=ALU.mult),
                  reads=[bbuf[bo], B_rd], writes=[B_ot])
            ob = (hd * 4 + tbk) % 2
            cx.op(cx.pool, lambda: nc.gpsimd.tensor_tensor(out=ozb[ob][:], in0=ot[:], in1=sz_s[sl][:, t0:t0 + 512], op=ALU.mult),
                  reads=[B_ot, B_hd[sl]], writes=[B_ozb[ob]])
            cx.dma(ozT_d[hd * 128:(hd + 1) * 128, t0:t0 + 512], ozb[ob][:], reads=[B_ozb[ob]], writes=[B_ozT])
    stage_end()

    B_mixT = Buf()
    hmg_v = hmgT_d.rearrange("(kc p) t -> p kc t", p=128)
    oz_v = ozT_d.rearrange("(kc p) t -> p kc t", p=128)
    wpm_v = w_pm.rearrange("(kc p) n -> p kc n", p=128)
    wpa_v = w_pa.rearrange("(kc p) n -> p kc n", p=128)
    for th in range(2):
        stage_begin()
        T0 = th * 1024
        hmg_r = st("hmg_r", [128, KC, 1024], BF16)
        oz_r = st("oz_r", [128, KC, 1024], BF16)
        B_res = Buf()
        for kc in range(KC):
            cx.dma(hmg_r[:, kc, :], hmg_v[:, kc, T0:T0 + 1024], reads=[B_hmgT], writes=[B_res])
            cx.dma(oz_r[:, kc, :], oz_v[:, kc, T0:T0 + 1024], reads=[B_ozT], writes=[B_res])
        wf8 = [st(f"wf8_{i}", [128, KC, 128], F32) for i in range(4)]
        wb8 = [st(f"wb8_{i}", [128, KC, 128], BF16) for i in range(4)]
        B_wf8 = [Buf() for _ in range(4)]
        B_wb8 = [Buf() for _ in range(4)]
        gg8 = [st(f"gg8_{i}", [128, 2, 512], BF16) for i in range(2)]
        B_gg8 = [Buf(), Buf()]
        ta8 = [st(f"ta8_{i}", [128, 512], F32) for i in range(2)]
        tb8 = [st(f"tb8_{i}", [128, 512], F32) for i in range(2)]
        mo8 = [st(f"mo8_{i}", [128, 512], BF16) for i in range(2)]
        B_ta8, B_tb8, B_mo8 = [Buf(), Buf()], [Buf(), Buf()], [Buf(), Buf()]
        it8 = 0
        def load_w8(j):
            for wsel, view in ((0, wpm_v), (1, wpa_v)):
                i = (j * 2 + wsel) % 4
                cx.dma(wf8[i][:], view[:, :, j * 128:(j + 1) * 128], writes=[B_wf8[i]])
                cx.op(cx.pool, lambda i=i: nc.gpsimd.tensor_copy(out=wb8[i][:], in_=wf8[i][:]), reads=[B_wf8[i]],
                      writes=[B_wb8[i]])

        load_w8(0)
        for j in range(KC):
            if j + 1 < KC:
                load_w8(j + 1)
            ws = [(j * 2) % 4, (j * 2 + 1) % 4]
            for blk in range(2):
                s8 = it8 % 2
                it8 += 1
                t0 = T0 + blk * 512
                cx.dma(gg8[s8][:, 0, :], ggT_d[j * 128:(j + 1) * 128, t0:t0 + 512], reads=[B_ggT], writes=[B_gg8[s8]])
                cx.dma(gg8[s8][:, 1, :], ggT_d[D + j * 128:D + (j + 1) * 128, t0:t0 + 512], reads=[B_ggT], writes=[B_gg8[s8]])
                bm, ba = (it8 % 2) * 2, (it8 % 2) * 2 + 1
                cx.mm_group(banks[bm][:, 0:512], [(wb8[ws[0]][:, kc, :], hmg_r[:, kc, blk * 512:(blk + 1) * 512]) for kc in range(KC)],
                            reads=[B_wb8[ws[0]], B_res], writes=[bbuf[bm]])
                cx.mm_group(banks[ba][:, 0:512], [(wb8[ws[1]][:, kc, :], oz_r[:, kc, blk * 512:(blk + 1) * 512]) for kc in range(KC)],
                            reads=[B_wb8[ws[1]], B_res], writes=[bbuf[ba]])
                cx.op(cx.dve, lambda: nc.vector.tensor_tensor(out=ta8[s8][:], in0=banks[bm][:, 0:512], in1=gg8[s8][:, 0, :],
                                                              op=ALU.mult), reads=[bbuf[bm], B_gg8[s8]], writes=[B_ta8[s8]])
                cx.op(cx.dve, lambda: nc.vector.tensor_tensor(out=tb8[s8][:], in0=banks[ba][:, 0:512], in1=gg8[s8][:, 1, :],
                                                              op=ALU.mult), reads=[bbuf[ba], B_gg8[s8]], writes=[B_tb8[s8]])
                cx.op(cx.dve, lambda: nc.vector.tensor_tensor(out=mo8[s8][:], in0=ta8[s8][:], in1=tb8[s8][:], op=ALU.add),
                      reads=[B_ta8[s8], B_tb8[s8]], writes=[B_mo8[s8]])
                cx.dma(mixT_d[j * 128:(j + 1) * 128, t0:t0 + 512], mo8[s8][:], reads=[B_mo8[s8]], writes=[B_mixT])
        stage_end()

    stage_begin()
    wo_b = st("wo_b", [128, KC, D], BF16)
    gate_bc = st("gate_bc", [128, D], F32)
    B_wo, B_gbc = Buf(), Buf()
    cx.dma(gate_bc[:], gateb_d[0:1, :].broadcast_to([128, D]), reads=[B_gateb], writes=[B_gbc])
    wof = [st(f"wof{i}", [128, KC, 256], F32) for i in range(2)]
    B_wof = [Buf(), Buf()]
    wout_v = w_out.rearrange("(kc p) n -> p kc n", p=128)
    for g8 in range(8):
        i = g8 % 2
        cx.dma(wof[i][:], wout_v[:, :, g8 * 256:(g8 + 1) * 256], writes=[B_wof[i]])
        cx.op(cx.pool, lambda i=i, g8=g8: nc.gpsimd.tensor_copy(out=wo_b[:, :, g8 * 256:(g8 + 1) * 256], in_=wof[i][:]),
              reads=[B_wof[i]], writes=[B_wo])
    mx = [st(f"mx{i}", [128, KC, 128], BF16) for i in range(2)]
    xo = [st(f"xo{i}", [128, D], F32) for i in range(2)]
    yo = [st(f"yo{i}", [128, D], F32) for i in range(2)]
    B_mx, B_xo, B_yo = [Buf(), Buf()], [Buf(), Buf()], [Buf(), Buf()]
    mix_v = mixT_d.rearrange("(kc p) t -> p kc t", p=128)
    B_out = Buf()
    for ti in range(NOWN // 128):
        s9 = ti % 2
        cx.dma(mx[s9][:], mix_v[:, :, ti * 128:(ti + 1) * 128], reads=[B_mixT], writes=[B_mx[s9]])
        cx.dma(xo[s9][:], xa[NCTX + ti * 128:NCTX + (ti + 1) * 128, :], writes=[B_xo[s9]])
        for fb in range(4):
            bk = (ti * 4 + fb) % 8
            cx.mm_group(banks[bk][:, 0:512], [(mx[s9][:, kc, :], wo_b[:, kc, fb * 512:(fb + 1) * 512]) for kc in range(KC)],
                        reads=[B_mx[s9], B_wo], writes=[bbuf[bk]])
            cx.op(cx.dve, lambda fb=fb, bk=bk: nc.vector.tensor_tensor(
                out=yo[s9][:, fb * 512:(fb + 1) * 512], in0=banks[bk][:, 0:512], in1=gate_bc[:, fb * 512:(fb + 1) * 512],
                op=ALU.mult), reads=[bbuf[bk], B_gbc], writes=[B_yo[s9]])
        cx.op(cx.pool, lambda: nc.gpsimd.tensor_tensor(out=yo[s9][:], in0=yo[s9][:], in1=xo[s9][:], op=ALU.add),
              reads=[B_yo[s9], B_xo[s9]], writes=[B_yo[s9]])
        cx.dma(out_d[ti * 128:(ti + 1) * 128, :], yo[s9][:], reads=[B_yo[s9]], writes=[B_out])
    stage_end()

    for t in cx.all_dma_toks[-40:]:
        cx.sp.wait(t)
    cx.es.close()
    outer.close()
    return nc


def _col_perm(half):
    M_QK_W, M_V_W, M_GATE_W = 1024, 2048, 32
    A_Q_W, A_V_W = 3072, 2048
    o_km = 0
    o_vm = o_km + M_QK_W
    o_gt = o_vm + M_V_W
    o_ckv = o_gt + M_GATE_W
    o_kr = o_ckv + KVR
    kvc = o_kr + AROPE
    o_qm = kvc
    o_om = o_qm + M_QK_W
    o_zm = o_om + M_V_W
    o_qa = o_zm + M_V_W
    o_za = o_qa + A_Q_W
    o_gm = o_za + A_V_W
    sw = np.array([ax * 32 + (1 - hf) * 16 + f for ax in range(2) for hf in range(2) for f in range(16)])
    perm = []
    perm += list(range(o_km, o_km + 1024))
    g = np.arange(32).reshape(4, 8)
    if half == 1:
        g = g[[2, 3, 0, 1]]
    perm += list(o_gt + g.reshape(-1))
    perm += list(range(o_ckv, o_ckv + KVR))
    perm += list(range(o_kr, o_kr + 64))
    perm += list(o_kr + sw)
    perm += list(range(o_vm, o_vm + 2048))
    perm += list(range(o_qm, o_qm + 1024))
    for j in range(16):
        perm += list(range(o_om + j * 128, o_om + (j + 1) * 128))
        perm += list(range(o_zm + j * 128, o_zm + (j + 1) * 128))
    for h in range(16):
        b0 = o_qa + h * 192
        perm += list(range(b0, b0 + 128))
        perm += list(range(b0 + 128, b0 + 192))
        perm += list(b0 + 128 + sw)
    perm += list(range(o_za, o_za + 2048))
    perm += list(range(o_gm, o_gm + 4096))
    perm = np.array(perm)
    assert perm.shape[0] == NCOLS
    return perm, sw


def _colT(v, n):
    return np.ascontiguousarray(np.asarray(v, np.float32).reshape(n, 128).T)


def _consts():
    ident = np.eye(128, dtype=np.float32)
    s = np.arange(128)[:, None]
    t = np.arange(128)[None, :]
    masks = np.zeros((128, 2, 128), np.float32)
    masks[:, 0, :] = np.where(s <= t, 0.0, -30000.0)
    masks[:, 1, :] = np.where(s >= t, 0.0, -30000.0)
    sel = np.zeros((32, 6, 32), np.float32)
    for h in range(8):
        sel[h, 0, h] = 1.0
        sel[16 + h, 0, 8 + h] = 1.0
        sel[8 + h, 1, h] = 1.0
        sel[24 + h, 2, 8 + h] = 1.0
        sel[8 + h, 3, 16 + h] = -1.0
        sel[24 + h, 4, 24 + h] = -1.0
    for r in range(16, 32):
        sel[r, 5, :] = 1.0
    selb = np.zeros((32, 2), np.float32)
    selb[16:, 0] = 1.0
    selb[:16, 1] = 1.0
    hmask = np.zeros((32, 16), np.float32)
    for j in range(16):
        hmask[j, j] = 1.0
        hmask[16 + j, j] = 1.0
    return ident, masks, sel, selb, hmask


def _rope_tables(pos):
    row = (pos // 64).astype(np.float32)
    col = (pos % 64).astype(np.float32)
    freqs = (np.float32(10000.0) ** (-np.arange(16, dtype=np.float32) / np.float32(16))).astype(np.float32)
    ang = np.stack([row[:, None] * freqs, col[:, None] * freqs], axis=1)
    cos = np.cos(ang).astype(np.float32)
    sin = np.sin(ang).astype(np.float32)
    cosT = np.ones((64, NALL), np.float32)
    sinT = np.zeros((64, NALL), np.float32)
    for ax in range(2):
        for hf in range(2):
            r0 = ax * 32 + hf * 16
            cosT[r0:r0 + 16, NCTX:] = cos[:, ax, :].T
            sgn = -1.0 if hf == 0 else 1.0
            sinT[r0:r0 + 16, NCTX:] = sgn * sin[:, ax, :].T
    return cosT, sinT


def make_in_maps(inp):
    x = np.asarray(inp["x"], np.float32)
    c = np.asarray(inp["c"], np.float32)
    ctx = np.asarray(inp["ctx"], np.float32)
    c_ctx = np.asarray(inp["c_ctx"], np.float32)
    ident, masks, sel, selb, hmask = _consts()
    w_in0 = np.asarray(inp["w_in"], np.float32)[0]
    conv_w = np.asarray(inp["conv_w"], np.float32)[0]
    conv_b = np.asarray(inp["conv_b"], np.float32)[0]
    gate_b = np.asarray(inp["gate_b"], np.float32)[0]
    qn = np.asarray(inp["q_norm_g"], np.float32)[0]
    kn = np.asarray(inp["k_norm_g"], np.float32)[0]
    shared = {}
    per_half = {}
    for half in range(2):
        perm, sw = _col_perm(half)
        d = {}
        d["w_in"] = np.ascontiguousarray(w_in0[:, perm])
        cw = conv_w if half == 0 else conv_w[::-1]
        d["conv_wq"] = np.ascontiguousarray(cw[:, :1024].reshape(5, MH, 128).transpose(2, 1, 0))
        d["conv_wk"] = np.ascontiguousarray(cw[:, 1024:].reshape(5, MH, 128).transpose(2, 1, 0))
        gb = gate_b.reshape(4, 8)
        if half == 1:
            gb = gb[[2, 3, 0, 1]]
        d["gate_b"] = np.ascontiguousarray(gb.reshape(32, 1))
        pos = np.arange(4096)
        if half == 1:
            pos = pos[::-1]
        d["cosT"], d["sinT"] = _rope_tables(pos)
        per_half[half] = d
    shared["ada_w"] = np.ascontiguousarray(np.asarray(inp["ada_w"], np.float32)[0])
    shared["ada_bT"] = _colT(np.asarray(inp["ada_b"])[0], 48)
    shared["norm_gT"] = _colT(np.asarray(inp["norm_g"])[0], KC)
    shared["conv_bq"] = np.ascontiguousarray(conv_b[:1024].reshape(MH, 128).T)
    shared["conv_bk"] = np.ascontiguousarray(conv_b[1024:].reshape(MH, 128).T)
    shared["mhgT"] = _colT(np.asarray(inp["mh_norm_g"])[0], KC)
    sw = _col_perm(0)[1]
    qg = np.zeros((128, 4), np.float32)
    qg[:, 0] = qn[:128]
    qg[:64, 1] = qn[128:]
    qg[:64, 2] = qn[128:][sw]
    kgm = np.zeros((128, 4), np.float32)
    kgm[:, 0] = kn[:128]
    kgm[:64, 1] = kn[128:]
    kgm[:64, 2] = kn[128:][sw]
    shared["qg"] = qg
    shared["kg"] = kgm
    shared["kvgT"] = _colT(np.asarray(inp["kv_norm_g"])[0], 4)
    shared["w_uk"] = np.ascontiguousarray(np.asarray(inp["w_uk"], np.float32)[0])
    shared["w_uv"] = np.ascontiguousarray(np.asarray(inp["w_uv"], np.float32)[0])
    shared["w_pm"] = np.ascontiguousarray(np.asarray(inp["w_proj_m"], np.float32)[0])
    shared["w_pa"] = np.ascontiguousarray(np.asarray(inp["w_proj_a"], np.float32)[0])
    shared["w_out"] = np.ascontiguousarray(np.asarray(inp["w_out"], np.float32)[0])
    shared["ident_f"] = ident
    shared["masks"] = masks
    shared["sel"] = sel
    shared["selb"] = selb
    shared["hmask"] = hmask
    eo = np.zeros((32, 128), np.float32)
    eo[16:] = 1.0
    shared["eones"] = eo
    maps = []
    for core in range(8):
        b, half = core // 2, core % 2
        m = dict(shared)
        m.update(per_half[half])
        if half == 0:
            xl = x[b]
            cl = ctx[b]
        else:
            xl = x[b][::-1]
            cl = ctx[b][::-1]
        m["xa"] = np.ascontiguousarray(np.concatenate([cl, xl], axis=0))
        cc = np.stack([c[b], c_ctx], axis=1)
        m["cT"] = np.ascontiguousarray(cc.reshape(KC, 128, 2).transpose(1, 0, 2))
        maps.append(m)
    return maps


_NC_CACHE = {}


def kernel(**inputs):
    maps = make_in_maps(inputs)
    if "nc" not in _NC_CACHE:
        _NC_CACHE["nc"] = build()
    nc = _NC_CACHE["nc"]
    res = run_bass_kernel_spmd(nc, maps, core_ids=list(range(8)))
    out = np.zeros((4, 4096, D), np.float32)
    for core in range(8):
        b, half = core // 2, core % 2
        o = np.asarray(res.results[core]["out"], np.float32)
        if half == 0:
            out[b, :2048] = o
        else:
            out[b, 2048:] = o[::-1]
    return out
```
